# Optimizing a Trainium2 kernel written in Bass

```python
import jax, jax.numpy as jnp
from jax import lax
import numpy as np

D_MODEL = 1024
BATCH = 16
SEQ = 256
DEPTH = 2
DEC_BATCH = 8
DEC_SEQ = 4096
PAST_LEN = 512

GRID_W = 64
N_MIXERS = 2
EPS = 1e-6
N_HEADS = 16
N_KV_HEADS = 4
HEAD_DIM = D_MODEL // N_HEADS
ATTN_WIDTH = N_HEADS * HEAD_DIM
KV_WIDTH = N_KV_HEADS * HEAD_DIM
ATTN_IN = 2 * ATTN_WIDTH + 2 * KV_WIDTH
AXIS_DIM = HEAD_DIM // 2
ROPE_THETA = 10000.0
Q_BLOCK = 128
EXPAND = 2
D_INNER = EXPAND * D_MODEL
SSD_HEAD_DIM = 64
SSD_HEADS = D_INNER // SSD_HEAD_DIM
SSD_GROUPS = 4
D_STATE = 128
CONV_W = 3
CHUNK = 128
CONV_DIM = D_INNER + 2 * SSD_GROUPS * D_STATE
SSD_IN = D_INNER + CONV_DIM + 2 * SSD_HEADS

kernel_name = "hybrid_dit_attn_ssd_step"


def rms_norm(x, w):
    xf = x.astype(jnp.float32)
    y = xf * lax.rsqrt(jnp.mean(xf * xf, axis=-1, keepdims=True) + EPS)
    return (y * w.astype(jnp.float32)).astype(x.dtype)


def ada_mod(cond, mod_w, mod_b):
    m = jax.nn.silu(cond) @ mod_w + mod_b
    return jnp.split(m, 3, axis=-1)


def modulate(x, norm_w, shift, scale):
    return rms_norm(x, norm_w) * (1.0 + scale) + shift


def axial_rope_tables(n_tokens):
    rows = n_tokens // GRID_W
    row_ids = jnp.repeat(jnp.arange(rows), GRID_W).astype(jnp.float32)
    col_ids = jnp.tile(jnp.arange(GRID_W), rows).astype(jnp.float32)
    inv_freq = 1.0 / (ROPE_THETA ** (jnp.arange(0, AXIS_DIM, 2, dtype=jnp.float32) / AXIS_DIM))
    ang = jnp.concatenate([row_ids[:, None] * inv_freq, col_ids[:, None] * inv_freq], axis=-1)
    return jnp.cos(ang), jnp.sin(ang)


def apply_axial_rope(x, cos, sin):
    B, T, H, _ = x.shape
    half = AXIS_DIM // 2
    xr = x.astype(jnp.float32).reshape(B, T, H, 2, 2, half)
    a, b = xr[..., 0, :], xr[..., 1, :]
    cs = cos.reshape(T, 1, 2, half)
    sn = sin.reshape(T, 1, 2, half)
    out = jnp.stack([a * cs - b * sn, a * sn + b * cs], axis=-2)
    return out.reshape(x.shape).astype(x.dtype)


def attend(q, k, v):
    B, T = q.shape[:2]
    rep = N_HEADS // N_KV_HEADS
    qb = q.reshape(B, T // Q_BLOCK, Q_BLOCK, N_KV_HEADS, rep, HEAD_DIM).transpose(1, 0, 2, 3, 4, 5)
    scale = HEAD_DIM ** -0.5

    def block(qi):
        s = jnp.einsum('bqgrd,bsgd->bgrqs', qi, k, preferred_element_type=jnp.float32) * scale
        p = jax.nn.softmax(s, axis=-1).astype(v.dtype)
        return jnp.einsum('bgrqs,bsgd->bqgrd', p, v)

    o = lax.map(block, qb)
    return o.transpose(1, 0, 2, 3, 4, 5).reshape(B, T, ATTN_WIDTH)


def attn_qkvg(h, w_in, q_norm, k_norm):
    B, T, _ = h.shape
    proj = h @ w_in
    q, k, v, g = jnp.split(proj, [ATTN_WIDTH, ATTN_WIDTH + KV_WIDTH, ATTN_WIDTH + 2 * KV_WIDTH], axis=-1)
    q = rms_norm(q.reshape(B, T, N_HEADS, HEAD_DIM), q_norm)
    k = rms_norm(k.reshape(B, T, N_KV_HEADS, HEAD_DIM), k_norm)
    v = v.reshape(B, T, N_KV_HEADS, HEAD_DIM)
    return q, k, v, g


def attn_context(h, w_in, q_norm, k_norm, w_out):
    q, k, v, g = attn_qkvg(h, w_in, q_norm, k_norm)
    o = attend(q, k, v)
    return (o * jax.nn.silu(g)) @ w_out, k, v


def attn_latent(h, w_in, q_norm, k_norm, w_out, ctx_k, ctx_v, cos, sin):
    q, k, v, g = attn_qkvg(h, w_in, q_norm, k_norm)
    q = apply_axial_rope(q, cos, sin)
    k = apply_axial_rope(k, cos, sin)
    k_all = jnp.concatenate([ctx_k.astype(k.dtype), k], axis=1)
    v_all = jnp.concatenate([ctx_v.astype(v.dtype), v], axis=1)
    o = attend(q, k_all, v_all)
    return (o * jax.nn.silu(g)) @ w_out


def depthwise_conv_centred(x, w, b):
    out = lax.conv_general_dilated(
        x, w[:, None, :].astype(x.dtype), window_strides=(1,),
        padding=[(CONV_W // 2, CONV_W // 2)],
        dimension_numbers=('NWC', 'WIO', 'NWC'),
        feature_group_count=x.shape[-1])
    return out + b


def ssd_chunked(x, a, b, c, h0):
    Bs, T, H, P = x.shape
    G, N = b.shape[-2:]
    R = H // G
    nc = T // CHUNK
    x = x.reshape(Bs, nc, CHUNK, G, R, P)
    a = a.reshape(Bs, nc, CHUNK, G, R)
    b = b.reshape(Bs, nc, CHUNK, G, N)
    c = c.reshape(Bs, nc, CHUNK, G, N)
    a_cs = jnp.cumsum(a, axis=2)
    lower = jnp.tril(jnp.ones((CHUNK, CHUNK), bool))[:, :, None, None]
    decay = jnp.exp(jnp.where(lower, a_cs[:, :, :, None] - a_cs[:, :, None, :], -jnp.inf))
    cb = jnp.einsum('bclgn,bcsgn->bclsg', c, b)
    y_diag = jnp.einsum('bclsgr,bcsgrp->bclgrp', cb[..., None] * decay, x)
    decay_end = jnp.exp(a_cs[:, :, -1:] - a_cs)
    states = jnp.einsum('bclgn,bclgrp->bcgrpn', b, x * decay_end[..., None])
    chunk_decay = jnp.exp(a_cs[:, :, -1])

    def step(h, inp):
        s, d = inp
        return h * d[..., None, None] + s, h

    h_final, h_prev = lax.scan(step, h0.reshape(Bs, G, R, P, N),
                               (jnp.moveaxis(states, 1, 0), jnp.moveaxis(chunk_decay, 1, 0)))
    h_prev = jnp.moveaxis(h_prev, 0, 1)
    y_off = jnp.einsum('bclgn,bcgrpn->bclgrp', c, h_prev) * jnp.exp(a_cs)[..., None]
    return (y_diag + y_off).reshape(Bs, T, H, P), h_final.reshape(Bs, H, P, N)


def ssd_branch(h, w_in, conv_w, conv_b, dt_bias_f, dt_bias_b, a_log_f, a_log_b, d_skip, gnorm_w, w_out, h0_f, h0_b):
    f32 = jnp.float32
    B, T, _ = h.shape
    proj = h @ w_in
    z, xbc, dt = jnp.split(proj, [D_INNER, D_INNER + CONV_DIM], axis=-1)
    xbc = jax.nn.silu(depthwise_conv_centred(xbc, conv_w, conv_b))
    xs, bs, cs = jnp.split(xbc, [D_INNER, D_INNER + SSD_GROUPS * D_STATE], axis=-1)
    xs = xs.reshape(B, T, SSD_HEADS, SSD_HEAD_DIM).astype(f32)
    bs = bs.reshape(B, T, SSD_GROUPS, D_STATE).astype(f32)
    cs = cs.reshape(B, T, SSD_GROUPS, D_STATE).astype(f32)
    dt = jax.nn.softplus(dt.astype(f32) + jnp.concatenate([dt_bias_f, dt_bias_b]).astype(f32))
    dt_f, dt_b = dt[..., :SSD_HEADS], dt[..., SSD_HEADS:]
    a_f = -jnp.exp(a_log_f.astype(f32))
    a_b = -jnp.exp(a_log_b.astype(f32))
    y_f, h_f = ssd_chunked(xs * dt_f[..., None], a_f * dt_f, bs, cs, h0_f.astype(f32))
    flip = lambda t: jnp.flip(t, axis=1)
    y_b, h_b = ssd_chunked(flip(xs * dt_b[..., None]), flip(a_b * dt_b), flip(bs), flip(cs), h0_b.astype(f32))
    y = y_f + flip(y_b) + d_skip.astype(f32)[:, None] * xs
    y = y.reshape(B, T, D_INNER).astype(h.dtype)
    y = rms_norm(y * jax.nn.silu(z), gnorm_w)
    return y @ w_out, h_f.astype(h.dtype), h_b.astype(h.dtype)


def setup_inputs(seed: int = 0) -> dict:
    key = jax.random.key(seed)
    ks = iter(jax.random.split(key, 40))
    f32 = jnp.float32
    nrm = lambda shape, s: s * jax.random.normal(next(ks), shape, f32)
    gain = lambda n: 1.0 + 0.02 * jax.random.normal(next(ks), (n,), f32)

    def dt_bias():
        u = jax.random.uniform(next(ks), (SSD_HEADS,), f32)
        dt = jnp.exp(u * (np.log(0.1) - np.log(0.001)) + np.log(0.001))
        return dt + jnp.log(-jnp.expm1(-dt))

    def a_log():
        return jnp.log(jax.random.uniform(next(ks), (SSD_HEADS,), f32, 1.0, 16.0))

    inp = {}
    inp['x_prompt'] = nrm((BATCH, SEQ, D_MODEL), 1.0)
    inp['x_sample'] = nrm((DEC_BATCH, DEC_SEQ, D_MODEL), 1.0)
    inp['cache_k_l0'] = nrm((DEC_BATCH, PAST_LEN, N_KV_HEADS, HEAD_DIM), 1.0)
    inp['cache_v_l0'] = nrm((DEC_BATCH, PAST_LEN, N_KV_HEADS, HEAD_DIM), 1.0)
    inp['state_fwd_l1'] = nrm((DEC_BATCH, SSD_HEADS, SSD_HEAD_DIM, D_STATE), 0.1)
    inp['state_bwd_l1'] = nrm((DEC_BATCH, SSD_HEADS, SSD_HEAD_DIM, D_STATE), 0.1)
    inp['c'] = nrm((DEC_BATCH, D_MODEL), 1.0)
    inp['c_ctx'] = nrm((D_MODEL,), 1.0)
    inp['l0_norm_w'] = gain(D_MODEL)
    inp['l0_mod_w'] = nrm((D_MODEL, 3 * D_MODEL), 0.5 * D_MODEL ** -0.5)
    inp['l0_mod_b'] = nrm((3 * D_MODEL,), 0.02)
    inp['l0_w_in'] = nrm((D_MODEL, ATTN_IN), D_MODEL ** -0.5)
    inp['l0_q_norm'] = gain(HEAD_DIM)
    inp['l0_k_norm'] = gain(HEAD_DIM)
    inp['l0_w_out'] = nrm((ATTN_WIDTH, D_MODEL), ATTN_WIDTH ** -0.5)
    inp['l1_norm_w'] = gain(D_MODEL)
    inp['l1_mod_w'] = nrm((D_MODEL, 3 * D_MODEL), 0.5 * D_MODEL ** -0.5)
    inp['l1_mod_b'] = nrm((3 * D_MODEL,), 0.02)
    inp['l1_w_in'] = nrm((D_MODEL, SSD_IN), D_MODEL ** -0.5)
    inp['l1_conv_w'] = nrm((CONV_W, CONV_DIM), CONV_W ** -0.5)
    inp['l1_conv_b'] = nrm((CONV_DIM,), 0.02)
    inp['l1_dt_bias_f'] = dt_bias()
    inp['l1_dt_bias_b'] = dt_bias()
    inp['l1_a_log_f'] = a_log()
    inp['l1_a_log_b'] = a_log()
    inp['l1_d_skip'] = gain(SSD_HEADS)
    inp['l1_gnorm_w'] = gain(D_INNER)
    inp['l1_w_out'] = nrm((D_INNER, D_MODEL), D_INNER ** -0.5)
    return inp


def reference(x_prompt, x_sample, cache_k_l0, cache_v_l0, state_fwd_l1, state_bwd_l1, c, c_ctx,
              l0_norm_w, l0_mod_w, l0_mod_b, l0_w_in, l0_q_norm, l0_k_norm, l0_w_out,
              l1_norm_w, l1_mod_w, l1_mod_b, l1_w_in, l1_conv_w, l1_conv_b, l1_dt_bias_f, l1_dt_bias_b,
              l1_a_log_f, l1_a_log_b, l1_d_skip, l1_gnorm_w, l1_w_out):
    layers = [
        (l0_norm_w, l0_mod_w, l0_mod_b, (l0_w_in, l0_q_norm, l0_k_norm, l0_w_out)),
        (l1_norm_w, l1_mod_w, l1_mod_b, (l1_w_in, l1_conv_w, l1_conv_b, l1_dt_bias_f, l1_dt_bias_b,
                                         l1_a_log_f, l1_a_log_b, l1_d_skip, l1_gnorm_w, l1_w_out)),
    ]
    caches = [(cache_k_l0, cache_v_l0), (state_fwd_l1, state_bwd_l1)]
    cos, sin = axial_rope_tables(x_sample.shape[1])
    cond_ctx = c_ctx[None, None, :]
    cond_lat = c[:, None, :]
    xp, xs = x_prompt, x_sample
    ctx_out = []
    for i in range(DEPTH):
        norm_w, mod_w, mod_b, mp = layers[i]
        cache_a, cache_b = caches[i]
        sh_p, sc_p, g_p = ada_mod(cond_ctx, mod_w, mod_b)
        sh_s, sc_s, g_s = ada_mod(cond_lat, mod_w, mod_b)
        hp = modulate(xp, norm_w, sh_p, sc_p)
        hs = modulate(xs, norm_w, sh_s, sc_s)
        if i % N_MIXERS == 0:
            yp, k_new, v_new = attn_context(hp, *mp)
            ys = attn_latent(hs, *mp, cache_a, cache_b, cos, sin)
            ctx_out.append((k_new, v_new))
        else:
            zeros = jnp.zeros((xp.shape[0], SSD_HEADS, SSD_HEAD_DIM, D_STATE), xp.dtype)
            yp, hf_new, hb_new = ssd_branch(hp, *mp, zeros, zeros)
            ys, _, _ = ssd_branch(hs, *mp, cache_a, cache_b)
            ctx_out.append((hf_new, hb_new))
        xp = xp + g_p * yp
        xs = xs + g_s * ys
    (new_k_l0, new_v_l0), (new_state_fwd_l1, new_state_bwd_l1) = ctx_out
    return (xp, xs, new_k_l0, new_v_l0, new_state_fwd_l1, new_state_bwd_l1)
```

```python
import os
import numpy as np
from contextlib import ExitStack
import concourse.bass as bass
import concourse.mybir as mybir
from concourse.bass_utils import run_bass_kernel_spmd

F32 = mybir.dt.float32
BF16 = mybir.dt.bfloat16
AF = mybir.ActivationFunctionType
ALU = mybir.AluOpType

D = 1024
KC = 8
TS = int(os.environ.get("K_TS", "4096"))
TP = 256
NTOK = TS + 2 * TP
CTX = 512
EPS = 1e-6
SEQS = [(0, TS, True), (TS, TP, False), (TS + TP, TP, False)]
NCH = NTOK // 128
H1OFF = [0, TS + 2, TS + 2 + TP + 2]
H1COLS = TS + 2 + 2 * (TP + 2)

ENGS = ["pe", "act", "dve", "pool", "sp"]
NSLOT = 8


class Op:
    __slots__ = ("q", "name", "args", "kw", "deps", "sig", "cnt", "dma", "slot", "dval", "gi")


class TK:
    def __init__(self):
        self.ops = {q: [] for q in ENGS}
        self.lastw = {}
        self.rd = {}
        self.ndma = {q: 0 for q in ENGS}
        self.fence = []
        self.fenced = {q: True for q in ENGS}
        self.last = {q: None for q in ENGS}
        self.dma_live = {}
        self.n = 0
        self.psum_keys = set()

    def op(self, q, name, args=(), kw=None, R=(), W=(), dma=False):
        o = Op()
        o.q, o.name, o.args, o.kw, o.dma = q, name, args, (kw or {}), dma
        o.sig, o.cnt, o.slot, o.dval = False, 0, 0, 0
        o.gi = self.n
        self.n += 1
        deps = {}
        if self.psum_keys:
            pr = [k for k in R if k in self.psum_keys]
            if pr:
                R = [k for k in R if k not in self.psum_keys]
                W = list(W) + [k for k in pr if k not in W]

        def need(d, raw):
            if d is None:
                return
            if d.dma or dma or d.q != q or raw or q != "pe":
                deps[d.gi] = d

        for k in R:
            need(self.lastw.get(k), True)
        for k in W:
            need(self.lastw.get(k), False)
            for r in self.rd.get(k, ()):
                need(r, False)
        if not self.fenced[q]:
            for d in self.fence:
                if d is not None:
                    deps[d.gi] = d
            self.fenced[q] = True
        o.deps = list(deps.values())
        for k in R:
            lst = self.rd.setdefault(k, [])
            if not dma:
                for i, r in enumerate(lst):
                    if (not r.dma) and r.q == q:
                        lst[i] = o
                        break
                else:
                    lst.append(o)
            else:
                lst.append(o)
        for k in W:
            self.lastw[k] = o
            self.rd[k] = []
        if dma:
            i = self.ndma[q]
            self.ndma[q] += 1
            o.slot = i % NSLOT
            o.dval = 16 * (i // NSLOT + 1)
            self.dma_live[(q, o.slot)] = o
        else:
            self.last[q] = o
        self.ops[q].append(o)
        return o

    def barrier(self, skip_dma_queues=()):
        self.fence = [self.last[q] for q in ENGS] + [o for (q, _), o in self.dma_live.items()
                                                     if q not in skip_dma_queues]
        self.fenced = {q: False for q in ENGS}

    def emit(self, nc, es, final_deps):
        fo = Op()
        fo.q, fo.name, fo.args, fo.kw, fo.dma = "sp", None, (), {}, False
        fo.sig, fo.cnt, fo.slot, fo.dval, fo.gi = False, 0, 0, 0, self.n
        fo.deps = [d for d in final_deps if d is not None] + [self.last[q] for q in ENGS if self.last[q] is not None] \
            + list(self.dma_live.values())
        self.ops["sp"].append(fo)
        for q in ENGS:
            for o in self.ops[q]:
                for d in o.deps:
                    if not d.dma:
                        d.sig = True
        for q in ENGS:
            c = 0
            for o in self.ops[q]:
                if o.sig:
                    c += 1
                    o.cnt = c
        sem = {q: es.enter_context(nc.semaphore("s_" + q)) for q in ENGS}
        dsem = {}
        for q in ENGS:
            if self.ndma[q]:
                dsem[q] = [es.enter_context(nc.semaphore("d_%s%d" % (q, i))) for i in range(NSLOT)]
        block = es.enter_context(nc.Block())
        bname = {"pe": "tensor", "act": "scalar", "dve": "vector", "pool": "gpsimd", "sp": "sync"}
        ops = self.ops

        def run(eng, q):
            waited = {}

            def w(key, s, val):
                if waited.get(key, 0) < val:
                    eng.wait_ge(s, val)
                    waited[key] = val

            for o in ops[q]:
                for d in o.deps:
                    if d.dma:
                        w((d.q, d.slot), dsem[d.q][d.slot], d.dval)
                    else:
                        w(d.q, sem[d.q], d.cnt)
                if o.dma and o.dval > 16:
                    w((q, o.slot), dsem[q][o.slot], o.dval - 16)
                if o.name is None:
                    continue
                ins = getattr(eng, o.name)(*o.args, **o.kw)
                if o.dma:
                    ins.then_inc(dsem[q][o.slot], 16)
                elif o.sig:
                    ins.then_inc(sem[q], 1)

        for q in ENGS:
            getattr(block, bname[q])(lambda eng, q=q: run(eng, q))


class B:
    def __init__(self, nc, es):
        self.nc, self.es, self.tk = nc, es, TK()
        self.outs = []

    def sb(self, name, shape, dt=F32):
        return self.es.enter_context(self.nc.sbuf_tensor(name, list(shape), dt))

    def ps(self, name, shape, dt=F32, keys=None):
        for k in (keys or [name]):
            self.tk.psum_keys.add(k)
        return self.es.enter_context(self.nc.psum_tensor(name, list(shape), dt))

    def dma(self, out, in_, R, W, q="sp", is_out=False, **kw):
        o = self.tk.op(q, "dma_start", (), dict(out=out, in_=in_, **kw), R, W, dma=True)
        if is_out:
            self.outs.append(o)
        return o

    def mm(self, out, lhsT, rhs, start, stop, R, W, **kw):
        return self.tk.op("pe", "matmul", (out,), dict(lhsT=lhsT, rhs=rhs, start=start, stop=stop, **kw), R, W)

    def tr(self, out, in_, ident, R, W):
        return self.tk.op("pe", "transpose", (out, in_, ident), {}, R, W)

    def act(self, out, in_, func, R, W, **kw):
        return self.tk.op("act", "activation", (), dict(out=out, in_=in_, func=func, **kw), R, W)

    def tt(self, out, in0, in1, op, R, W, q="dve"):
        return self.tk.op(q, "tensor_tensor", (), dict(out=out, in0=in0, in1=in1, op=op), R, W)

    def ts(self, out, in0, s1, s2, op0, op1, R, W, q="dve"):
        kw = dict(out=out, in0=in0, scalar1=s1, scalar2=s2, op0=op0)
        if op1 is not None:
            kw["op1"] = op1
        return self.tk.op(q, "tensor_scalar", (), kw, R, W)

    def stt(self, out, in0, scalar, in1, op0, op1, R, W, q="dve"):
        return self.tk.op(q, "scalar_tensor_tensor", (), dict(out=out, in0=in0, scalar=scalar, in1=in1, op0=op0, op1=op1), R, W)

    def cp(self, out, in_, R, W, q="dve"):
        return self.tk.op(q, "tensor_copy", (), dict(out=out, in_=in_), R, W)

    def rcp(self, out, in_, R, W):
        return self.tk.op("dve", "reciprocal", (), dict(out=out, in_=in_), R, W)

    def mset(self, ap, val, W, q="dve"):
        return self.tk.op(q, "memset", (ap, val), {}, (), W)


def bc(ap, shape):
    return ap.to_broadcast(list(shape))


def build_program(stage=99, debug=False):
    nc = bass.Bass("TRN2", target_bir_lowering=False)
    es = ExitStack()
    b = B(nc, es)
    tk = b.tk

    def din(name, shape, dt=F32):
        return nc.dram_tensor(name, list(shape), dt, kind="ExternalInput").ap()

    def dout(name, shape, dt=F32):
        return nc.dram_tensor(name, list(shape), dt, kind="ExternalOutput").ap()

    def dscr(name, shape, dt=F32, dbg=False):
        if dbg and debug:
            return nc.dram_tensor(name, list(shape), dt, kind="ExternalOutput").ap()
        return nc.dram_tensor(name, list(shape), dt).ap()

    xs = din("xs", [NTOK, D])
    ck = din("ck", [CTX, 256])
    cv = din("cv", [CTX, 256])
    stf = din("stf", [2048, 128])
    stb = din("stb", [2048, 128])
    cvec = din("cvec", [128, KC, 2])
    modw = [din("l0_mod_w", [D, 3 * D]), din("l1_mod_w", [D, 3 * D])]
    w0in = din("l0_w_in", [D, 2560])
    w0out = din("l0_w_out", [D, D])
    w1in = din("l1_w_in", [D, 5184])
    w1out = din("l1_w_out", [2048, D])
    nw_d = [din("l0_nw", [128, KC]), din("l1_nw", [128, KC])]
    mb_d = [din("l0_mb", [128, 24]), din("l1_mb", [128, 24])]
    mbg_d = [din("l0_mbg", [1, D]), din("l1_mbg", [1, D])]
    gvec_d = din("gvec", [128, 4])
    convw_d = din("convw", [128, 24, 4])
    dtb_d = din("dtb", [1, 64])
    alog_d = din("alog", [1, 64])
    dsk_d = din("dsk", [1, 32])
    gnw_d = din("gnw", [128, 16])
    gnrow_d = din("gnrow", [1, 2048])
    ident_d = din("ident", [128, 128])
    pm_d = din("pm", [128, 128])
    bones_d = din("bones", [128, 128])
    utf_d = din("utf", [128, 128])
    utb_d = din("utb", [128, 128])
    negf_d = din("negf", [128, 128])
    negb_d = din("negb", [128, 128])
    i4_d = din("i4", [128, 512])
    cos_d = din("cosT", [128, TS])
    sin_d = din("sinT", [128, TS])

    y_o = dout("y", [NTOK, D])
    nk_o = dout("nk", [2 * TP, 256])
    nv_o = dout("nv", [2 * TP, 256])
    sf_o = dout("sf", [2, 2048, 128])
    sb_o = dout("sbw", [2, 2048, 128])

    h0T = dscr("h0T", [128, KC, NTOK], BF16, dbg=True)
    h1T = dscr("h1T", [128, KC, H1COLS], BF16, dbg=True)
    x1s = dscr("x1s", [NTOK, D], F32, dbg=True)
    wqg_s = dscr("wqg_s", [16, 128, KC, 128], BF16)
    w1_s = dscr("w1_s", [D, 5184], BF16)
    w1o_s = dscr("w1o_s", [2048, D], BF16)

    ident_bf = b.sb("ident_bf", [128, 128], BF16)
    ident_f = b.sb("ident_f", [128, 128], F32)
    ones_bf = b.sb("ones_bf", [128, 128], BF16)
    ones_f = b.sb("ones_f", [128, 128], F32)
    mw = [[b.sb("mw%d%d" % (l, c), [128, KC]) for c in range(2)] for l in range(2)]
    sh = [[b.sb("sh%d%d" % (l, c), [128, KC]) for c in range(2)] for l in range(2)]
    gate_bc = [[b.sb("gbc%d%d" % (l, c), [128, D]) for c in range(2)] for l in range(2)]

    b.dma(ident_f[:], ident_d, [], ["ident_f"])
    b.dma(ident_bf[:], ident_d, [], ["ident_bf"], q="pool")
    b.mset(ones_bf[:], 1.0, ["ones_bf"])
    b.mset(ones_f[:], 1.0, ["ones_f"])

    eps_t = b.sb("eps_t", [128, 1])
    b.mset(eps_t[:], EPS, ["eps_t"])
    prep_tmp = {"junk": b.sb("pjunk", [128, D], BF16), "ss": b.sb("pss", [128, 4]),
                "xn": [b.sb("pxn%d" % i, [128, D], BF16) for i in range(2)]}

    l0w_scope = ExitStack()
    _es0 = b.es
    b.es = l0w_scope
    wkv = b.sb("wkv", [128, KC, 512], BF16)
    wout = b.sb("wout", [128, 8, D], BF16)
    pm_bf = b.sb("pm_bf", [128, 128], BF16)
    bones_bf = b.sb("bones_bf", [128, 128], BF16)
    gv = b.sb("gv", [128, 4])
    VA = b.sb("VA", [128, (TS + CTX) // 128, 4, 128], BF16)
    ckt = b.sb("ckt", [128, 4, 256], BF16)
    b.es = _es0
    b.mset(VA[:, :, :, 64:128], 1.0, ["VAones"])
    for g4_ in range(4):
        b.dma(VA[:, 0:4, g4_, 0:64], cv[:, g4_ * 64:(g4_ + 1) * 64].rearrange("(kt p) d -> p kt d", p=128),
              [], [("VA", kt_) for kt_ in range(4)], q="pool")
    b.dma(ckt[:], ck.rearrange("(kt p) c -> p kt c", p=128), [], ["ckt"], q="pool")
    b.dma(wkv[:], w0in[:, 1024:1536].rearrange("(kc p) c -> p kc c", p=128), [], ["wkv"], q="pool")
    for p_ in range(8):
        a_, i_ = p_ // 4, p_ % 4
        hx_, hy_ = 8 * a_ + i_, 8 * a_ + 4 + i_
        b.dma(wout[0:64, p_, :], w0out[hx_ * 64:(hx_ + 1) * 64, :], [], [("wout", p_)], q="pool")
        b.dma(wout[64:128, p_, :], w0out[hy_ * 64:(hy_ + 1) * 64, :], [], [("wout", p_)], q="pool")
    b.dma(pm_bf[:], pm_d, [], ["pm_bf"], q="pool")
    b.dma(bones_bf[:], bones_d, [], ["bones_bf"], q="pool")
    b.dma(gv[:], gvec_d, [], ["gv"])


    with ExitStack() as ph:
        es_save = b.es
        b.es = ph
        cv_f = b.sb("cv_f", [128, KC, 2])
        scT = b.sb("scT", [128, KC, 2], BF16)
        screp = [b.sb("screp%d" % c, [128, KC, 128], BF16) for c in range(2)]
        wpart = b.sb("wpart", [128, KC, D], BF16)
        wpf = [b.sb("wpf%d" % i, [128, KC, D]) for i in range(2)]
        npart = [0]
        nw_t = b.sb("nw_t", [128, KC])
        mb_t = b.sb("mb_t", [128, 24])
        mbg_t = b.sb("mbg_t", [128, D])
        tmp8 = b.sb("tmp8", [128, KC])
        pm0 = b.ps("pm0", [128, 512])
        pg = [b.ps("pg%d" % i, [128, 512]) for i in range(2)]

        b.dma(cv_f[:], cvec, [], ["cv_f"])
        b.act(scT[:], cv_f[:], AF.Silu, ["cv_f"], ["scT"])
        for c in range(2):
            b.cp(screp[c][:], bc(scT[:, :, c:c + 1], [128, KC, 128]), ["scT"], ["screp%d" % c])
        for l in range(2):
            b.dma(nw_t[:], nw_d[l], [], ["nw_t"])
            b.dma(mb_t[:], mb_d[l], [], ["mb_t"])
            b.dma(mbg_t[:], mbg_d[l].partition_broadcast(128), [], ["mbg_t"])
            for part in range(3):
                wf_, wfk = wpf[npart[0] % 2], "wpf%d" % (npart[0] % 2)
                npart[0] += 1
                for hq, qn in ((0, "sp"), (1, "act")):
                    b.dma(wf_[:, hq * 4:(hq + 1) * 4, :],
                          modw[l][hq * 512:(hq + 1) * 512, part * D:(part + 1) * D].rearrange("(kc p) c -> p kc c", p=128),
                          [], [(wfk, hq)], q=qn)
                for kc in range(KC):
                    if kc % 3 == 2:
                        b.act(wpart[:, kc, :], wf_[:, kc, :], AF.Copy, [(wfk, kc // 4)], [("wpart", kc)])
                    else:
                        b.cp(wpart[:, kc, :], wf_[:, kc, :], [(wfk, kc // 4)], [("wpart", kc)],
                             q=("dve" if kc % 3 == 0 else "pool"))
                if part < 2:
                    pmv = pm0[:, 0:16].rearrange("p (f c) -> p f c", c=2)
                    for fc in range(KC):
                        for kc in range(KC):
                            b.mm(pmv[:, fc, :], wpart[:, kc, fc * 128:(fc + 1) * 128], scT[:, kc, :],
                                 kc == 0, kc == KC - 1, [("wpart", kc), "scT"], ["pm0"])
                    for c in range(2):
                        if part == 0:
                            b.tt(sh[l][c][:], pmv[:, :, c], mb_t[:, 0:8], ALU.add, ["pm0", "mb_t"], ["sh%d%d" % (l, c)])
                        else:
                            b.stt(tmp8[:], pmv[:, :, c], 1.0, mb_t[:, 8:16], ALU.add, ALU.add,
                                  ["pm0", "mb_t"], ["tmp8"])
                            b.tt(mw[l][c][:], tmp8[:], nw_t[:], ALU.mult, ["tmp8", "nw_t"], ["mw%d%d" % (l, c)])
                else:
                    for c in range(2):
                        for half in range(2):
                            for kc in range(KC):
                                b.mm(pg[half][:], screp[c][:, kc, :], wpart[:, kc, half * 512:(half + 1) * 512],
                                     kc == 0, kc == KC - 1, [("wpart", kc), "screp%d" % c], ["pg%d" % half])
                            b.tt(gate_bc[l][c][:, half * 512:(half + 1) * 512], pg[half][:],
                                 mbg_t[:, half * 512:(half + 1) * 512], ALU.add, ["pg%d" % half, "mbg_t"],
                                 ["gbc%d%d" % (l, c)])
        b.es = es_save
        ph0_keep = ph.pop_all()

    def prep_block(ph_tag, xtiles, nt, l, cond, hblk, hkey, pst, pskeys):
        for ti, (xap, xkey) in enumerate(xtiles):
            prep_tile(ti, xap, xkey, pst, pskeys)
        prep_evac(nt, l, cond, hblk, hkey, pst, pskeys)

    def prep_tile(ti, xap, xkey, pst, pskeys):
        xkeys = list(xkey) if isinstance(xkey, list) else [xkey]
        if True:
            junk, ss, xn = prep_tmp["junk"], prep_tmp["ss"], prep_tmp["xn"][ti % 2]
            xnk = "xn%d" % (ti % 2)
            b.act(junk[:], xap, AF.Square, xkeys, ["pjunk"], accum_out=ss[:, 0:1])
            b.act(ss[:, 1:2], ss[:, 0:1], AF.Ln, ["pjunk", "eps_t"], ["pss1"], scale=1.0 / D, bias=eps_t[:, 0:1])
            b.act(ss[:, 2:3], ss[:, 1:2], AF.Exp, ["pss1"], ["pss2"], scale=-0.5)
            b.ts(xn[:], xap, ss[:, 2:3], None, ALU.mult, None, xkeys + ["pss2"], [xnk])
            for kc in range(KC):
                pv = pst[kc // 2][:].bitcast(BF16)
                c0 = (kc % 2) * 512 + ti * 128
                b.tr(pv[:, c0:c0 + 128], xn[:, kc * 128:(kc + 1) * 128], ident_bf[:], [xnk, "ident_bf"],
                     [pskeys[kc // 2]])
    def prep_evac(nt, l, cond, hblk, hkey, pst, pskeys):
        for kc in range(KC):
            pv = pst[kc // 2][:].bitcast(BF16)
            c0 = (kc % 2) * 512
            b.act(hblk[:, kc, 0:nt], pv[:, c0:c0 + nt], AF.Identity,
                  [pskeys[kc // 2], "mw%d%d" % (l, cond), "sh%d%d" % (l, cond)], [hkey],
                  scale=mw[l][cond][:, kc:kc + 1], bias=sh[l][cond][:, kc:kc + 1])


    CUT = float(os.environ.get("K_CUT", "99"))

    def pair_heads(p):
        a, i = p // 4, p % 4
        return 8 * a + i, 8 * a + 4 + i

    with ExitStack() as ph:
        es_save = b.es
        b.es = ph
        for t in range(16):
            p = t % 8
            base = 0 if t < 8 else 1536
            hx, hy = pair_heads(p)
            for half, hh in enumerate((hx, hy)):
                b.dma(wqg_s[t, :, :, half * 64:(half + 1) * 64],
                      w0in[:, base + hh * 64:base + (hh + 1) * 64].rearrange("(kc p) c -> p kc c", p=128),
                      [], [("wqg_s", t)], q="pool")
        tk.barrier(skip_dma_queues=("pool",))
        b.es = es_save
    ph0_keep.close()

    if stage >= 1:
        with ExitStack() as ph:
            es_save = b.es
            b.es = ph
            layer0(b, locals())
            tk.barrier()
            b.es = es_save
    l0w_scope.close()

    if stage >= 2:
        with ExitStack() as ph:
            es_save = b.es
            b.es = ph
            layer1(b, locals())
            b.es = es_save

    tk.emit(nc, es, b.outs)
    es.close()
    return nc


def layer0(b, g):
    tk = b.tk
    nc = b.nc
    xs, ck, cv, w0in, w0out, gvec_d = g["xs"], g["ck"], g["cv"], g["w0in"], g["w0out"], g["gvec_d"]
    pm_d, bones_d, cos_d, sin_d = g["pm_d"], g["bones_d"], g["cos_d"], g["sin_d"]
    h0T, h1T, x1s, wqg_s = g["h0T"], g["h1T"], g["x1s"], g["wqg_s"]
    nk_o, nv_o = g["nk_o"], g["nv_o"]
    ident_bf, ident_f, gate_bc, eps_t = g["ident_bf"], g["ident_f"], g["gate_bc"], g["eps_t"]
    prep_block, pair_heads = g["prep_block"], g["pair_heads"]
    prep_tile, prep_evac = g["prep_tile"], g["prep_evac"]
    CUT = g["CUT"]

    wkv, wout, pm_bf, bones_bf, gv = g["wkv"], g["wout"], g["pm_bf"], g["bones_bf"], g["gv"]

    NKT = (TS + CTX) // 128
    KT = b.sb("KT", [128, 2, TS + CTX], BF16)
    VA, ckt = g["VA"], g["ckt"]

    hblk = b.sb("hblk", [128, KC, 512], BF16)
    qr = b.sb("qr", [128, 8, 512], BF16)
    sg = b.sb("sg", [128, 8, 512], BF16)
    Pt = [b.sb("Pt%d" % i, [128, 2, 512], BF16) for i in range(3)]
    wt = [b.sb("wt%d" % i, [128, KC, 128], BF16) for i in range(2)]
    cosr = b.sb("cosr", [128, 512])
    sinr = b.sb("sinr", [128, 512])
    cosg = [b.sb("cosg%d" % i, [128, 512]) for i in range(2)]
    sing = [b.sb("sing%d" % i, [128, 512]) for i in range(2)]
    qb2 = [b.sb("qb%d" % i, [128, 512], BF16) for i in range(2)]
    sq2 = [b.sb("sq%d" % i, [128, 512], BF16) for i in range(2)]
    t12 = [b.sb("t1_%d" % i, [128, 512]) for i in range(2)]
    t22 = [b.sb("t2_%d" % i, [128, 512]) for i in range(2)]
    lnr2 = [b.sb("lnr%d" % i, [128, 512]) for i in range(2)]
    rstd2 = [b.sb("rstd%d" % i, [128, 512]) for i in range(2)]
    kf = b.sb("kf", [128, 256])
    kfT = b.sb("kfT", [128, 256])
    vf = b.sb("vf", [128, 256])
    ftmp = b.sb("ftmp", [128, 512])
    frec = b.sb("frec", [128, 512])
    xt = [b.sb("l0x%d" % i, [128, D]) for i in range(2)]
    x1 = [b.sb("l0x1%d" % i, [128, D]) for i in range(2)]
    h1b = b.sb("h1b", [128, KC, 512], BF16)
    zcol = b.sb("zcol", [128, KC, 2], BF16)
    b.mset(zcol[:], 0.0, ["zcol"])

    SA = b.ps("SA", [128, 1024], keys=[("SA", 0), ("SA", 1)])
    SB = b.ps("SB", [128, 1024], keys=[("SB", 0), ("SB", 1)])
    OA = b.ps("OA", [128, 512])
    OB = b.ps("OB", [128, 512])
    R0 = b.ps("R0", [128, 512])
    R1 = b.ps("R1", [128, 512])
    Sb = [SA, SB]
    Sk = [[("SA", 0), ("SA", 1)], [("SB", 0), ("SB", 1)]]
    Ob = [[OA, OB], [R0, R1]]
    Okey = [["OA", "OB"], ["R0", "R1"]]

    ncall = [0]

    def qk_stages(src, skey, nt, ci, outs):
        ncall[0] += 1
        par = ncall[0] % 2
        qb, sq, t1, t2, lnr, rstd = qb2[par], sq2[par], t12[par], t22[par], lnr2[par], rstd2[par]
        kq, ks, k1, k2, kl, kr = "qb%d" % par, "sq%d" % par, "t1_%d" % par, "t2_%d" % par, "lnr%d" % par, "rstd%d" % par
        (Ra, rak), (Rb, rbk) = ((R0, "R0"), (R1, "R1")) if par == 0 else ((OA, "OA"), (OB, "OB"))

        def A1():
            b.tt(t1[:, 0:nt], src, cosg[ci][:, 0:nt], ALU.mult, [skey, "cosg%d" % ci], [k1])
            b.act(qb[:, 0:nt], src, AF.Copy, [skey], [kq])
            b.tt(sq[:, 0:nt], qb[:, 0:nt], qb[:, 0:nt], ALU.mult, [kq], [ks])
            b.mm(Ra[:, 0:nt], bones_bf[:], sq[:, 0:nt], True, True, ["bones_bf", ks], [rak])
            b.mm(Rb[:, 0:nt], pm_bf[:], qb[:, 0:nt], True, True, ["pm_bf", kq], [rbk])

        def A2():
            b.act(lnr[:, 0:nt], Ra[:, 0:nt], AF.Ln, [rak, "eps_t"], [kl], scale=1.0 / 64, bias=eps_t[:, 0:1])
            b.act(rstd[:, 0:nt], lnr[:, 0:nt], AF.Exp, [kl], [kr], scale=-0.5)
            b.tt(t2[:, 0:nt], Rb[:, 0:nt], sing[ci][:, 0:nt], ALU.mult, [rbk, "sing%d" % ci], [k2])
            b.tt(t1[:, 0:nt], t1[:, 0:nt], t2[:, 0:nt], ALU.add, [k1, k2], [k1])

        def Bst():
            for (oap, okey) in outs:
                b.tt(oap, t1[:, 0:nt], rstd[:, 0:nt], ALU.mult, [k1, kr], [okey])
        return A1, A2, Bst

    def qk_pipe(src, skey, nt, ci, outs):
        for st in qk_stages(src, skey, nt, ci, outs):
            st()

    pst = [R0, R1, OA, OB]
    pk = ["R0", "R1", "OA", "OB"]
    if CUT <= 1:
        return
    wti = 0
    prompt_i = 0
    for si, (tok0, T, is_s) in enumerate(SEQS):
        nt = 512 if is_s else 256
        cond = 0 if is_s else 1
        ctx = CTX if is_s else 0
        nkt = (T + ctx) // 128
        nblk = T // nt
        if is_s:
            for kt in range(4):
                for a in range(2):
                    pv = R0[:].bitcast(BF16)
                    b.tr(pv[:, 0:128], ckt[:, kt, a * 128:(a + 1) * 128], ident_bf[:], ["ckt", "ident_bf"], ["R0"])
                    b.cp(KT[:, a, kt * 128:(kt + 1) * 128], pv[:, 0:128], ["R0"], [("KT", a)])
        if si == 0:
            w1in_, w1out_, w1_s_, w1o_s_ = g["w1in"], g["w1out"], g["w1_s"], g["w1o_s"]
            for r4 in range(4):
                b.dma(w1_s_[r4 * 256:(r4 + 1) * 256, :], w1in_[r4 * 256:(r4 + 1) * 256, :], [], [("w1_s", r4)], q="pool")
        if CUT <= 2:
            return
        if not is_s:
            b.mset(cosr[:], 1.0, ["cosr"])
            b.mset(sinr[:], 0.0, ["sinr"])
            for ci in range(2):
                b.ts(cosg[ci][:], cosr[:], gv[:, 2 * ci:2 * ci + 1], None, ALU.mult, None, ["cosr", "gv"], ["cosg%d" % ci])
                b.ts(sing[ci][:], sinr[:], gv[:, 2 * ci + 1:2 * ci + 2], None, ALU.mult, None, ["sinr", "gv"], ["sing%d" % ci])

        def load_tables(t0):
            if not is_s:
                return
            b.dma(cosr[:], cos_d[:, t0:t0 + 512], [], ["cosr"])
            b.dma(sinr[:], sin_d[:, t0:t0 + 512], [], ["sinr"])
            for ci in range(2):
                b.ts(cosg[ci][:], cosr[:], gv[:, 2 * ci:2 * ci + 1], None, ALU.mult, None, ["cosr", "gv"], ["cosg%d" % ci])
                b.ts(sing[ci][:], sinr[:], gv[:, 2 * ci + 1:2 * ci + 2], None, ALU.mult, None, ["sinr", "gv"], ["sing%d" % ci])

        for bi in range(nblk):
            t0 = bi * nt
            xbufs = [(xt[0], "l0x0"), (xt[1], "l0x1"), (x1[0], "l0x10"), (x1[1], "l0x11")]
            for ti in range(nt // 128):
                xb_, xk_ = xbufs[ti]
                g0_ = tok0 + t0 + ti * 128
                wk_ = [xk_] if ti < 2 else [(xk_, 0), (xk_, 1)]
                b.dma(xb_[:], xs[g0_:g0_ + 128, :], [], wk_)
                prep_tile(ti, xb_[:], wk_, pst, pk)
            prep_evac(nt, 0, cond, hblk, "hblk", pst, pk)
            b.dma(h0T[:, :, tok0 + t0:tok0 + t0 + nt], hblk[:, :, 0:nt], ["hblk"], [("h0T", tok0 + t0)])
            load_tables(t0)
            for a in range(2):
                src = SA[:, a * 512:a * 512 + nt]
                for kc in range(KC):
                    b.mm(src, wkv[:, kc, a * 128:(a + 1) * 128], hblk[:, kc, 0:nt], kc == 0, kc == KC - 1,
                         ["wkv", "hblk"], [("SA", a)])
                if CUT <= 2.2:
                    return
                outs = [(KT[:, a, ctx + t0:ctx + t0 + nt], ("KT", a))]
                if not is_s:
                    outs.append((kf[:, 0:nt], "kf"))
                qk_pipe(src, ("SA", a), nt, 1, outs)
                if CUT <= 2.5:
                    return
                if not is_s:
                    for ti in range(nt // 128):
                        b.tr(SB[:, ti * 128:(ti + 1) * 128], kf[:, ti * 128:(ti + 1) * 128], ident_f[:], ["kf", "ident_f"],
                             [("SB", 0)])
                    for ti in range(nt // 128):
                        b.cp(kfT[:, ti * 128:(ti + 1) * 128], SB[:, ti * 128:(ti + 1) * 128], [("SB", 0)], ["kfT"])
                        r0 = prompt_i * TP + t0 + ti * 128
                        b.dma(nk_o[r0:r0 + 128, a * 128:(a + 1) * 128], kfT[:, ti * 128:(ti + 1) * 128], ["kfT"], [],
                              is_out=True)
            if CUT <= 2.6:
                return
            for ti in range(nt // 128):
                kt = (ctx + t0) // 128 + ti
                vp = SB[:, 512:768]
                for kc in range(KC):
                    b.mm(vp, hblk[:, kc, ti * 128:(ti + 1) * 128], wkv[:, kc, 256:512], kc == 0, kc == KC - 1,
                         ["wkv", "hblk"], [("SB", 1)])
                if CUT <= 2.7:
                    return
                b.cp(VA[:, kt, :, 0:64], vp.rearrange("p (g d) -> p g d", g=4), [("SB", 1)], [("VA", kt)])
                if not is_s:
                    b.act(vf[:], vp, AF.Copy, [("SB", 1)], ["vf"])
                    r0 = prompt_i * TP + t0 + ti * 128
                    b.dma(nv_o[r0:r0 + 128, :], vf[:], ["vf"], [], is_out=True)

        if CUT <= 3:
            return
        def load_block_inputs(bi_):
            t0_ = bi_ * nt
            b.dma(hblk[:, :, 0:nt], h0T[:, :, tok0 + t0_:tok0 + t0_ + nt], [("h0T", tok0 + t0_)], ["hblk"])
            load_tables(t0_)

        load_block_inputs(0)
        gt_total = nblk * 16
        issued = set()

        def issue_w(gt):
            if gt >= gt_total or gt in issued:
                return
            issued.add(gt)
            b.dma(wt[gt % 2][:], wqg_s[gt % 16], [("wqg_s", gt % 16)], ["wt%d" % (gt % 2)])

        issue_w(0)
        for bi in range(nblk):
            t0 = bi * nt

            def proj(t):
                gt = bi * 16 + t
                w, wk = wt[gt % 2], "wt%d" % (gt % 2)
                issue_w(gt + 1)
                src = SA[:, (t % 2) * 512:(t % 2) * 512 + nt]
                skey = ("SA", t % 2)
                for kc in range(KC):
                    b.mm(src, w[:, kc, :], hblk[:, kc, 0:nt], kc == 0, kc == KC - 1, [wk, "hblk"], [skey])

            proj(0)
            prev = None
            for t in range(16):
                p = t % 8
                if t + 1 < 16:
                    proj(t + 1)
                src = SA[:, (t % 2) * 512:(t % 2) * 512 + nt]
                skey = ("SA", t % 2)
                if t < 8:
                    st3 = qk_stages(src, skey, nt, 0, [(qr[:, p, 0:nt], ("qr", p))])
                    st3[0]()
                    if prev is not None:
                        prev[1]()
                        prev[2]()
                    prev = st3
                else:
                    if prev is not None:
                        prev[1]()
                        prev[2]()
                        prev = None
                    b.act(sg[:, p, 0:nt], src, AF.Silu, [skey], [("sg", p)])
            if bi + 1 < nblk:
                load_block_inputs(bi + 1)
            if CUT <= 4:
                return
            for p in range(8):
                a = p // 4
                ob = Ob[p % 2]
                okey = Okey[p % 2]

                def qk(kt):
                    S = Sb[kt % 2]
                    for hh in range(2):
                        b.mm(S[:, hh * 512:hh * 512 + nt], KT[hh * 64:(hh + 1) * 64, a, kt * 128:(kt + 1) * 128],
                             qr[hh * 64:(hh + 1) * 64, p, 0:nt], True, True, [("KT", a), ("qr", p)],
                             [Sk[kt % 2][hh]])

                qk(0)
                if nkt > 1:
                    qk(1)
                for kt in range(nkt):
                    S = Sb[kt % 2]
                    P = Pt[kt % 3]
                    b.act(P[:, :, 0:nt], S[:].rearrange("p (h t) -> p h t", h=2)[:, :, 0:nt], AF.Exp,
                          Sk[kt % 2], [("P", kt % 3)], scale=0.125)
                    if kt + 2 < nkt:
                        qk(kt + 2)
                    for hh in range(2):
                        b.mm(ob[hh][:, 0:nt], VA[:, kt, 2 * a + hh, :], P[:, hh, 0:nt], kt == 0, kt == nkt - 1,
                             [("VA", kt), "VAones", ("P", kt % 3)], [okey[hh]])
                for hh in range(2):
                    lo, hi = hh * 64, (hh + 1) * 64
                    b.tt(ftmp[lo:hi, 0:nt], ob[hh][0:64, 0:nt], sg[lo:hi, p, 0:nt], ALU.mult,
                         [okey[hh], ("sg", p)], [("ftmp", hh)])
                    b.rcp(frec[lo:hi, 0:nt], ob[hh][64:128, 0:nt], [okey[hh]], [("frec", hh)])
                    b.tt(qr[lo:hi, p, 0:nt], ftmp[lo:hi, 0:nt], frec[lo:hi, 0:nt], ALU.mult,
                         [("ftmp", hh), ("frec", hh)], [("qr", p)])
            if CUT <= 5:
                return
            issue_w((bi + 1) * 16)
            issue_w((bi + 1) * 16 + 1)
            def outproj(ti_):
                S_, sn = (SA, "SA") if ti_ % 2 == 0 else (SB, "SB")
                g0_ = tok0 + t0 + ti_ * 128
                b.dma(xt[ti_ % 2][:], xs[g0_:g0_ + 128, :], [], ["l0x%d" % (ti_ % 2)])
                for half in range(2):
                    for p in range(8):
                        b.mm(S_[:, half * 512:(half + 1) * 512], qr[:, p, ti_ * 128:(ti_ + 1) * 128],
                             wout[:, p, half * 512:(half + 1) * 512], p == 0, p == 7,
                             [("qr", p), ("wout", p)], [(sn, half)])

            outproj(0)
            for ti in range(nt // 128):
                g0 = tok0 + t0 + ti * 128
                xti = xt[ti % 2]
                S_, sn = (SA, "SA") if ti % 2 == 0 else (SB, "SB")
                if ti + 1 < nt // 128:
                    outproj(ti + 1)
                for half in range(2):
                    x1h = x1[ti % 2][:, half * 512:(half + 1) * 512]
                    b.tt(x1h, S_[:, half * 512:(half + 1) * 512],
                         gate_bc[0][cond][:, half * 512:(half + 1) * 512], ALU.mult,
                         [(sn, half), "gbc0%d" % cond], [("l0x1%d" % (ti % 2), half)])
                    b.tt(x1h, x1h, xti[:, half * 512:(half + 1) * 512], ALU.add,
                         [("l0x1%d" % (ti % 2), half), "l0x%d" % (ti % 2)], [("l0x1%d" % (ti % 2), half)])
                x1k_ = [("l0x1%d" % (ti % 2), 0), ("l0x1%d" % (ti % 2), 1)]
                b.dma(x1s[g0:g0 + 128, :], x1[ti % 2][:], x1k_, [("x1s", g0)])
                prep_tile(ti, x1[ti % 2][:], x1k_, pst, pk)
            prep_evac(nt, 1, cond, h1b, "h1b", pst, pk)
            c0 = H1OFF[si] + 1 + t0
            b.dma(h1T[:, :, c0:c0 + nt], h1b[:, :, 0:nt], ["h1b"], [("h1T", si)])
        if CUT <= 7:
            return
        b.dma(h1T[:, :, H1OFF[si]:H1OFF[si] + 1], zcol[:, :, 0:1], ["zcol"], [("h1T", si)],
              allow_slow_non_contiguous=True)
        b.dma(h1T[:, :, H1OFF[si] + T + 1:H1OFF[si] + T + 2], zcol[:, :, 1:2], ["zcol"], [("h1T", si)],
              allow_slow_non_contiguous=True)
        if not is_s:
            prompt_i += 1


def layer1(b, g):
    tk = b.tk
    nc = b.nc
    w1in, w1out = g["w1in"], g["w1out"]
    h1T, x1s = g["h1T"], g["x1s"]
    convw_d, dtb_d, alog_d, dsk_d, gnw_d = g["convw_d"], g["dtb_d"], g["alog_d"], g["dsk_d"], g["gnw_d"]
    utf_d, utb_d, negf_d, negb_d, i4_d = g["utf_d"], g["utb_d"], g["negf_d"], g["negb_d"], g["i4_d"]
    stf, stb, y_o, sf_o, sb_o = g["stf"], g["stb"], g["y_o"], g["sf_o"], g["sb_o"]
    ident_bf, ident_f, ones_bf, gate_bc, eps_t = g["ident_bf"], g["ident_f"], g["ones_bf"], g["gate_bc"], g["eps_t"]
    dscr = g["dscr"]
    LCUT = float(os.environ.get("K_LCUT", "99"))

    xtok_s = dscr("xtok_s", [NCH, 128, 2048], BF16, dbg=True)
    btok_s = dscr("btok_s", [NCH, 128, 512], BF16, dbg=True)
    bT_s = dscr("bT_s", [NCH, 128, 4, 128], BF16, dbg=True)
    cT_s = dscr("cT_s", [NCH, 128, 4, 128], BF16, dbg=True)
    sz_s = dscr("sz_s", [NCH, 128, 2048], BF16, dbg=True)
    dts_s = dscr("dts_s", [NCH, 128, 192], F32, dbg=True)
    yp_s = dscr("yp_s", [NCH, 128, 2048], F32, dbg=True)
    xdf_s = dscr("xdf_s", [NCH, 128, 2048], BF16)
    eac_s = dscr("eac_s", [NCH, 128, 64], F32)

    with ExitStack() as ph:
        es_save = b.es
        b.es = ph
        w1 = b.sb("w1", [128, KC, 5184], BF16)
        w1_s = g["w1_s"]
        for kc in range(KC):
            b.dma(w1[:, kc, :], w1_s[kc * 128:(kc + 1) * 128, :], [("w1_s", kc // 2)], [("w1", kc)])
        w1k = [("w1", kc) for kc in range(KC)]
        cw = b.sb("cw", [128, 24, 4])
        dtb_bc = b.sb("dtb_bc", [128, 64])
        A_bc = b.sb("A_bc", [128, 64])
        b.dma(cw[:], convw_d, [], ["cw"])
        b.dma(dtb_bc[:], dtb_d.partition_broadcast(128), [], ["dtb_bc"])
        b.dma(A_bc[:], alog_d.partition_broadcast(128), [], ["A_bc"])
        b.act(A_bc[:], A_bc[:], AF.Exp, ["A_bc"], ["A_bc"])
        b.ts(A_bc[:], A_bc[:], -1.0, None, ALU.mult, None, ["A_bc"], ["A_bc"])
        hwin = [b.sb("hwin%d" % i, [128, KC, 258], BF16) for i in range(2)]
        xbcT = b.sb("xbcT", [128, 24, 256], BF16)
        acc = [b.sb("cacc%d" % i, [128, 256]) for i in range(3)]
        rawb = [b.sb("rawb%d" % i, [128, 258]) for i in range(3)]
        xtok = [b.sb("a_xtok%d" % i, [128, 2048], BF16) for i in range(2)]
        btok = [b.sb("a_btok%d" % i, [128, 512], BF16) for i in range(2)]
        szt = [b.sb("a_sz%d" % i, [128, 2048], BF16) for i in range(2)]
        dtt = [b.sb("a_dt%d" % i, [128, 192]) for i in range(2)]
        dtmp = b.sb("a_dtmp", [128, 64])
        RA = [b.ps("a_R%d" % i, [128, 512]) for i in range(2)]
        TA = b.ps("a_T", [128, 1024], keys=[("a_T", 0), ("a_T", 1)])
        ZA = [b.ps("a_Z%d" % i, [128, 512]) for i in range(2)]
        DA = b.ps("a_D", [128, 512])
        TAv = TA[:].bitcast(BF16)
        DAv = DA[:].bitcast(BF16)
        ci = 0
        wins = [(si, tok0, w0) for si, (tok0, T, is_s) in enumerate(SEQS) for w0 in range(0, T, 256)]

        def load_win(i):
            si_, _, w0_ = wins[i]
            c0_ = H1OFF[si_] + w0_
            b.dma(hwin[i % 2][:], h1T[:, :, c0_:c0_ + 258], [("h1T", si_)], ["hwin%d" % (i % 2)])

        load_win(0)
        for wi, (si, tok0, w0) in enumerate(wins):
            if True:
                hw_ = hwin[wi % 2]
                hk = "hwin%d" % (wi % 2)
                if wi + 1 < len(wins):
                    load_win(wi + 1)
                for cb in range(24):
                    R_ = RA[cb % 2]
                    rk = "a_R%d" % (cb % 2)
                    ac_ = acc[cb % 3]
                    ak = "cacc%d" % (cb % 3)
                    for kc in range(KC):
                        b.mm(R_[:, 0:258], w1[:, kc, 2048 + cb * 128:2048 + (cb + 1) * 128], hw_[:, kc, :],
                             kc == 0, kc == KC - 1, [w1k[kc], hk], [rk])
                    rw_ = rawb[cb % 3]
                    rwk = "rawb%d" % (cb % 3)
                    eng = "dve"
                    b.act(rw_[:], R_[:, 0:258], AF.Copy, [rk], [rwk])
                    b.ts(ac_[:], rw_[:, 1:257], cw[:, cb, 1:2], cw[:, cb, 3:4], ALU.mult, ALU.add, [rwk, "cw"], [ak], q=eng)
                    b.stt(ac_[:], rw_[:, 0:256], cw[:, cb, 0:1], ac_[:], ALU.mult, ALU.add, [rwk, ak, "cw"], [ak], q=eng)
                    b.stt(ac_[:], rw_[:, 2:258], cw[:, cb, 2:3], ac_[:], ALU.mult, ALU.add, [rwk, ak, "cw"], [ak], q=eng)
                    b.act(xbcT[:, cb, :], ac_[:], AF.Silu, [ak], [("xbcT", cb)])
                for ch in range(2):
                    gc = (tok0 + w0) // 128 + ch
                    cs = slice(ch * 128, (ch + 1) * 128)
                    xt_, xk = xtok[ci % 2], "a_xtok%d" % (ci % 2)
                    bt_, bk = btok[ci % 2], "a_btok%d" % (ci % 2)
                    sz_, sk = szt[ci % 2], "a_sz%d" % (ci % 2)
                    dt_, dk = dtt[ci % 2], "a_dt%d" % (ci % 2)
                    ci += 1
                    for cb in range(16):
                        b.tr(TAv[:, cb * 128:(cb + 1) * 128], xbcT[:, cb, cs], ident_bf[:], [("xbcT", cb), "ident_bf"],
                             [("a_T", cb // 8)])
                    b.act(xt_[:, 0:1024], TAv[:, 0:1024], AF.Copy, [("a_T", 0)], [(xk, 0)])
                    b.cp(xt_[:, 1024:2048], TAv[:, 1024:2048], [("a_T", 1)], [(xk, 1)])
                    b.dma(xtok_s[gc], xt_[:], [(xk, 0), (xk, 1)], [("xtok_s", gc)])
                    for g4 in range(4):
                        b.tr(DAv[:, g4 * 128:(g4 + 1) * 128], xbcT[:, 16 + g4, cs], ident_bf[:],
                             [("xbcT", 16 + g4), "ident_bf"], ["a_D"])
                    b.cp(bt_[:], DAv[:, 0:512], ["a_D"], [bk])
                    b.dma(btok_s[gc], bt_[:], [bk], [("btok_s", gc)])
                    b.dma(bT_s[gc], xbcT[:, 16:20, cs], [("xbcT", 16 + i) for i in range(4)], [("bT_s", gc)])
                    b.dma(cT_s[gc], xbcT[:, 20:24, cs], [("xbcT", 20 + i) for i in range(4)], [("cT_s", gc)])
                    for zb in range(4):
                        Z_ = ZA[zb % 2]
                        zk = "a_Z%d" % (zb % 2)
                        for kc in range(KC):
                            b.mm(Z_[:], hw_[:, kc, 1 + ch * 128:1 + (ch + 1) * 128], w1[:, kc, zb * 512:(zb + 1) * 512],
                                 kc == 0, kc == KC - 1, [w1k[kc], hk], [zk])
                        b.act(sz_[:, zb * 512:(zb + 1) * 512], Z_[:], AF.Silu, [zk], [(sk, zb)])
                    b.dma(sz_s[gc], sz_[:], [(sk, i) for i in range(4)], [("sz_s", gc)])
                    for kc in range(KC):
                        b.mm(DA[:, 256:320], hw_[:, kc, 1 + ch * 128:1 + (ch + 1) * 128], w1[:, kc, 5120:5184],
                             kc == 0, kc == KC - 1, [w1k[kc], hk], ["a_D"])
                    b.tt(dtmp[:], DA[:, 256:320], dtb_bc[:], ALU.add, ["a_D", "dtb_bc"], ["a_dtmp"])
                    b.act(dtmp[:], dtmp[:], AF.Exp, ["a_dtmp"], ["a_dtmp"])
                    b.act(dt_[:, 0:64], dtmp[:], AF.Ln, ["a_dtmp"], [dk], bias=1.0)
                    b.act(dt_[:, 64:128], dt_[:, 0:64], AF.Ln, [dk], [dk])
                    b.tt(dt_[:, 128:192], dt_[:, 0:64], A_bc[:], ALU.mult, [dk, "A_bc"], [dk])
                    b.dma(dts_s[gc], dt_[:], [dk], [("dts_s", gc)])
        tk.barrier()
        b.es = es_save
    if LCUT <= 1:
        return

    def make_state_fns(hst, hst_bf, htmp, stin, stout, ST, ST_alt=None, htmp_alt=None):
        def load_state(src):
            for blk in range(16):
                b.dma(stin[:], src[blk * 128:(blk + 1) * 128, :], [], ["stin"])
                b.tr(ST[:, 0:128], stin[:], ident_f[:], ["stin", "ident_f"], ["b_ST"])
                b.cp(hst[:, blk * 128:(blk + 1) * 128], ST[:, 0:128], ["b_ST"], [("hst", blk // 4)])
            for g4 in range(4):
                b.act(hst_bf[:, g4 * 512:(g4 + 1) * 512], hst[:, g4 * 512:(g4 + 1) * 512], AF.Copy, [("hst", g4)],
                      [("hst_bf", g4)])

        def zero_state():
            for g4 in range(4):
                b.mset(hst[:, g4 * 512:(g4 + 1) * 512], 0.0, [("hst", g4)])
                b.mset(hst_bf[:, g4 * 512:(g4 + 1) * 512], 0.0, [("hst_bf", g4)], q="pool")

        def store_state(dst):
            for blk in range(16):
                b.tr(ST[:, 0:128], hst[:, blk * 128:(blk + 1) * 128], ident_f[:], [("hst", blk // 4), "ident_f"], ["b_ST"])
                b.cp(stout[:], ST[:, 0:128], ["b_ST"], ["stout"])
                b.dma(dst[blk * 128:(blk + 1) * 128, :], stout[:], ["stout"], [], is_out=True)

        def state_update(bt_, bk, xd_, xdk, cdt, cdk, off):
            sts = [(ST, "b_ST")] + ([(ST_alt, "b_ST2")] if ST_alt is not None else [])
            tmps = [(htmp, "htmp")] + ([(htmp_alt, "htmp2")] if htmp_alt is not None else [])
            nb_ = len(sts)
            for g0 in range(0, 4, nb_):
                for g4 in range(g0, g0 + nb_):
                    S_, sk_ = sts[g4 % nb_]
                    T_, tk_ = tmps[g4 % len(tmps)]
                    b.mm(S_[:], bt_[:, g4 * 128:(g4 + 1) * 128], xd_[:, g4 * 512:(g4 + 1) * 512], True, True,
                         [bk, xdk], [sk_])
                    hv = hst[:, g4 * 512:(g4 + 1) * 512].rearrange("p (h q) -> p h q", h=8)
                    b.tt(T_[:].rearrange("p (h q) -> p h q", h=8), hv,
                         bc(cdt[:, off + g4 * 8:off + (g4 + 1) * 8].unsqueeze(2), [128, 8, 64]), ALU.mult,
                         [("hst", g4), cdk], [tk_], q="pool")
                for g4 in range(g0, g0 + nb_):
                    S_, sk_ = sts[g4 % nb_]
                    T_, tk_ = tmps[g4 % len(tmps)]
                    b.tt(hst[:, g4 * 512:(g4 + 1) * 512], T_[:], S_[:], ALU.add, [tk_, sk_], [("hst", g4)])
                    b.act(hst_bf[:, g4 * 512:(g4 + 1) * 512], hst[:, g4 * 512:(g4 + 1) * 512], AF.Copy, [("hst", g4)],
                          [("hst_bf", g4)])
        return load_state, zero_state, store_state, state_update

    with ExitStack() as ph:
        es_save = b.es
        b.es = ph
        utri = [b.sb("utri%d" % d, [128, 128]) for d in range(2)]
        negm = [b.sb("negm%d" % d, [128, 128], BF16) for d in range(2)]
        i4 = b.sb("i4_sb", [128, 512], BF16)
        D_bc = b.sb("D_bc", [128, 32])
        DI = b.sb("DI", [128, 32, 128], BF16)
        ones_f = g["ones_f"]
        b.dma(utri[0][:], utf_d, [], ["utri0"])
        b.dma(utri[1][:], utb_d, [], ["utri1"])
        b.dma(negm[0][:], negf_d, [], ["negm0"], q="pool")
        b.dma(negm[1][:], negb_d, [], ["negm1"], q="pool")
        b.dma(i4[:], i4_d, [], ["i4"], q="pool")
        b.dma(D_bc[:], dsk_d.partition_broadcast(128), [], ["D_bc"])
        b.tt(DI[:], bc(ident_bf[:].unsqueeze(1), [128, 32, 128]), bc(D_bc[:].unsqueeze(2), [128, 32, 128]), ALU.mult,
             ["ident_bf", "D_bc"], ["DI"])

        NL = 3
        xt2 = [b.sb("b_xtok%d" % i, [128, 2048], BF16) for i in range(NL)]
        bt2 = [b.sb("b_btok%d" % i, [128, 512], BF16) for i in range(NL)]
        bT2 = [b.sb("b_bT%d" % i, [128, 4, 128], BF16) for i in range(NL)]
        cT2 = [b.sb("b_cT%d" % i, [128, 4, 128], BF16) for i in range(NL)]
        dt2 = [b.sb("b_dt%d" % i, [128, 192]) for i in range(NL)]
        acs2 = [b.sb("acs%d" % i, [128, 64]) for i in range(2)]
        ea2 = [b.sb("ea%d" % i, [128, 64]) for i in range(2)]
        de2 = [b.sb("de%d" % i, [128, 64]) for i in range(2)]
        cd2 = [b.sb("cd%d" % i, [128, 64]) for i in range(2)]
        wl2 = [b.sb("wl%d" % i, [128, 64]) for i in range(2)]
        nb2 = [b.sb("nb%d" % i, [128, 64]) for i in range(2)]
        Dm = [b.sb("Dm%d" % i, [128, 512]) for i in range(4)]
        Eb = [b.sb("E%d" % d, [128, 4096], BF16) for d in range(2)]
        MT2 = [[b.sb("MT%d_%d" % (i, d), [128, 4096], BF16) for d in range(2)] for i in range(2)]
        cbT = b.sb("cbT", [128, 512], BF16)
        xdec2 = [[b.sb("xdec%d_%d" % (i, d), [128, 2048], BF16) for d in range(2)] for i in range(2)]
        hst = b.sb("hst", [128, 2048])
        hst_bf = b.sb("hst_bf", [128, 2048], BF16)
        htmp = b.sb("htmp", [128, 512])
        ypt = [b.sb("ypt%d" % i, [128, 2048]) for i in range(2)]
        yot = [b.sb("yot%d" % i, [128, 512]) for i in range(2)]
        stin = b.sb("stin", [128, 128])
        stout = b.sb("stout", [128, 128])
        eact = [b.sb("eact%d" % i, [128, 64]) for i in range(2)]

        AC = b.ps("b_AC", [128, 512])
        CB = b.ps("b_CB", [128, 512])
        DB = [b.ps("b_DB%d" % i, [128, 512]) for i in range(2)]
        YG = [b.ps("b_YG%d" % i, [128, 512]) for i in range(2)]
        YO = b.ps("b_YO", [128, 512])
        ST = b.ps("b_ST", [128, 512])
        load_state, zero_state, store_state, state_update = make_state_fns(hst, hst_bf, htmp, stin, stout, ST)
        v3 = lambda t: t[:].rearrange("p (h l) -> p h l", h=32)

        items = []
        for si, (tok0, T, is_s) in enumerate(SEQS):
            nch = T // 128
            for c in range(nch - 1, -1, -1):
                items.append((tok0 // 128 + c, c == nch - 1, c == 0, si))

        def load_b(i):
            gcx, sx = items[i][0], i % NL
            b.dma(xt2[sx][:], xtok_s[gcx], [("xtok_s", gcx)], ["b_xtok%d" % sx])
            b.dma(bt2[sx][:], btok_s[gcx], [("btok_s", gcx)], ["b_btok%d" % sx])
            b.dma(bT2[sx][:], bT_s[gcx], [("bT_s", gcx)], ["b_bT%d" % sx])
            b.dma(cT2[sx][:], cT_s[gcx], [("cT_s", gcx)], ["b_cT%d" % sx])
            b.dma(dt2[sx][:], dts_s[gcx], [("dts_s", gcx)], ["b_dt%d" % sx])

        def early_stages(i):
            sl, s2 = i % NL, i % 2
            xt_, xk = xt2[sl], "b_xtok%d" % sl
            bT_, bTk = bT2[sl], "b_bT%d" % sl
            cT_, cTk = cT2[sl], "b_cT%d" % sl
            dt_, dk = dt2[sl], "b_dt%d" % sl
            acs, ea, de, cd, wl, nb = acs2[s2], ea2[s2], de2[s2], cd2[s2], wl2[s2], nb2[s2]
            ka = lambda n: "%s%d" % (n, s2)
            a_ = dt_[:, 128:192]

            def prologue():
                if i + 1 < len(items):
                    load_b(i + 1)
                for d in range(2):
                    b.mm(AC[:, d * 32:(d + 1) * 32], utri[d][:], a_[:, d * 32:(d + 1) * 32], True, True,
                         ["utri%d" % d, dk], ["b_AC"])
                    b.mm(AC[:, 64 + d * 32:64 + (d + 1) * 32], ones_f[:], a_[:, d * 32:(d + 1) * 32], True, True,
                         ["ones_f", dk], ["b_AC"])
                b.cp(acs[:], AC[:, 0:64], ["b_AC"], [ka("acs")])
                b.act(ea[:], acs[:], AF.Exp, [ka("acs")], [ka("ea")])
                b.act(cd[:], AC[:, 64:128], AF.Exp, ["b_AC"], [ka("cd")])
                b.tt(de[:], AC[:, 64:128], acs[:], ALU.subtract, ["b_AC", ka("acs")], [ka("de")])
                b.act(de[:], de[:], AF.Exp, [ka("de")], [ka("de")])
                b.tt(wl[:], dt_[:, 0:64], de[:], ALU.mult, [dk, ka("de")], [ka("wl")])
                b.tt(nb[:], dt_[:, 64:128], acs[:], ALU.subtract, [dk, ka("acs")], [ka("nb")])
                for g4 in range(4):
                    b.mm(CB[:, g4 * 128:(g4 + 1) * 128], bT_[:, g4, :], cT_[:, g4, :], True, True, [bTk, cTk], ["b_CB"])
                b.act(cbT[:], CB[:], AF.Copy, ["b_CB"], ["cbT"])
                xv = xt_[:].rearrange("p (h q) -> p h q", h=32)
                for d in range(2):
                    b.tt(xdec2[s2][d][:].rearrange("p (h q) -> p h q", h=32), xv,
                         bc(wl[:, d * 32:(d + 1) * 32].unsqueeze(2), [128, 32, 64]), ALU.mult, [xk, ka("wl")],
                         ["xdec%d_%d" % (s2, d)], q=("pool" if d == 0 else "dve"))

            def banks(d, js):
                for j in js:
                    DB_, dbk = DB[j % 2], "b_DB%d" % (j % 2)
                    Dm_, dmk = Dm[j % 4], "Dm%d" % (j % 4)
                    b.mm(DB_[:], negm[d][:], i4[:], True, False, ["negm%d" % d, "i4"], [dbk])
                    for hh in range(4):
                        h = j * 4 + hh
                        b.mm(DB_[:, hh * 128:(hh + 1) * 128], bc(a_[:, d * 32 + h:d * 32 + h + 1], [128, 128]),
                             utri[d][:], False, hh == 3, [dk, "utri%d" % d], [dbk])
                    b.tt(Dm_[:].rearrange("p (h l) -> p h l", h=4), DB_[:].rearrange("p (h l) -> p h l", h=4),
                         bc(nb[:, d * 32 + j * 4:d * 32 + (j + 1) * 4].unsqueeze(2), [128, 4, 128]), ALU.add,
                         [dbk, ka("nb")], [dmk])
                    b.act(Eb[d][:, j * 512:(j + 1) * 512], Dm_[:], AF.Exp, [dmk], [("E", d, j)])

            def mtbuild(d):
                b.tt(MT2[s2][d][:].rearrange("p (g r l) -> p g r l", g=4, r=8),
                     Eb[d][:].rearrange("p (g r l) -> p g r l", g=4, r=8),
                     bc(cbT[:].rearrange("p (g l) -> p g l", g=4).unsqueeze(2), [128, 4, 8, 128]), ALU.mult,
                     [("E", d, j) for j in range(8)] + ["cbT"], ["MT%d_%d" % (s2, d)], q="pool")

            def q0():
                banks(0, range(0, 4))

            def q1():
                banks(0, range(4, 8))
                mtbuild(0)

            def q2():
                banks(1, range(0, 4))

            def q3():
                banks(1, range(4, 8))
                mtbuild(1)
            return [prologue, q0, q1, q2, q3]

        def late_stages(i):
            gc, first, last, si = items[i]
            sl, s2 = i % NL, i % 2
            xt_, xk = xt2[sl], "b_xtok%d" % sl
            bt_, bk = bt2[sl], "b_btok%d" % sl
            cT_, cTk = cT2[sl], "b_cT%d" % sl
            ea, cd = ea2[s2], cd2[s2]
            eak, cdk = "ea%d" % s2, "cd%d" % s2
            yp_, ypk = ypt[s2], "ypt%d" % s2
            MT = MT2[s2]

            def head():
                if first:
                    if SEQS[si][2]:
                        load_state(stb)
                    else:
                        zero_state()

            def group(g4):
                YG_, ygk = YG[g4 % 2], "b_YG%d" % (g4 % 2)
                yo_, yok2 = yot[g4 % 2], "yot%d" % (g4 % 2)
                b.mm(YO[:], cT_[:, g4, :], hst_bf[:, g4 * 512:(g4 + 1) * 512], True, True, [cTk, ("hst_bf", g4)],
                     ["b_YO"])
                b.tt(yo_[:].rearrange("p (h q) -> p h q", h=8), YO[:].rearrange("p (h q) -> p h q", h=8),
                     bc(ea[:, 32 + g4 * 8:32 + (g4 + 1) * 8].unsqueeze(2), [128, 8, 64]), ALU.mult,
                     ["b_YO", eak], [yok2])
                for hh in range(8):
                    h = g4 * 8 + hh
                    xs_ = xt_[:, h * 64:(h + 1) * 64]
                    b.mm(YG_[:, hh * 64:(hh + 1) * 64], MT[0][:, h * 128:(h + 1) * 128], xs_, True, False,
                         ["MT%d_0" % s2, xk], [ygk])
                    b.mm(YG_[:, hh * 64:(hh + 1) * 64], MT[1][:, h * 128:(h + 1) * 128], xs_, False, False,
                         ["MT%d_1" % s2, xk], [ygk])
                    b.mm(YG_[:, hh * 64:(hh + 1) * 64], DI[:, h, :], xs_, False, True, ["DI", xk], [ygk])
                b.tt(yp_[:, g4 * 512:(g4 + 1) * 512], yo_[:], YG_[:], ALU.add, [yok2, ygk], [(ypk, g4)])

            def tail():
                b.dma(yp_s[gc], yp_[:], [(ypk, j) for j in range(4)], [("yp_s", gc)])
                b.dma(xdf_s[gc], xdec2[s2][0][:], ["xdec%d_0" % s2], [("xdf_s", gc)])
                ec_, eck = eact[s2], "eact%d" % s2
                b.cp(ec_[:, 0:32], ea[:, 0:32], [eak], [eck])
                b.cp(ec_[:, 32:64], cd[:, 0:32], [cdk], [eck])
                b.dma(eac_s[gc], ec_[:], [eck], [("eac_s", gc)])
                state_update(bt_, bk, xdec2[s2][1], "xdec%d_1" % s2, cd, cdk, 32)
                if last and not SEQS[si][2]:
                    store_state(sb_o[si - 1])
            return [head] + [(lambda g4=g4: group(g4)) for g4 in range(4)] + [tail]

        gnw_fm = b.sb("gnw_fm", [128, 16])
        b.dma(gnw_fm[:], gnw_d, [], ["gnw_fm"])
        w1stg = [b.sb("w1stg%d" % i, [128, D]) for i in range(2)]
        w1ob = [b.sb("w1ob%d" % i, [128, D], BF16) for i in range(2)]
        w1o_s = g["w1o_s"]

        def w1o_load(kc):
            b.dma(w1stg[kc % 2][:], w1out[kc * 128:(kc + 1) * 128, :], [], ["w1stg%d" % (kc % 2)])

        def w1o_scale(kc):
            b.act(w1ob[kc % 2][:], w1stg[kc % 2][:], AF.Copy, ["w1stg%d" % (kc % 2), "gnw_fm"], ["w1ob%d" % (kc % 2)],
                  scale=gnw_fm[:, kc:kc + 1])

        def w1o_store(kc):
            b.dma(w1o_s[kc * 128:(kc + 1) * 128, :], w1ob[kc % 2][:], ["w1ob%d" % (kc % 2)], [("w1o_s", kc)])

        load_b(0)
        for st in early_stages(0):
            st()
        for i in range(len(items)):
            es_ = early_stages(i + 1) if i + 1 < len(items) else None
            ls_ = late_stages(i)
            ls_[0]()
            kper = -(-16 // max(1, len(items) - 2))
            slabs = lambda j: range(j * kper, min(16, (j + 1) * kper)) if j >= 0 else range(0)
            if kper == 1:
                for kc_ in slabs(i - 2):
                    w1o_store(kc_)
                for kc_ in slabs(i):
                    w1o_load(kc_)
                for kc_ in slabs(i - 1):
                    w1o_scale(kc_)
            else:
                for kc_ in slabs(i):
                    w1o_load(kc_)
                    w1o_scale(kc_)
                    w1o_store(kc_)
            if es_:
                es_[0]()
            for q4 in range(4):
                if es_:
                    es_[1 + q4]()
                ls_[1 + q4]()
            ls_[5]()
        tk.barrier()
        b.es = es_save
    if LCUT <= 2:
        return

    with ExitStack() as ph:
        es_save = b.es
        b.es = ph
        w1o = b.sb("w1o", [128, 16, D], BF16)
        w1o_s = g["w1o_s"]
        for kc in range(16):
            b.dma(w1o[:, kc, :], w1o_s[kc * 128:(kc + 1) * 128, :], [("w1o_s", kc)], [("w1o", kc)])
        NB2 = 3
        bt3 = [b.sb("c_btok%d" % i, [128, 512], BF16) for i in range(NB2)]
        cT3 = [b.sb("c_cT%d" % i, [128, 4, 128], BF16) for i in range(NB2)]
        xd3 = [b.sb("c_xdf%d" % i, [128, 2048], BF16) for i in range(NB2)]
        yp3 = [b.sb("c_yp%d" % i, [128, 2048]) for i in range(NB2)]
        ec3 = [b.sb("c_ec%d" % i, [128, 64]) for i in range(NB2)]
        sz3 = [b.sb("c_sz%d" % i, [128, 2048], BF16) for i in range(NB2)]
        x13 = [b.sb("c_x1%d" % i, [128, D]) for i in range(NB2)]
        ygw3 = [b.sb("c_ygw%d" % i, [128, 2048], BF16) for i in range(NB2)]
        ss3 = [b.sb("c_ss%d" % i, [128, 8]) for i in range(NB2)]
        rs3 = [b.sb("c_rs%d" % i, [128, 2]) for i in range(NB2)]
        ygT = b.sb("c_ygT", [128, 2048], BF16)
        yout = [b.sb("c_yout%d" % i, [128, D]) for i in range(2)]
        junk2 = b.sb("c_junk2", [128, 512], BF16)
        yot = [b.sb("c_yot%d" % i, [128, 512]) for i in range(2)]
        hst = b.sb("c_hst", [128, 2048])
        hst_bf = b.sb("c_hst_bf", [128, 2048], BF16)
        htmp = b.sb("c_htmp", [128, 512])
        stin = b.sb("c_stin", [128, 128])
        stout = b.sb("c_stout", [128, 128])
        YO2 = [b.ps("c_YO%d" % i, [128, 512]) for i in range(2)]
        ST = b.ps("c_ST", [128, 512], keys=["b_ST"])
        TP2 = b.ps("c_TP", [128, 1024], keys=[("c_TP", 0), ("c_TP", 1)])
        OP2 = b.ps("c_OP", [128, 1024], keys=[("c_OP", 0), ("c_OP", 1)])
        ST2 = b.ps("c_ST2", [128, 512], keys=["b_ST2"])
        htmp2 = b.sb("c_htmp2", [128, 512])
        load_state, zero_state, store_state, state_update = make_state_fns(hst, hst_bf, htmp, stin, stout, ST, ST2, htmp2)
        TPv = TP2[:].bitcast(BF16)

        def chunk_front(gc, s_, cond):
            bt_, bk = bt3[s_], "c_btok%d" % s_
            cT_, cTk = cT3[s_], "c_cT%d" % s_
            xd_, xdk = xd3[s_], "c_xdf%d" % s_
            yp_, ypk = yp3[s_], "c_yp%d" % s_
            ec_, eck = ec3[s_], "c_ec%d" % s_
            sz_, szk = sz3[s_], "c_sz%d" % s_
            x1_, x1k = x13[s_], "c_x1%d" % s_
            ygw, ygk = ygw3[s_], "c_ygw%d" % s_
            ss4, ssk = ss3[s_], "c_ss%d" % s_
            rs, rsk = rs3[s_], "c_rs%d" % s_
            for g4 in range(4):
                YO_, yk = YO2[g4 % 2], "c_YO%d" % (g4 % 2)
                yo_, yok2 = yot[g4 % 2], "c_yot%d" % (g4 % 2)
                b.mm(YO_[:], cT_[:, g4, :], hst_bf[:, g4 * 512:(g4 + 1) * 512], True, True, [cTk, ("hst_bf", g4)], [yk])
                b.tt(yo_[:].rearrange("p (h q) -> p h q", h=8), YO_[:].rearrange("p (h q) -> p h q", h=8),
                     bc(ec_[:, g4 * 8:(g4 + 1) * 8].unsqueeze(2), [128, 8, 64]), ALU.mult, [yk, eck], [yok2])
                ysl = yp_[:, g4 * 512:(g4 + 1) * 512]
                b.tt(ysl, ysl, yo_[:], ALU.add, [(ypk, g4), yok2], [(ypk, g4)])
            state_update(bt_, bk, xd_, xdk, ec_, eck, 32)
            for g4 in range(4):
                ysl = yp_[:, g4 * 512:(g4 + 1) * 512]
                ygs = ygw[:, g4 * 512:(g4 + 1) * 512]
                b.tt(ygs, ysl, sz_[:, g4 * 512:(g4 + 1) * 512], ALU.mult, [(ypk, g4), szk], [(ygk, g4)],
                     q=("pool" if g4 % 2 == 0 else "dve"))
                b.act(junk2[:], ygs, AF.Square, [(ygk, g4)], ["c_junk2"], accum_out=ss4[:, g4:g4 + 1])
            b.tt(ss4[:, 4:5], ss4[:, 0:1], ss4[:, 1:2], ALU.add, ["c_junk2"], [ssk + "a"])
            b.tt(ss4[:, 5:6], ss4[:, 2:3], ss4[:, 3:4], ALU.add, ["c_junk2"], [ssk + "b"])
            b.tt(ss4[:, 6:7], ss4[:, 4:5], ss4[:, 5:6], ALU.add, [ssk + "a", ssk + "b"], [ssk + "c"])
            b.act(rs[:, 0:1], ss4[:, 6:7], AF.Ln, [ssk + "c", "eps_t"], [rsk + "0"], scale=1.0 / 2048, bias=eps_t[:, 0:1])
            b.act(rs[:, 1:2], rs[:, 0:1], AF.Exp, [rsk + "0"], [rsk], scale=-0.5)

        def load_c(gc, s_):
            b.dma(bt3[s_][:], btok_s[gc], [("btok_s", gc)], ["c_btok%d" % s_])
            b.dma(cT3[s_][:], cT_s[gc], [("cT_s", gc)], ["c_cT%d" % s_])
            b.dma(ec3[s_][:], eac_s[gc], [("eac_s", gc)], ["c_ec%d" % s_])
            b.dma(xd3[s_][:], xdf_s[gc], [("xdf_s", gc)], ["c_xdf%d" % s_])
            b.dma(yp3[s_][:], yp_s[gc], [("yp_s", gc)], [("c_yp%d" % s_, i) for i in range(4)])
            b.dma(sz3[s_][:], sz_s[gc], [("sz_s", gc)], ["c_sz%d" % s_])
            b.dma(x13[s_][:], x1s[gc * 128:(gc + 1) * 128, :], [("x1s", gc * 128)], ["c_x1%d" % s_])

        def chunk_back(gc, s_, cond, oi):
            x1_, x1k = x13[s_], "c_x1%d" % s_
            ygw, ygk = ygw3[s_], "c_ygw%d" % s_
            rs, rsk = rs3[s_], "c_rs%d" % s_
            yo, yok = yout[oi % 2], "c_yout%d" % (oi % 2)
            for kc in range(16):
                b.tr(TPv[:, kc * 128:(kc + 1) * 128], ygw[:, kc * 128:(kc + 1) * 128], ident_bf[:],
                     [(ygk, kc // 4), "ident_bf"], [("c_TP", kc // 8)])
            b.act(ygT[:, 0:1024], TPv[:, 0:1024], AF.Copy, [("c_TP", 0)], [("c_ygT", 0)])
            b.cp(ygT[:, 1024:2048], TPv[:, 1024:2048], [("c_TP", 1)], [("c_ygT", 1)])
            for half in range(2):
                for kc in range(16):
                    b.mm(OP2[:, half * 512:(half + 1) * 512], ygT[:, kc * 128:(kc + 1) * 128],
                         w1o[:, kc, half * 512:(half + 1) * 512], kc == 0, kc == 15,
                         [("c_ygT", kc // 8), ("w1o", kc)], [("c_OP", half)])
                b.stt(yo[:, half * 512:(half + 1) * 512], OP2[:, half * 512:(half + 1) * 512], rs[:, 1:2],
                      gate_bc[1][cond][:, half * 512:(half + 1) * 512], ALU.mult, ALU.mult,
                      [("c_OP", half), rsk, "gbc1%d" % cond], [(yok, half)])
                b.tt(yo[:, half * 512:(half + 1) * 512], yo[:, half * 512:(half + 1) * 512],
                     x1_[:, half * 512:(half + 1) * 512], ALU.add, [(yok, half), x1k], [(yok, half)], q="pool")
            b.dma(y_o[gc * 128:(gc + 1) * 128, :], yo[:], [(yok, 0), (yok, 1)], [], is_out=True)

        li = 0
        oi = 0
        prompt_i = 0
        pending = None
        for si, (tok0, T, is_s) in enumerate(SEQS):
            nch = T // 128
            gc0 = tok0 // 128
            cond = 0 if is_s else 1
            if is_s:
                load_state(stf)
            else:
                zero_state()
            for c in range(nch):
                s_ = li % NB2
                if li == 0:
                    load_c(gc0 + c, s_)
                if gc0 + c + 1 < NCH:
                    load_c(gc0 + c + 1, (li + 1) % NB2)
                li += 1
                chunk_front(gc0 + c, s_, cond)
                if pending is not None:
                    chunk_back(*pending, oi)
                    oi += 1
                pending = (gc0 + c, s_, cond)
            if not is_s:
                store_state(sf_o[prompt_i])
                prompt_i += 1
        if pending is not None:
            chunk_back(*pending, oi)
        b.es = es_save


def _consts():
    ident = np.eye(128, dtype=np.float32)
    pm = np.zeros((128, 128), np.float32)
    for m in range(128):
        d = m % 64
        part = (d % 32) // 16
        k = m + 16 if part == 0 else m - 16
        pm[k, m] = 1.0
    bones = np.zeros((128, 128), np.float32)
    bones[0:64, 0:64] = 1.0
    bones[64:128, 64:128] = 1.0
    s = np.arange(128)[:, None]
    l = np.arange(128)[None, :]
    utf = (s <= l).astype(np.float32)
    utb = (s >= l).astype(np.float32)
    negf = np.where(s < l, -30000.0, 0.0).astype(np.float32)
    negb = np.where(s > l, -30000.0, 0.0).astype(np.float32)
    i4 = np.tile(ident, (1, 4)).astype(np.float32)
    t = np.arange(TS)
    pos = np.stack([t // 64, t % 64]).astype(np.float32)
    inv = (1.0 / (10000.0 ** (np.arange(0, 32, 2, dtype=np.float32) / 32.0))).astype(np.float32)
    cosT = np.zeros((128, TS), np.float32)
    sinT = np.zeros((128, TS), np.float32)
    for m in range(128):
        d = m % 64
        j, part, i = d // 32, (d % 32) // 16, d % 16
        ang = (pos[j] * inv[i]).astype(np.float32)
        cosT[m] = np.cos(ang)
        sinT[m] = np.sin(ang) * (-1.0 if part == 0 else 1.0)
    return dict(ident=ident, pm=pm, bones=bones, utf=utf, utb=utb, negf=negf, negb=negb, i4=i4, cosT=cosT, sinT=sinT)


def _fm(v, n):
    return np.ascontiguousarray(np.asarray(v, np.float32).reshape(n, 128).T)


def make_in_maps(inp):
    c = _consts()
    f32 = lambda a: np.ascontiguousarray(np.asarray(a, np.float32))
    shared = dict(c)
    for k in ("l0_mod_w", "l1_mod_w", "l0_w_in", "l0_w_out", "l1_w_in", "l1_w_out"):
        shared[k] = f32(inp[k])
    shared["l0_nw"] = _fm(inp["l0_norm_w"], 8)
    shared["l1_nw"] = _fm(inp["l1_norm_w"], 8)
    shared["l0_mb"] = _fm(inp["l0_mod_b"], 24)
    shared["l1_mb"] = _fm(inp["l1_mod_b"], 24)
    shared["l0_mbg"] = f32(inp["l0_mod_b"])[2048:3072].reshape(1, 1024)
    shared["l1_mbg"] = f32(inp["l1_mod_b"])[2048:3072].reshape(1, 1024)
    gq, gk = f32(inp["l0_q_norm"]), f32(inp["l0_k_norm"])
    d = np.arange(128) % 64
    part = (d % 32) // 16
    pi = np.where(part == 0, d + 16, d - 16)
    shared["gvec"] = np.ascontiguousarray(np.stack([gq[d], gq[pi], gk[d], gk[pi]], axis=1))
    cw = f32(inp["l1_conv_w"])
    cb_ = f32(inp["l1_conv_b"])
    convw = np.stack([cw[0], cw[1], cw[2], cb_], axis=1)
    shared["convw"] = np.ascontiguousarray(convw.reshape(24, 128, 4).transpose(1, 0, 2))
    shared["dtb"] = np.concatenate([f32(inp["l1_dt_bias_f"]), f32(inp["l1_dt_bias_b"])]).reshape(1, 64)
    shared["alog"] = np.concatenate([f32(inp["l1_a_log_f"]), f32(inp["l1_a_log_b"])]).reshape(1, 64)
    shared["dsk"] = f32(inp["l1_d_skip"]).reshape(1, 32)
    shared["gnw"] = _fm(inp["l1_gnorm_w"], 16)
    shared["gnrow"] = f32(inp["l1_gnorm_w"]).reshape(1, 2048)
    xp, xsm = f32(inp["x_prompt"]), f32(inp["x_sample"])
    maps = []
    for core in range(8):
        m = dict(shared)
        m["xs"] = np.ascontiguousarray(np.concatenate([xsm[core], xp[2 * core], xp[2 * core + 1]], axis=0))
        m["ck"] = f32(inp["cache_k_l0"])[core].reshape(CTX, 256)
        m["cv"] = f32(inp["cache_v_l0"])[core].reshape(CTX, 256)
        m["stf"] = f32(inp["state_fwd_l1"])[core].reshape(2048, 128)
        m["stb"] = f32(inp["state_bwd_l1"])[core].reshape(2048, 128)
        cc = np.stack([f32(inp["c"])[core], f32(inp["c_ctx"])], axis=1)
        m["cvec"] = np.ascontiguousarray(cc.reshape(8, 128, 2).transpose(1, 0, 2))
        maps.append(m)
    return maps


_NC_CACHE = {}


def kernel(**inputs):
    if "nc" not in _NC_CACHE:
        _NC_CACHE["nc"] = build_program()
    nc = _NC_CACHE["nc"]
    maps = make_in_maps(inputs)
    res = run_bass_kernel_spmd(nc, maps, core_ids=list(range(8)))
    r = res.results
    y_prompt = np.zeros((16, TP, D), np.float32)
    y_sample = np.zeros((8, TS, D), np.float32)
    nk = np.zeros((16, TP, 4, 64), np.float32)
    nv = np.zeros((16, TP, 4, 64), np.float32)
    sf = np.zeros((16, 32, 64, 128), np.float32)
    sbw = np.zeros((16, 32, 64, 128), np.float32)
    for c in range(8):
        y = r[c]["y"]
        y_sample[c] = y[0:TS]
        y_prompt[2 * c] = y[TS:TS + TP]
        y_prompt[2 * c + 1] = y[TS + TP:]
        nk[2 * c:2 * c + 2] = r[c]["nk"].reshape(2, TP, 4, 64)
        nv[2 * c:2 * c + 2] = r[c]["nv"].reshape(2, TP, 4, 64)
        sf[2 * c:2 * c + 2] = r[c]["sf"].reshape(2, 32, 64, 128)
        sbw[2 * c:2 * c + 2] = r[c]["sbw"].reshape(2, 32, 64, 128)
    return (y_prompt, y_sample, nk, nv, sf, sbw)
```

```python
import os
import numpy as np
from contextlib import ExitStack
import concourse.bass as bass
import concourse.mybir as mybir
from concourse.bass_utils import run_bass_kernel_spmd

F32 = mybir.dt.float32
BF16 = mybir.dt.bfloat16
AF = mybir.ActivationFunctionType
ALU = mybir.AluOpType

D = 1024
KC = 8
TS = int(os.environ.get("K_TS", "4096"))
TP = 256
NTOK = TS + 2 * TP
CTX = 512
EPS = 1e-6
SEQS = [(0, TS, True), (TS, TP, False), (TS + TP, TP, False)]
NCH = NTOK // 128
H1OFF = [0, TS + 2, TS + 2 + TP + 2]
H1COLS = TS + 2 + 2 * (TP + 2)

ENGS = ["pe", "act", "dve", "pool", "sp"]
NSLOT = 8


class Op:
    __slots__ = ("q", "name", "args", "kw", "deps", "sig", "cnt", "dma", "slot", "dval", "gi")


class TK:
    def __init__(self):
        self.ops = {q: [] for q in ENGS}
        self.lastw = {}
        self.rd = {}
        self.ndma = {q: 0 for q in ENGS}
        self.fence = []
        self.fenced = {q: True for q in ENGS}
        self.last = {q: None for q in ENGS}
        self.dma_live = {}
        self.n = 0
        self.psum_keys = set()

    def op(self, q, name, args=(), kw=None, R=(), W=(), dma=False):
        o = Op()
        o.q, o.name, o.args, o.kw, o.dma = q, name, args, (kw or {}), dma
        o.sig, o.cnt, o.slot, o.dval = False, 0, 0, 0
        o.gi = self.n
        self.n += 1
        deps = {}
        if self.psum_keys:
            pr = [k for k in R if k in self.psum_keys]
            if pr:
                R = [k for k in R if k not in self.psum_keys]
                W = list(W) + [k for k in pr if k not in W]

        def need(d, raw):
            if d is None:
                return
            if d.dma or dma or d.q != q or raw or q != "pe":
                deps[d.gi] = d

        for k in R:
            need(self.lastw.get(k), True)
        for k in W:
            need(self.lastw.get(k), False)
            for r in self.rd.get(k, ()):
                need(r, False)
        if not self.fenced[q]:
            for d in self.fence:
                if d is not None:
                    deps[d.gi] = d
            self.fenced[q] = True
        o.deps = list(deps.values())
        for k in R:
            lst = self.rd.setdefault(k, [])
            if not dma:
                for i, r in enumerate(lst):
                    if (not r.dma) and r.q == q:
                        lst[i] = o
                        break
                else:
                    lst.append(o)
            else:
                lst.append(o)
        for k in W:
            self.lastw[k] = o
            self.rd[k] = []
        if dma:
            i = self.ndma[q]
            self.ndma[q] += 1
            o.slot = i % NSLOT
            o.dval = 16 * (i // NSLOT + 1)
            self.dma_live[(q, o.slot)] = o
        else:
            self.last[q] = o
        self.ops[q].append(o)
        return o

    def barrier(self, skip_dma_queues=()):
        self.fence = [self.last[q] for q in ENGS] + [o for (q, _), o in self.dma_live.items()
                                                     if q not in skip_dma_queues]
        self.fenced = {q: False for q in ENGS}

    def emit(self, nc, es, final_deps):
        fo = Op()
        fo.q, fo.name, fo.args, fo.kw, fo.dma = "sp", None, (), {}, False
        fo.sig, fo.cnt, fo.slot, fo.dval, fo.gi = False, 0, 0, 0, self.n
        fo.deps = [d for d in final_deps if d is not None] + [self.last[q] for q in ENGS if self.last[q] is not None] \
            + list(self.dma_live.values())
        self.ops["sp"].append(fo)
        for q in ENGS:
            for o in self.ops[q]:
                for d in o.deps:
                    if not d.dma:
                        d.sig = True
        for q in ENGS:
            c = 0
            for o in self.ops[q]:
                if o.sig:
                    c += 1
                    o.cnt = c
        sem = {q: es.enter_context(nc.semaphore("s_" + q)) for q in ENGS}
        dsem = {}
        for q in ENGS:
            if self.ndma[q]:
                dsem[q] = [es.enter_context(nc.semaphore("d_%s%d" % (q, i))) for i in range(NSLOT)]
        block = es.enter_context(nc.Block())
        bname = {"pe": "tensor", "act": "scalar", "dve": "vector", "pool": "gpsimd", "sp": "sync"}
        ops = self.ops

        def run(eng, q):
            waited = {}

            def w(key, s, val):
                if waited.get(key, 0) < val:
                    eng.wait_ge(s, val)
                    waited[key] = val

            for o in ops[q]:
                for d in o.deps:
                    if d.dma:
                        w((d.q, d.slot), dsem[d.q][d.slot], d.dval)
                    else:
                        w(d.q, sem[d.q], d.cnt)
                if o.dma and o.dval > 16:
                    w((q, o.slot), dsem[q][o.slot], o.dval - 16)
                if o.name is None:
                    continue
                ins = getattr(eng, o.name)(*o.args, **o.kw)
                if o.dma:
                    ins.then_inc(dsem[q][o.slot], 16)
                elif o.sig:
                    ins.then_inc(sem[q], 1)

        for q in ENGS:
            getattr(block, bname[q])(lambda eng, q=q: run(eng, q))


class B:
    def __init__(self, nc, es):
        self.nc, self.es, self.tk = nc, es, TK()
        self.outs = []

    def sb(self, name, shape, dt=F32):
        return self.es.enter_context(self.nc.sbuf_tensor(name, list(shape), dt))

    def ps(self, name, shape, dt=F32, keys=None):
        for k in (keys or [name]):
            self.tk.psum_keys.add(k)
        return self.es.enter_context(self.nc.psum_tensor(name, list(shape), dt))

    def dma(self, out, in_, R, W, q="sp", is_out=False, **kw):
        o = self.tk.op(q, "dma_start", (), dict(out=out, in_=in_, **kw), R, W, dma=True)
        if is_out:
            self.outs.append(o)
        return o

    def mm(self, out, lhsT, rhs, start, stop, R, W, **kw):
        return self.tk.op("pe", "matmul", (out,), dict(lhsT=lhsT, rhs=rhs, start=start, stop=stop, **kw), R, W)

    def tr(self, out, in_, ident, R, W):
        return self.tk.op("pe", "transpose", (out, in_, ident), {}, R, W)

    def act(self, out, in_, func, R, W, **kw):
        return self.tk.op("act", "activation", (), dict(out=out, in_=in_, func=func, **kw), R, W)

    def tt(self, out, in0, in1, op, R, W, q="dve"):
        return self.tk.op(q, "tensor_tensor", (), dict(out=out, in0=in0, in1=in1, op=op), R, W)

    def ts(self, out, in0, s1, s2, op0, op1, R, W, q="dve"):
        kw = dict(out=out, in0=in0, scalar1=s1, scalar2=s2, op0=op0)
        if op1 is not None:
            kw["op1"] = op1
        return self.tk.op(q, "tensor_scalar", (), kw, R, W)

    def stt(self, out, in0, scalar, in1, op0, op1, R, W, q="dve"):
        return self.tk.op(q, "scalar_tensor_tensor", (), dict(out=out, in0=in0, scalar=scalar, in1=in1, op0=op0, op1=op1), R, W)

    def cp(self, out, in_, R, W, q="dve"):
        return self.tk.op(q, "tensor_copy", (), dict(out=out, in_=in_), R, W)

    def rcp(self, out, in_, R, W):
        return self.tk.op("dve", "reciprocal", (), dict(out=out, in_=in_), R, W)

    def mset(self, ap, val, W, q="dve"):
        return self.tk.op(q, "memset", (ap, val), {}, (), W)


def bc(ap, shape):
    return ap.to_broadcast(list(shape))


def build_program(stage=99, debug=False):
    nc = bass.Bass("TRN2", target_bir_lowering=False)
    es = ExitStack()
    b = B(nc, es)
    tk = b.tk

    def din(name, shape, dt=F32):
        return nc.dram_tensor(name, list(shape), dt, kind="ExternalInput").ap()

    def dout(name, shape, dt=F32):
        return nc.dram_tensor(name, list(shape), dt, kind="ExternalOutput").ap()

    def dscr(name, shape, dt=F32, dbg=False):
        if dbg and debug:
            return nc.dram_tensor(name, list(shape), dt, kind="ExternalOutput").ap()
        return nc.dram_tensor(name, list(shape), dt).ap()

    xs = din("xs", [NTOK, D])
    ck = din("ck", [CTX, 256])
    cv = din("cv", [CTX, 256])
    stf = din("stf", [2048, 128])
    stb = din("stb", [2048, 128])
    cvec = din("cvec", [128, KC, 2])
    modw = [din("l0_mod_w", [D, 3 * D]), din("l1_mod_w", [D, 3 * D])]
    w0in = din("l0_w_in", [D, 2560])
    w0out = din("l0_w_out", [D, D])
    w1in = din("l1_w_in", [D, 5184])
    w1out = din("l1_w_out", [2048, D])
    nw_d = [din("l0_nw", [128, KC]), din("l1_nw", [128, KC])]
    mb_d = [din("l0_mb", [128, 24]), din("l1_mb", [128, 24])]
    mbg_d = [din("l0_mbg", [1, D]), din("l1_mbg", [1, D])]
    gvec_d = din("gvec", [128, 4])
    convw_d = din("convw", [128, 24, 4])
    dtb_d = din("dtb", [1, 64])
    alog_d = din("alog", [1, 64])
    dsk_d = din("dsk", [1, 32])
    gnw_d = din("gnw", [128, 16])
    gnrow_d = din("gnrow", [1, 2048])
    ident_d = din("ident", [128, 128])
    pm_d = din("pm", [128, 128])
    bones_d = din("bones", [128, 128])
    utf_d = din("utf", [128, 128])
    utb_d = din("utb", [128, 128])
    negf_d = din("negf", [128, 128])
    negb_d = din("negb", [128, 128])
    i4_d = din("i4", [128, 512])
    cos_d = din("cosT", [128, TS])
    sin_d = din("sinT", [128, TS])

    y_o = dout("y", [NTOK, D])
    nk_o = dout("nk", [2 * TP, 256])
    nv_o = dout("nv", [2 * TP, 256])
    sf_o = dout("sf", [2, 2048, 128])
    sb_o = dout("sbw", [2, 2048, 128])

    h0T = dscr("h0T", [128, KC, NTOK], BF16, dbg=True)
    h1T = dscr("h1T", [128, KC, H1COLS], BF16, dbg=True)
    x1s = dscr("x1s", [NTOK, D], F32, dbg=True)
    wqg_s = dscr("wqg_s", [16, 128, KC, 128], BF16)
    w1_s = dscr("w1_s", [D, 5184], BF16)
    w1o_s = dscr("w1o_s", [2048, D], BF16)

    ident_bf = b.sb("ident_bf", [128, 128], BF16)
    ident_f = b.sb("ident_f", [128, 128], F32)
    ones_bf = b.sb("ones_bf", [128, 128], BF16)
    ones_f = b.sb("ones_f", [128, 128], F32)
    mw = [[b.sb("mw%d%d" % (l, c), [128, KC]) for c in range(2)] for l in range(2)]
    sh = [[b.sb("sh%d%d" % (l, c), [128, KC]) for c in range(2)] for l in range(2)]
    gate_bc = [[b.sb("gbc%d%d" % (l, c), [128, D]) for c in range(2)] for l in range(2)]

    b.dma(ident_f[:], ident_d, [], ["ident_f"])
    b.dma(ident_bf[:], ident_d, [], ["ident_bf"], q="pool")
    b.mset(ones_bf[:], 1.0, ["ones_bf"])
    b.mset(ones_f[:], 1.0, ["ones_f"])

    eps_t = b.sb("eps_t", [128, 1])
    b.mset(eps_t[:], EPS, ["eps_t"])
    prep_tmp = {"junk": b.sb("pjunk", [128, D], BF16), "ss": b.sb("pss", [128, 4]),
                "xn": [b.sb("pxn%d" % i, [128, D], BF16) for i in range(2)]}

    l0w_scope = ExitStack()
    _es0 = b.es
    b.es = l0w_scope
    wkv = b.sb("wkv", [128, KC, 512], BF16)
    wout = b.sb("wout", [128, 8, D], BF16)
    pm_bf = b.sb("pm_bf", [128, 128], BF16)
    bones_bf = b.sb("bones_bf", [128, 128], BF16)
    gv = b.sb("gv", [128, 4])
    VA = b.sb("VA", [128, (TS + CTX) // 128, 4, 128], BF16)
    ckt = b.sb("ckt", [128, 4, 256], BF16)
    b.es = _es0
    b.mset(VA[:, :, :, 64:128], 1.0, ["VAones"])
    for g4_ in range(4):
        b.dma(VA[:, 0:4, g4_, 0:64], cv[:, g4_ * 64:(g4_ + 1) * 64].rearrange("(kt p) d -> p kt d", p=128),
              [], [("VA", kt_) for kt_ in range(4)], q="pool")
    b.dma(ckt[:], ck.rearrange("(kt p) c -> p kt c", p=128), [], ["ckt"], q="pool")
    b.dma(wkv[:], w0in[:, 1024:1536].rearrange("(kc p) c -> p kc c", p=128), [], ["wkv"], q="pool")
    for p_ in range(8):
        a_, i_ = p_ // 4, p_ % 4
        hx_, hy_ = 8 * a_ + i_, 8 * a_ + 4 + i_
        b.dma(wout[0:64, p_, :], w0out[hx_ * 64:(hx_ + 1) * 64, :], [], [("wout", p_)], q="pool")
        b.dma(wout[64:128, p_, :], w0out[hy_ * 64:(hy_ + 1) * 64, :], [], [("wout", p_)], q="pool")
    b.dma(pm_bf[:], pm_d, [], ["pm_bf"], q="pool")
    b.dma(bones_bf[:], bones_d, [], ["bones_bf"], q="pool")
    b.dma(gv[:], gvec_d, [], ["gv"])


    with ExitStack() as ph:
        es_save = b.es
        b.es = ph
        cv_f = b.sb("cv_f", [128, KC, 2])
        scT = b.sb("scT", [128, KC, 2], BF16)
        screp = [b.sb("screp%d" % c, [128, KC, 128], BF16) for c in range(2)]
        wpart = b.sb("wpart", [128, KC, D], BF16)
        wpf = [b.sb("wpf%d" % i, [128, KC, D]) for i in range(2)]
        npart = [0]
        nw_t = b.sb("nw_t", [128, KC])
        mb_t = b.sb("mb_t", [128, 24])
        mbg_t = b.sb("mbg_t", [128, D])
        tmp8 = b.sb("tmp8", [128, KC])
        pm0 = b.ps("pm0", [128, 512])
        pg = [b.ps("pg%d" % i, [128, 512]) for i in range(2)]

        b.dma(cv_f[:], cvec, [], ["cv_f"])
        b.act(scT[:], cv_f[:], AF.Silu, ["cv_f"], ["scT"])
        for c in range(2):
            b.cp(screp[c][:], bc(scT[:, :, c:c + 1], [128, KC, 128]), ["scT"], ["screp%d" % c])
        for l in range(2):
            b.dma(nw_t[:], nw_d[l], [], ["nw_t"])
            b.dma(mb_t[:], mb_d[l], [], ["mb_t"])
            b.dma(mbg_t[:], mbg_d[l].partition_broadcast(128), [], ["mbg_t"])
            for part in range(3):
                wf_, wfk = wpf[npart[0] % 2], "wpf%d" % (npart[0] % 2)
                npart[0] += 1
                for hq, qn in ((0, "sp"), (1, "act")):
                    b.dma(wf_[:, hq * 4:(hq + 1) * 4, :],
                          modw[l][hq * 512:(hq + 1) * 512, part * D:(part + 1) * D].rearrange("(kc p) c -> p kc c", p=128),
                          [], [(wfk, hq)], q=qn)
                for kc in range(KC):
                    if kc % 3 == 2:
                        b.act(wpart[:, kc, :], wf_[:, kc, :], AF.Copy, [(wfk, kc // 4)], [("wpart", kc)])
                    else:
                        b.cp(wpart[:, kc, :], wf_[:, kc, :], [(wfk, kc // 4)], [("wpart", kc)],
                             q=("dve" if kc % 3 == 0 else "pool"))
                if part < 2:
                    pmv = pm0[:, 0:16].rearrange("p (f c) -> p f c", c=2)
                    for fc in range(KC):
                        for kc in range(KC):
                            b.mm(pmv[:, fc, :], wpart[:, kc, fc * 128:(fc + 1) * 128], scT[:, kc, :],
                                 kc == 0, kc == KC - 1, [("wpart", kc), "scT"], ["pm0"])
                    for c in range(2):
                        if part == 0:
                            b.tt(sh[l][c][:], pmv[:, :, c], mb_t[:, 0:8], ALU.add, ["pm0", "mb_t"], ["sh%d%d" % (l, c)])
                        else:
                            b.stt(tmp8[:], pmv[:, :, c], 1.0, mb_t[:, 8:16], ALU.add, ALU.add,
                                  ["pm0", "mb_t"], ["tmp8"])
                            b.tt(mw[l][c][:], tmp8[:], nw_t[:], ALU.mult, ["tmp8", "nw_t"], ["mw%d%d" % (l, c)])
                else:
                    for c in range(2):
                        for half in range(2):
                            for kc in range(KC):
                                b.mm(pg[half][:], screp[c][:, kc, :], wpart[:, kc, half * 512:(half + 1) * 512],
                                     kc == 0, kc == KC - 1, [("wpart", kc), "screp%d" % c], ["pg%d" % half])
                            b.tt(gate_bc[l][c][:, half * 512:(half + 1) * 512], pg[half][:],
                                 mbg_t[:, half * 512:(half + 1) * 512], ALU.add, ["pg%d" % half, "mbg_t"],
                                 ["gbc%d%d" % (l, c)])
        b.es = es_save
        ph0_keep = ph.pop_all()

    def prep_block(ph_tag, xtiles, nt, l, cond, hblk, hkey, pst, pskeys):
        for ti, (xap, xkey) in enumerate(xtiles):
            prep_tile(ti, xap, xkey, pst, pskeys)
        prep_evac(nt, l, cond, hblk, hkey, pst, pskeys)

    def prep_tile(ti, xap, xkey, pst, pskeys):
        xkeys = list(xkey) if isinstance(xkey, list) else [xkey]
        if True:
            junk, ss, xn = prep_tmp["junk"], prep_tmp["ss"], prep_tmp["xn"][ti % 2]
            xnk = "xn%d" % (ti % 2)
            b.act(junk[:], xap, AF.Square, xkeys, ["pjunk"], accum_out=ss[:, 0:1])
            b.act(ss[:, 1:2], ss[:, 0:1], AF.Ln, ["pjunk", "eps_t"], ["pss1"], scale=1.0 / D, bias=eps_t[:, 0:1])
            b.act(ss[:, 2:3], ss[:, 1:2], AF.Exp, ["pss1"], ["pss2"], scale=-0.5)
            b.ts(xn[:], xap, ss[:, 2:3], None, ALU.mult, None, xkeys + ["pss2"], [xnk])
            for kc in range(KC):
                pv = pst[kc // 2][:].bitcast(BF16)
                c0 = (kc % 2) * 512 + ti * 128
                b.tr(pv[:, c0:c0 + 128], xn[:, kc * 128:(kc + 1) * 128], ident_bf[:], [xnk, "ident_bf"],
                     [pskeys[kc // 2]])
    def prep_evac(nt, l, cond, hblk, hkey, pst, pskeys):
        for kc in range(KC):
            pv = pst[kc // 2][:].bitcast(BF16)
            c0 = (kc % 2) * 512
            b.act(hblk[:, kc, 0:nt], pv[:, c0:c0 + nt], AF.Identity,
                  [pskeys[kc // 2], "mw%d%d" % (l, cond), "sh%d%d" % (l, cond)], [hkey],
                  scale=mw[l][cond][:, kc:kc + 1], bias=sh[l][cond][:, kc:kc + 1])


    CUT = float(os.environ.get("K_CUT", "99"))

    def pair_heads(p):
        a, i = p // 4, p % 4
        return 8 * a + i, 8 * a + 4 + i

    with ExitStack() as ph:
        es_save = b.es
        b.es = ph
        for t in range(16):
            p = t % 8
            base = 0 if t < 8 else 1536
            hx, hy = pair_heads(p)
            for half, hh in enumerate((hx, hy)):
                b.dma(wqg_s[t, :, :, half * 64:(half + 1) * 64],
                      w0in[:, base + hh * 64:base + (hh + 1) * 64].rearrange("(kc p) c -> p kc c", p=128),
                      [], [("wqg_s", t)], q="pool")
        tk.barrier(skip_dma_queues=("pool",))
        b.es = es_save
    ph0_keep.close()

    if stage >= 1:
        with ExitStack() as ph:
            es_save = b.es
            b.es = ph
            layer0(b, locals())
            tk.barrier()
            b.es = es_save
    l0w_scope.close()

    if stage >= 2:
        with ExitStack() as ph:
            es_save = b.es
            b.es = ph
            layer1(b, locals())
            b.es = es_save

    tk.emit(nc, es, b.outs)
    es.close()
    return nc


def layer0(b, g):
    tk = b.tk
    nc = b.nc
    xs, ck, cv, w0in, w0out, gvec_d = g["xs"], g["ck"], g["cv"], g["w0in"], g["w0out"], g["gvec_d"]
    pm_d, bones_d, cos_d, sin_d = g["pm_d"], g["bones_d"], g["cos_d"], g["sin_d"]
    h0T, h1T, x1s, wqg_s = g["h0T"], g["h1T"], g["x1s"], g["wqg_s"]
    nk_o, nv_o = g["nk_o"], g["nv_o"]
    ident_bf, ident_f, gate_bc, eps_t = g["ident_bf"], g["ident_f"], g["gate_bc"], g["eps_t"]
    prep_block, pair_heads = g["prep_block"], g["pair_heads"]
    prep_tile, prep_evac = g["prep_tile"], g["prep_evac"]
    CUT = g["CUT"]

    wkv, wout, pm_bf, bones_bf, gv = g["wkv"], g["wout"], g["pm_bf"], g["bones_bf"], g["gv"]

    NKT = (TS + CTX) // 128
    KT = b.sb("KT", [128, 2, TS + CTX], BF16)
    VA, ckt = g["VA"], g["ckt"]

    hblk = b.sb("hblk", [128, KC, 512], BF16)
    qr = b.sb("qr", [128, 8, 512], BF16)
    sg = b.sb("sg", [128, 8, 512], BF16)
    Pt = [b.sb("Pt%d" % i, [128, 2, 512], BF16) for i in range(3)]
    wt = [b.sb("wt%d" % i, [128, KC, 128], BF16) for i in range(2)]
    cosr = b.sb("cosr", [128, 512])
    sinr = b.sb("sinr", [128, 512])
    cosg = [b.sb("cosg%d" % i, [128, 512]) for i in range(2)]
    sing = [b.sb("sing%d" % i, [128, 512]) for i in range(2)]
    qb2 = [b.sb("qb%d" % i, [128, 512], BF16) for i in range(2)]
    sq2 = [b.sb("sq%d" % i, [128, 512], BF16) for i in range(2)]
    t12 = [b.sb("t1_%d" % i, [128, 512]) for i in range(2)]
    t22 = [b.sb("t2_%d" % i, [128, 512]) for i in range(2)]
    lnr2 = [b.sb("lnr%d" % i, [128, 512]) for i in range(2)]
    rstd2 = [b.sb("rstd%d" % i, [128, 512]) for i in range(2)]
    kf = b.sb("kf", [128, 256])
    kfT = b.sb("kfT", [128, 256])
    vf = b.sb("vf", [128, 256])
    ftmp = b.sb("ftmp", [128, 512])
    frec = b.sb("frec", [128, 512])
    xt = [b.sb("l0x%d" % i, [128, D]) for i in range(2)]
    x1 = [b.sb("l0x1%d" % i, [128, D]) for i in range(2)]
    h1b = b.sb("h1b", [128, KC, 512], BF16)
    zcol = b.sb("zcol", [128, KC, 2], BF16)
    b.mset(zcol[:], 0.0, ["zcol"])

    SA = b.ps("SA", [128, 1024], keys=[("SA", 0), ("SA", 1)])
    SB = b.ps("SB", [128, 1024], keys=[("SB", 0), ("SB", 1)])
    OA = b.ps("OA", [128, 512])
    OB = b.ps("OB", [128, 512])
    R0 = b.ps("R0", [128, 512])
    R1 = b.ps("R1", [128, 512])
    Sb = [SA, SB]
    Sk = [[("SA", 0), ("SA", 1)], [("SB", 0), ("SB", 1)]]
    Ob = [[OA, OB], [R0, R1]]
    Okey = [["OA", "OB"], ["R0", "R1"]]

    ncall = [0]

    def qk_stages(src, skey, nt, ci, outs):
        ncall[0] += 1
        par = ncall[0] % 2
        qb, sq, t1, t2, lnr, rstd = qb2[par], sq2[par], t12[par], t22[par], lnr2[par], rstd2[par]
        kq, ks, k1, k2, kl, kr = "qb%d" % par, "sq%d" % par, "t1_%d" % par, "t2_%d" % par, "lnr%d" % par, "rstd%d" % par
        (Ra, rak), (Rb, rbk) = ((R0, "R0"), (R1, "R1")) if par == 0 else ((OA, "OA"), (OB, "OB"))

        def A1():
            b.tt(t1[:, 0:nt], src, cosg[ci][:, 0:nt], ALU.mult, [skey, "cosg%d" % ci], [k1])
            b.act(qb[:, 0:nt], src, AF.Copy, [skey], [kq])
            b.tt(sq[:, 0:nt], qb[:, 0:nt], qb[:, 0:nt], ALU.mult, [kq], [ks])
            b.mm(Ra[:, 0:nt], bones_bf[:], sq[:, 0:nt], True, True, ["bones_bf", ks], [rak])
            b.mm(Rb[:, 0:nt], pm_bf[:], qb[:, 0:nt], True, True, ["pm_bf", kq], [rbk])

        def A2():
            b.act(lnr[:, 0:nt], Ra[:, 0:nt], AF.Ln, [rak, "eps_t"], [kl], scale=1.0 / 64, bias=eps_t[:, 0:1])
            b.act(rstd[:, 0:nt], lnr[:, 0:nt], AF.Exp, [kl], [kr], scale=-0.5)
            b.tt(t2[:, 0:nt], Rb[:, 0:nt], sing[ci][:, 0:nt], ALU.mult, [rbk, "sing%d" % ci], [k2])
            b.tt(t1[:, 0:nt], t1[:, 0:nt], t2[:, 0:nt], ALU.add, [k1, k2], [k1])

        def Bst():
            for (oap, okey) in outs:
                b.tt(oap, t1[:, 0:nt], rstd[:, 0:nt], ALU.mult, [k1, kr], [okey])
        return A1, A2, Bst

    def qk_pipe(src, skey, nt, ci, outs):
        for st in qk_stages(src, skey, nt, ci, outs):
            st()

    pst = [R0, R1, OA, OB]
    pk = ["R0", "R1", "OA", "OB"]
    if CUT <= 1:
        return
    wti = 0
    prompt_i = 0
    for si, (tok0, T, is_s) in enumerate(SEQS):
        nt = 512 if is_s else 256
        cond = 0 if is_s else 1
        ctx = CTX if is_s else 0
        nkt = (T + ctx) // 128
        nblk = T // nt
        if is_s:
            for kt in range(4):
                for a in range(2):
                    pv = R0[:].bitcast(BF16)
                    b.tr(pv[:, 0:128], ckt[:, kt, a * 128:(a + 1) * 128], ident_bf[:], ["ckt", "ident_bf"], ["R0"])
                    b.cp(KT[:, a, kt * 128:(kt + 1) * 128], pv[:, 0:128], ["R0"], [("KT", a)])
        if si == 0:
            w1in_, w1out_, w1_s_, w1o_s_ = g["w1in"], g["w1out"], g["w1_s"], g["w1o_s"]
            for r4 in range(4):
                b.dma(w1_s_[r4 * 256:(r4 + 1) * 256, :], w1in_[r4 * 256:(r4 + 1) * 256, :], [], [("w1_s", r4)], q="pool")
        if CUT <= 2:
            return
        if not is_s:
            b.mset(cosr[:], 1.0, ["cosr"])
            b.mset(sinr[:], 0.0, ["sinr"])
            for ci in range(2):
                b.ts(cosg[ci][:], cosr[:], gv[:, 2 * ci:2 * ci + 1], None, ALU.mult, None, ["cosr", "gv"], ["cosg%d" % ci])
                b.ts(sing[ci][:], sinr[:], gv[:, 2 * ci + 1:2 * ci + 2], None, ALU.mult, None, ["sinr", "gv"], ["sing%d" % ci])

        def load_tables(t0):
            if not is_s:
                return
            b.dma(cosr[:], cos_d[:, t0:t0 + 512], [], ["cosr"])
            b.dma(sinr[:], sin_d[:, t0:t0 + 512], [], ["sinr"])
            for ci in range(2):
                b.ts(cosg[ci][:], cosr[:], gv[:, 2 * ci:2 * ci + 1], None, ALU.mult, None, ["cosr", "gv"], ["cosg%d" % ci])
                b.ts(sing[ci][:], sinr[:], gv[:, 2 * ci + 1:2 * ci + 2], None, ALU.mult, None, ["sinr", "gv"], ["sing%d" % ci])

        for bi in range(nblk):
            t0 = bi * nt
            xbufs = [(xt[0], "l0x0"), (xt[1], "l0x1"), (x1[0], "l0x10"), (x1[1], "l0x11")]
            for ti in range(nt // 128):
                xb_, xk_ = xbufs[ti]
                g0_ = tok0 + t0 + ti * 128
                wk_ = [xk_] if ti < 2 else [(xk_, 0), (xk_, 1)]
                b.dma(xb_[:], xs[g0_:g0_ + 128, :], [], wk_)
                prep_tile(ti, xb_[:], wk_, pst, pk)
            prep_evac(nt, 0, cond, hblk, "hblk", pst, pk)
            b.dma(h0T[:, :, tok0 + t0:tok0 + t0 + nt], hblk[:, :, 0:nt], ["hblk"], [("h0T", tok0 + t0)])
            load_tables(t0)
            for a in range(2):
                src = SA[:, a * 512:a * 512 + nt]
                for kc in range(KC):
                    b.mm(src, wkv[:, kc, a * 128:(a + 1) * 128], hblk[:, kc, 0:nt], kc == 0, kc == KC - 1,
                         ["wkv", "hblk"], [("SA", a)])
                if CUT <= 2.2:
                    return
                outs = [(KT[:, a, ctx + t0:ctx + t0 + nt], ("KT", a))]
                if not is_s:
                    outs.append((kf[:, 0:nt], "kf"))
                qk_pipe(src, ("SA", a), nt, 1, outs)
                if CUT <= 2.5:
                    return
                if not is_s:
                    for ti in range(nt // 128):
                        b.tr(SB[:, ti * 128:(ti + 1) * 128], kf[:, ti * 128:(ti + 1) * 128], ident_f[:], ["kf", "ident_f"],
                             [("SB", 0)])
                    for ti in range(nt // 128):
                        b.cp(kfT[:, ti * 128:(ti + 1) * 128], SB[:, ti * 128:(ti + 1) * 128], [("SB", 0)], ["kfT"])
                        r0 = prompt_i * TP + t0 + ti * 128
                        b.dma(nk_o[r0:r0 + 128, a * 128:(a + 1) * 128], kfT[:, ti * 128:(ti + 1) * 128], ["kfT"], [],
                              is_out=True)
            if CUT <= 2.6:
                return
            for ti in range(nt // 128):
                kt = (ctx + t0) // 128 + ti
                vp = SB[:, 512:768]
                for kc in range(KC):
                    b.mm(vp, hblk[:, kc, ti * 128:(ti + 1) * 128], wkv[:, kc, 256:512], kc == 0, kc == KC - 1,
                         ["wkv", "hblk"], [("SB", 1)])
                if CUT <= 2.7:
                    return
                b.cp(VA[:, kt, :, 0:64], vp.rearrange("p (g d) -> p g d", g=4), [("SB", 1)], [("VA", kt)])
                if not is_s:
                    b.act(vf[:], vp, AF.Copy, [("SB", 1)], ["vf"])
                    r0 = prompt_i * TP + t0 + ti * 128
                    b.dma(nv_o[r0:r0 + 128, :], vf[:], ["vf"], [], is_out=True)

        if CUT <= 3:
            return
        def load_block_inputs(bi_):
            t0_ = bi_ * nt
            b.dma(hblk[:, :, 0:nt], h0T[:, :, tok0 + t0_:tok0 + t0_ + nt], [("h0T", tok0 + t0_)], ["hblk"])
            load_tables(t0_)

        load_block_inputs(0)
        gt_total = nblk * 16
        issued = set()

        def issue_w(gt):
            if gt >= gt_total or gt in issued:
                return
            issued.add(gt)
            b.dma(wt[gt % 2][:], wqg_s[gt % 16], [("wqg_s", gt % 16)], ["wt%d" % (gt % 2)])

        issue_w(0)
        for bi in range(nblk):
            t0 = bi * nt

            def proj(t):
                gt = bi * 16 + t
                w, wk = wt[gt % 2], "wt%d" % (gt % 2)
                issue_w(gt + 1)
                src = SA[:, (t % 2) * 512:(t % 2) * 512 + nt]
                skey = ("SA", t % 2)
                for kc in range(KC):
                    b.mm(src, w[:, kc, :], hblk[:, kc, 0:nt], kc == 0, kc == KC - 1, [wk, "hblk"], [skey])

            proj(0)
            prev = None
            for t in range(16):
                p = t % 8
                if t + 1 < 16:
                    proj(t + 1)
                src = SA[:, (t % 2) * 512:(t % 2) * 512 + nt]
                skey = ("SA", t % 2)
                if t < 8:
                    st3 = qk_stages(src, skey, nt, 0, [(qr[:, p, 0:nt], ("qr", p))])
                    st3[0]()
                    if prev is not None:
                        prev[1]()
                        prev[2]()
                    prev = st3
                else:
                    if prev is not None:
                        prev[1]()
                        prev[2]()
                        prev = None
                    b.act(sg[:, p, 0:nt], src, AF.Silu, [skey], [("sg", p)])
            if bi + 1 < nblk:
                load_block_inputs(bi + 1)
            if CUT <= 4:
                return
            for p in range(8):
                a = p // 4
                ob = Ob[p % 2]
                okey = Okey[p % 2]

                def qk(kt):
                    S = Sb[kt % 2]
                    for hh in range(2):
                        b.mm(S[:, hh * 512:hh * 512 + nt], KT[hh * 64:(hh + 1) * 64, a, kt * 128:(kt + 1) * 128],
                             qr[hh * 64:(hh + 1) * 64, p, 0:nt], True, True, [("KT", a), ("qr", p)],
                             [Sk[kt % 2][hh]])

                qk(0)
                if nkt > 1:
                    qk(1)
                for kt in range(nkt):
                    S = Sb[kt % 2]
                    P = Pt[kt % 3]
                    b.act(P[:, :, 0:nt], S[:].rearrange("p (h t) -> p h t", h=2)[:, :, 0:nt], AF.Exp,
                          Sk[kt % 2], [("P", kt % 3)], scale=0.125)
                    if kt + 2 < nkt:
                        qk(kt + 2)
                    for hh in range(2):
                        b.mm(ob[hh][:, 0:nt], VA[:, kt, 2 * a + hh, :], P[:, hh, 0:nt], kt == 0, kt == nkt - 1,
                             [("VA", kt), "VAones", ("P", kt % 3)], [okey[hh]])
                for hh in range(2):
                    lo, hi = hh * 64, (hh + 1) * 64
                    b.tt(ftmp[lo:hi, 0:nt], ob[hh][0:64, 0:nt], sg[lo:hi, p, 0:nt], ALU.mult,
                         [okey[hh], ("sg", p)], [("ftmp", hh)])
                    b.rcp(frec[lo:hi, 0:nt], ob[hh][64:128, 0:nt], [okey[hh]], [("frec", hh)])
                    b.tt(qr[lo:hi, p, 0:nt], ftmp[lo:hi, 0:nt], frec[lo:hi, 0:nt], ALU.mult,
                         [("ftmp", hh), ("frec", hh)], [("qr", p)])
            if CUT <= 5:
                return
            issue_w((bi + 1) * 16)
            issue_w((bi + 1) * 16 + 1)
            def outproj(ti_):
                S_, sn = (SA, "SA") if ti_ % 2 == 0 else (SB, "SB")
                g0_ = tok0 + t0 + ti_ * 128
                b.dma(xt[ti_ % 2][:], xs[g0_:g0_ + 128, :], [], ["l0x%d" % (ti_ % 2)])
                for half in range(2):
                    for p in range(8):
                        b.mm(S_[:, half * 512:(half + 1) * 512], qr[:, p, ti_ * 128:(ti_ + 1) * 128],
                             wout[:, p, half * 512:(half + 1) * 512], p == 0, p == 7,
                             [("qr", p), ("wout", p)], [(sn, half)])

            outproj(0)
            for ti in range(nt // 128):
                g0 = tok0 + t0 + ti * 128
                xti = xt[ti % 2]
                S_, sn = (SA, "SA") if ti % 2 == 0 else (SB, "SB")
                if ti + 1 < nt // 128:
                    outproj(ti + 1)
                for half in range(2):
                    x1h = x1[ti % 2][:, half * 512:(half + 1) * 512]
                    b.tt(x1h, S_[:, half * 512:(half + 1) * 512],
                         gate_bc[0][cond][:, half * 512:(half + 1) * 512], ALU.mult,
                         [(sn, half), "gbc0%d" % cond], [("l0x1%d" % (ti % 2), half)])
                    b.tt(x1h, x1h, xti[:, half * 512:(half + 1) * 512], ALU.add,
                         [("l0x1%d" % (ti % 2), half), "l0x%d" % (ti % 2)], [("l0x1%d" % (ti % 2), half)])
                x1k_ = [("l0x1%d" % (ti % 2), 0), ("l0x1%d" % (ti % 2), 1)]
                b.dma(x1s[g0:g0 + 128, :], x1[ti % 2][:], x1k_, [("x1s", g0)])
                prep_tile(ti, x1[ti % 2][:], x1k_, pst, pk)
            prep_evac(nt, 1, cond, h1b, "h1b", pst, pk)
            c0 = H1OFF[si] + 1 + t0
            b.dma(h1T[:, :, c0:c0 + nt], h1b[:, :, 0:nt], ["h1b"], [("h1T", si)])
        if CUT <= 7:
            return
        b.dma(h1T[:, :, H1OFF[si]:H1OFF[si] + 1], zcol[:, :, 0:1], ["zcol"], [("h1T", si)],
              allow_slow_non_contiguous=True)
        b.dma(h1T[:, :, H1OFF[si] + T + 1:H1OFF[si] + T + 2], zcol[:, :, 1:2], ["zcol"], [("h1T", si)],
              allow_slow_non_contiguous=True)
        if not is_s:
            prompt_i += 1


def layer1(b, g):
    tk = b.tk
    nc = b.nc
    w1in, w1out = g["w1in"], g["w1out"]
    h1T, x1s = g["h1T"], g["x1s"]
    convw_d, dtb_d, alog_d, dsk_d, gnw_d = g["convw_d"], g["dtb_d"], g["alog_d"], g["dsk_d"], g["gnw_d"]
    utf_d, utb_d, negf_d, negb_d, i4_d = g["utf_d"], g["utb_d"], g["negf_d"], g["negb_d"], g["i4_d"]
    stf, stb, y_o, sf_o, sb_o = g["stf"], g["stb"], g["y_o"], g["sf_o"], g["sb_o"]
    ident_bf, ident_f, ones_bf, gate_bc, eps_t = g["ident_bf"], g["ident_f"], g["ones_bf"], g["gate_bc"], g["eps_t"]
    dscr = g["dscr"]
    LCUT = float(os.environ.get("K_LCUT", "99"))

    xtok_s = dscr("xtok_s", [NCH, 128, 2048], BF16, dbg=True)
    btok_s = dscr("btok_s", [NCH, 128, 512], BF16, dbg=True)
    bT_s = dscr("bT_s", [NCH, 128, 4, 128], BF16, dbg=True)
    cT_s = dscr("cT_s", [NCH, 128, 4, 128], BF16, dbg=True)
    sz_s = dscr("sz_s", [NCH, 128, 2048], BF16, dbg=True)
    dts_s = dscr("dts_s", [NCH, 128, 192], F32, dbg=True)
    yp_s = dscr("yp_s", [NCH, 128, 2048], F32, dbg=True)
    xdf_s = dscr("xdf_s", [NCH, 128, 2048], BF16)
    eac_s = dscr("eac_s", [NCH, 128, 64], F32)

    with ExitStack() as ph:
        es_save = b.es
        b.es = ph
        w1 = b.sb("w1", [128, KC, 5184], BF16)
        w1_s = g["w1_s"]
        for kc in range(KC):
            b.dma(w1[:, kc, :], w1_s[kc * 128:(kc + 1) * 128, :], [("w1_s", kc // 2)], [("w1", kc)])
        w1k = [("w1", kc) for kc in range(KC)]
        cw = b.sb("cw", [128, 24, 4])
        dtb_bc = b.sb("dtb_bc", [128, 64])
        A_bc = b.sb("A_bc", [128, 64])
        b.dma(cw[:], convw_d, [], ["cw"])
        b.dma(dtb_bc[:], dtb_d.partition_broadcast(128), [], ["dtb_bc"])
        b.dma(A_bc[:], alog_d.partition_broadcast(128), [], ["A_bc"])
        b.act(A_bc[:], A_bc[:], AF.Exp, ["A_bc"], ["A_bc"])
        b.ts(A_bc[:], A_bc[:], -1.0, None, ALU.mult, None, ["A_bc"], ["A_bc"])
        hwin = [b.sb("hwin%d" % i, [128, KC, 258], BF16) for i in range(2)]
        xbcT = b.sb("xbcT", [128, 24, 256], BF16)
        acc = [b.sb("cacc%d" % i, [128, 256]) for i in range(3)]
        rawb = [b.sb("rawb%d" % i, [128, 258]) for i in range(3)]
        xtok = [b.sb("a_xtok%d" % i, [128, 2048], BF16) for i in range(2)]
        btok = [b.sb("a_btok%d" % i, [128, 512], BF16) for i in range(2)]
        szt = [b.sb("a_sz%d" % i, [128, 2048], BF16) for i in range(2)]
        dtt = [b.sb("a_dt%d" % i, [128, 192]) for i in range(2)]
        dtmp = b.sb("a_dtmp", [128, 64])
        RA = [b.ps("a_R%d" % i, [128, 512]) for i in range(2)]
        TA = b.ps("a_T", [128, 1024], keys=[("a_T", 0), ("a_T", 1)])
        ZA = [b.ps("a_Z%d" % i, [128, 512]) for i in range(2)]
        DA = b.ps("a_D", [128, 512])
        TAv = TA[:].bitcast(BF16)
        DAv = DA[:].bitcast(BF16)
        ci = 0
        wins = [(si, tok0, w0) for si, (tok0, T, is_s) in enumerate(SEQS) for w0 in range(0, T, 256)]

        def load_win(i):
            si_, _, w0_ = wins[i]
            c0_ = H1OFF[si_] + w0_
            b.dma(hwin[i % 2][:], h1T[:, :, c0_:c0_ + 258], [("h1T", si_)], ["hwin%d" % (i % 2)])

        load_win(0)
        for wi, (si, tok0, w0) in enumerate(wins):
            if True:
                hw_ = hwin[wi % 2]
                hk = "hwin%d" % (wi % 2)
                if wi + 1 < len(wins):
                    load_win(wi + 1)
                for cb in range(24):
                    R_ = RA[cb % 2]
                    rk = "a_R%d" % (cb % 2)
                    ac_ = acc[cb % 3]
                    ak = "cacc%d" % (cb % 3)
                    for kc in range(KC):
                        b.mm(R_[:, 0:258], w1[:, kc, 2048 + cb * 128:2048 + (cb + 1) * 128], hw_[:, kc, :],
                             kc == 0, kc == KC - 1, [w1k[kc], hk], [rk])
                    rw_ = rawb[cb % 3]
                    rwk = "rawb%d" % (cb % 3)
                    eng = "dve"
                    b.act(rw_[:], R_[:, 0:258], AF.Copy, [rk], [rwk])
                    b.ts(ac_[:], rw_[:, 1:257], cw[:, cb, 1:2], cw[:, cb, 3:4], ALU.mult, ALU.add, [rwk, "cw"], [ak], q=eng)
                    b.stt(ac_[:], rw_[:, 0:256], cw[:, cb, 0:1], ac_[:], ALU.mult, ALU.add, [rwk, ak, "cw"], [ak], q=eng)
                    b.stt(ac_[:], rw_[:, 2:258], cw[:, cb, 2:3], ac_[:], ALU.mult, ALU.add, [rwk, ak, "cw"], [ak], q=eng)
                    b.act(xbcT[:, cb, :], ac_[:], AF.Silu, [ak], [("xbcT", cb)])
                for ch in range(2):
                    gc = (tok0 + w0) // 128 + ch
                    cs = slice(ch * 128, (ch + 1) * 128)
                    xt_, xk = xtok[ci % 2], "a_xtok%d" % (ci % 2)
                    bt_, bk = btok[ci % 2], "a_btok%d" % (ci % 2)
                    sz_, sk = szt[ci % 2], "a_sz%d" % (ci % 2)
                    dt_, dk = dtt[ci % 2], "a_dt%d" % (ci % 2)
                    ci += 1
                    for cb in range(16):
                        b.tr(TAv[:, cb * 128:(cb + 1) * 128], xbcT[:, cb, cs], ident_bf[:], [("xbcT", cb), "ident_bf"],
                             [("a_T", cb // 8)])
                    b.act(xt_[:, 0:1024], TAv[:, 0:1024], AF.Copy, [("a_T", 0)], [(xk, 0)])
                    b.cp(xt_[:, 1024:2048], TAv[:, 1024:2048], [("a_T", 1)], [(xk, 1)])
                    b.dma(xtok_s[gc], xt_[:], [(xk, 0), (xk, 1)], [("xtok_s", gc)])
                    for g4 in range(4):
                        b.tr(DAv[:, g4 * 128:(g4 + 1) * 128], xbcT[:, 16 + g4, cs], ident_bf[:],
                             [("xbcT", 16 + g4), "ident_bf"], ["a_D"])
                    b.cp(bt_[:], DAv[:, 0:512], ["a_D"], [bk])
                    b.dma(btok_s[gc], bt_[:], [bk], [("btok_s", gc)])
                    b.dma(bT_s[gc], xbcT[:, 16:20, cs], [("xbcT", 16 + i) for i in range(4)], [("bT_s", gc)])
                    b.dma(cT_s[gc], xbcT[:, 20:24, cs], [("xbcT", 20 + i) for i in range(4)], [("cT_s", gc)])
                    for zb in range(4):
                        Z_ = ZA[zb % 2]
                        zk = "a_Z%d" % (zb % 2)
                        for kc in range(KC):
                            b.mm(Z_[:], hw_[:, kc, 1 + ch * 128:1 + (ch + 1) * 128], w1[:, kc, zb * 512:(zb + 1) * 512],
                                 kc == 0, kc == KC - 1, [w1k[kc], hk], [zk])
                        b.act(sz_[:, zb * 512:(zb + 1) * 512], Z_[:], AF.Silu, [zk], [(sk, zb)])
                    b.dma(sz_s[gc], sz_[:], [(sk, i) for i in range(4)], [("sz_s", gc)])
                    for kc in range(KC):
                        b.mm(DA[:, 256:320], hw_[:, kc, 1 + ch * 128:1 + (ch + 1) * 128], w1[:, kc, 5120:5184],
                             kc == 0, kc == KC - 1, [w1k[kc], hk], ["a_D"])
                    b.tt(dtmp[:], DA[:, 256:320], dtb_bc[:], ALU.add, ["a_D", "dtb_bc"], ["a_dtmp"])
                    b.act(dtmp[:], dtmp[:], AF.Exp, ["a_dtmp"], ["a_dtmp"])
                    b.act(dt_[:, 0:64], dtmp[:], AF.Ln, ["a_dtmp"], [dk], bias=1.0)
                    b.act(dt_[:, 64:128], dt_[:, 0:64], AF.Ln, [dk], [dk])
                    b.tt(dt_[:, 128:192], dt_[:, 0:64], A_bc[:], ALU.mult, [dk, "A_bc"], [dk])
                    b.dma(dts_s[gc], dt_[:], [dk], [("dts_s", gc)])
        tk.barrier()
        b.es = es_save
    if LCUT <= 1:
        return

    def make_state_fns(hst, hst_bf, htmp, stin, stout, ST, ST_alt=None, htmp_alt=None):
        def load_state(src):
            for blk in range(16):
                b.dma(stin[:], src[blk * 128:(blk + 1) * 128, :], [], ["stin"])
                b.tr(ST[:, 0:128], stin[:], ident_f[:], ["stin", "ident_f"], ["b_ST"])
                b.cp(hst[:, blk * 128:(blk + 1) * 128], ST[:, 0:128], ["b_ST"], [("hst", blk // 4)])
            for g4 in range(4):
                b.act(hst_bf[:, g4 * 512:(g4 + 1) * 512], hst[:, g4 * 512:(g4 + 1) * 512], AF.Copy, [("hst", g4)],
                      [("hst_bf", g4)])

        def zero_state():
            for g4 in range(4):
                b.mset(hst[:, g4 * 512:(g4 + 1) * 512], 0.0, [("hst", g4)])
                b.mset(hst_bf[:, g4 * 512:(g4 + 1) * 512], 0.0, [("hst_bf", g4)], q="pool")

        def store_state(dst):
            for blk in range(16):
                b.tr(ST[:, 0:128], hst[:, blk * 128:(blk + 1) * 128], ident_f[:], [("hst", blk // 4), "ident_f"], ["b_ST"])
                b.cp(stout[:], ST[:, 0:128], ["b_ST"], ["stout"])
                b.dma(dst[blk * 128:(blk + 1) * 128, :], stout[:], ["stout"], [], is_out=True)

        def state_update(bt_, bk, xd_, xdk, cdt, cdk, off):
            sts = [(ST, "b_ST")] + ([(ST_alt, "b_ST2")] if ST_alt is not None else [])
            tmps = [(htmp, "htmp")] + ([(htmp_alt, "htmp2")] if htmp_alt is not None else [])
            nb_ = len(sts)
            for g0 in range(0, 4, nb_):
                for g4 in range(g0, g0 + nb_):
                    S_, sk_ = sts[g4 % nb_]
                    T_, tk_ = tmps[g4 % len(tmps)]
                    b.mm(S_[:], bt_[:, g4 * 128:(g4 + 1) * 128], xd_[:, g4 * 512:(g4 + 1) * 512], True, True,
                         [bk, xdk], [sk_])
                    hv = hst[:, g4 * 512:(g4 + 1) * 512].rearrange("p (h q) -> p h q", h=8)
                    b.tt(T_[:].rearrange("p (h q) -> p h q", h=8), hv,
                         bc(cdt[:, off + g4 * 8:off + (g4 + 1) * 8].unsqueeze(2), [128, 8, 64]), ALU.mult,
                         [("hst", g4), cdk], [tk_], q="pool")
                for g4 in range(g0, g0 + nb_):
                    S_, sk_ = sts[g4 % nb_]
                    T_, tk_ = tmps[g4 % len(tmps)]
                    b.tt(hst[:, g4 * 512:(g4 + 1) * 512], T_[:], S_[:], ALU.add, [tk_, sk_], [("hst", g4)])
                    b.act(hst_bf[:, g4 * 512:(g4 + 1) * 512], hst[:, g4 * 512:(g4 + 1) * 512], AF.Copy, [("hst", g4)],
                          [("hst_bf", g4)])
        return load_state, zero_state, store_state, state_update

    with ExitStack() as ph:
        es_save = b.es
        b.es = ph
        utri = [b.sb("utri%d" % d, [128, 128]) for d in range(2)]
        negm = [b.sb("negm%d" % d, [128, 128], BF16) for d in range(2)]
        i4 = b.sb("i4_sb", [128, 512], BF16)
        D_bc = b.sb("D_bc", [128, 32])
        DI = b.sb("DI", [128, 32, 128], BF16)
        ones_f = g["ones_f"]
        b.dma(utri[0][:], utf_d, [], ["utri0"])
        b.dma(utri[1][:], utb_d, [], ["utri1"])
        b.dma(negm[0][:], negf_d, [], ["negm0"], q="pool")
        b.dma(negm[1][:], negb_d, [], ["negm1"], q="pool")
        b.dma(i4[:], i4_d, [], ["i4"], q="pool")
        b.dma(D_bc[:], dsk_d.partition_broadcast(128), [], ["D_bc"])
        b.tt(DI[:], bc(ident_bf[:].unsqueeze(1), [128, 32, 128]), bc(D_bc[:].unsqueeze(2), [128, 32, 128]), ALU.mult,
             ["ident_bf", "D_bc"], ["DI"])

        NL = 3
        xt2 = [b.sb("b_xtok%d" % i, [128, 2048], BF16) for i in range(NL)]
        bt2 = [b.sb("b_btok%d" % i, [128, 512], BF16) for i in range(NL)]
        bT2 = [b.sb("b_bT%d" % i, [128, 4, 128], BF16) for i in range(NL)]
        cT2 = [b.sb("b_cT%d" % i, [128, 4, 128], BF16) for i in range(NL)]
        dt2 = [b.sb("b_dt%d" % i, [128, 192]) for i in range(NL)]
        acs2 = [b.sb("acs%d" % i, [128, 64]) for i in range(2)]
        ea2 = [b.sb("ea%d" % i, [128, 64]) for i in range(2)]
        de2 = [b.sb("de%d" % i, [128, 64]) for i in range(2)]
        cd2 = [b.sb("cd%d" % i, [128, 64]) for i in range(2)]
        wl2 = [b.sb("wl%d" % i, [128, 64]) for i in range(2)]
        nb2 = [b.sb("nb%d" % i, [128, 64]) for i in range(2)]
        Dm = [b.sb("Dm%d" % i, [128, 512]) for i in range(4)]
        Eb = [b.sb("E%d" % d, [128, 4096], BF16) for d in range(2)]
        MT2 = [[b.sb("MT%d_%d" % (i, d), [128, 4096], BF16) for d in range(2)] for i in range(2)]
        cbT = b.sb("cbT", [128, 512], BF16)
        xdec2 = [[b.sb("xdec%d_%d" % (i, d), [128, 2048], BF16) for d in range(2)] for i in range(2)]
        hst = b.sb("hst", [128, 2048])
        hst_bf = b.sb("hst_bf", [128, 2048], BF16)
        htmp = b.sb("htmp", [128, 512])
        ypt = [b.sb("ypt%d" % i, [128, 2048]) for i in range(2)]
        yot = [b.sb("yot%d" % i, [128, 512]) for i in range(2)]
        stin = b.sb("stin", [128, 128])
        stout = b.sb("stout", [128, 128])
        eact = [b.sb("eact%d" % i, [128, 64]) for i in range(2)]

        AC = b.ps("b_AC", [128, 512])
        CB = b.ps("b_CB", [128, 512])
        DB = [b.ps("b_DB%d" % i, [128, 512]) for i in range(2)]
        YG = [b.ps("b_YG%d" % i, [128, 512]) for i in range(2)]
        YO = b.ps("b_YO", [128, 512])
        ST = b.ps("b_ST", [128, 512])
        load_state, zero_state, store_state, state_update = make_state_fns(hst, hst_bf, htmp, stin, stout, ST)
        v3 = lambda t: t[:].rearrange("p (h l) -> p h l", h=32)

        items = []
        for si, (tok0, T, is_s) in enumerate(SEQS):
            nch = T // 128
            for c in range(nch - 1, -1, -1):
                items.append((tok0 // 128 + c, c == nch - 1, c == 0, si))

        def load_b(i):
            gcx, sx = items[i][0], i % NL
            b.dma(xt2[sx][:], xtok_s[gcx], [("xtok_s", gcx)], ["b_xtok%d" % sx])
            b.dma(bt2[sx][:], btok_s[gcx], [("btok_s", gcx)], ["b_btok%d" % sx])
            b.dma(bT2[sx][:], bT_s[gcx], [("bT_s", gcx)], ["b_bT%d" % sx])
            b.dma(cT2[sx][:], cT_s[gcx], [("cT_s", gcx)], ["b_cT%d" % sx])
            b.dma(dt2[sx][:], dts_s[gcx], [("dts_s", gcx)], ["b_dt%d" % sx])

        def early_stages(i):
            sl, s2 = i % NL, i % 2
            xt_, xk = xt2[sl], "b_xtok%d" % sl
            bT_, bTk = bT2[sl], "b_bT%d" % sl
            cT_, cTk = cT2[sl], "b_cT%d" % sl
            dt_, dk = dt2[sl], "b_dt%d" % sl
            acs, ea, de, cd, wl, nb = acs2[s2], ea2[s2], de2[s2], cd2[s2], wl2[s2], nb2[s2]
            ka = lambda n: "%s%d" % (n, s2)
            a_ = dt_[:, 128:192]

            def prologue():
                if i + 1 < len(items):
                    load_b(i + 1)
                for d in range(2):
                    b.mm(AC[:, d * 32:(d + 1) * 32], utri[d][:], a_[:, d * 32:(d + 1) * 32], True, True,
                         ["utri%d" % d, dk], ["b_AC"])
                    b.mm(AC[:, 64 + d * 32:64 + (d + 1) * 32], ones_f[:], a_[:, d * 32:(d + 1) * 32], True, True,
                         ["ones_f", dk], ["b_AC"])
                b.cp(acs[:], AC[:, 0:64], ["b_AC"], [ka("acs")])
                b.act(ea[:], acs[:], AF.Exp, [ka("acs")], [ka("ea")])
                b.act(cd[:], AC[:, 64:128], AF.Exp, ["b_AC"], [ka("cd")])
                b.tt(de[:], AC[:, 64:128], acs[:], ALU.subtract, ["b_AC", ka("acs")], [ka("de")])
                b.act(de[:], de[:], AF.Exp, [ka("de")], [ka("de")])
                b.tt(wl[:], dt_[:, 0:64], de[:], ALU.mult, [dk, ka("de")], [ka("wl")])
                b.tt(nb[:], dt_[:, 64:128], acs[:], ALU.subtract, [dk, ka("acs")], [ka("nb")])
                for g4 in range(4):
                    b.mm(CB[:, g4 * 128:(g4 + 1) * 128], bT_[:, g4, :], cT_[:, g4, :], True, True, [bTk, cTk], ["b_CB"])
                b.act(cbT[:], CB[:], AF.Copy, ["b_CB"], ["cbT"])
                xv = xt_[:].rearrange("p (h q) -> p h q", h=32)
                for d in range(2):
                    b.tt(xdec2[s2][d][:].rearrange("p (h q) -> p h q", h=32), xv,
                         bc(wl[:, d * 32:(d + 1) * 32].unsqueeze(2), [128, 32, 64]), ALU.mult, [xk, ka("wl")],
                         ["xdec%d_%d" % (s2, d)], q=("pool" if d == 0 else "dve"))

            def banks(d, js):
                for j in js:
                    DB_, dbk = DB[j % 2], "b_DB%d" % (j % 2)
                    Dm_, dmk = Dm[j % 4], "Dm%d" % (j % 4)
                    b.mm(DB_[:], negm[d][:], i4[:], True, False, ["negm%d" % d, "i4"], [dbk])
                    for hh in range(4):
                        h = j * 4 + hh
                        b.mm(DB_[:, hh * 128:(hh + 1) * 128], bc(a_[:, d * 32 + h:d * 32 + h + 1], [128, 128]),
                             utri[d][:], False, hh == 3, [dk, "utri%d" % d], [dbk])
                    b.tt(Dm_[:].rearrange("p (h l) -> p h l", h=4), DB_[:].rearrange("p (h l) -> p h l", h=4),
                         bc(nb[:, d * 32 + j * 4:d * 32 + (j + 1) * 4].unsqueeze(2), [128, 4, 128]), ALU.add,
                         [dbk, ka("nb")], [dmk])
                    b.act(Eb[d][:, j * 512:(j + 1) * 512], Dm_[:], AF.Exp, [dmk], [("E", d, j)])

            def mtbuild(d):
                b.tt(MT2[s2][d][:].rearrange("p (g r l) -> p g r l", g=4, r=8),
                     Eb[d][:].rearrange("p (g r l) -> p g r l", g=4, r=8),
                     bc(cbT[:].rearrange("p (g l) -> p g l", g=4).unsqueeze(2), [128, 4, 8, 128]), ALU.mult,
                     [("E", d, j) for j in range(8)] + ["cbT"], ["MT%d_%d" % (s2, d)], q="pool")

            def q0():
                banks(0, range(0, 4))

            def q1():
                banks(0, range(4, 8))
                mtbuild(0)

            def q2():
                banks(1, range(0, 4))

            def q3():
                banks(1, range(4, 8))
                mtbuild(1)
            return [prologue, q0, q1, q2, q3]

        def late_stages(i):
            gc, first, last, si = items[i]
            sl, s2 = i % NL, i % 2
            xt_, xk = xt2[sl], "b_xtok%d" % sl
            bt_, bk = bt2[sl], "b_btok%d" % sl
            cT_, cTk = cT2[sl], "b_cT%d" % sl
            ea, cd = ea2[s2], cd2[s2]
            eak, cdk = "ea%d" % s2, "cd%d" % s2
            yp_, ypk = ypt[s2], "ypt%d" % s2
            MT = MT2[s2]

            def head():
                if first:
                    if SEQS[si][2]:
                        load_state(stb)
                    else:
                        zero_state()

            def group(g4):
                YG_, ygk = YG[g4 % 2], "b_YG%d" % (g4 % 2)
                yo_, yok2 = yot[g4 % 2], "yot%d" % (g4 % 2)
                b.mm(YO[:], cT_[:, g4, :], hst_bf[:, g4 * 512:(g4 + 1) * 512], True, True, [cTk, ("hst_bf", g4)],
                     ["b_YO"])
                b.tt(yo_[:].rearrange("p (h q) -> p h q", h=8), YO[:].rearrange("p (h q) -> p h q", h=8),
                     bc(ea[:, 32 + g4 * 8:32 + (g4 + 1) * 8].unsqueeze(2), [128, 8, 64]), ALU.mult,
                     ["b_YO", eak], [yok2])
                for hh in range(8):
                    h = g4 * 8 + hh
                    xs_ = xt_[:, h * 64:(h + 1) * 64]
                    b.mm(YG_[:, hh * 64:(hh + 1) * 64], MT[0][:, h * 128:(h + 1) * 128], xs_, True, False,
                         ["MT%d_0" % s2, xk], [ygk])
                    b.mm(YG_[:, hh * 64:(hh + 1) * 64], MT[1][:, h * 128:(h + 1) * 128], xs_, False, False,
                         ["MT%d_1" % s2, xk], [ygk])
                    b.mm(YG_[:, hh * 64:(hh + 1) * 64], DI[:, h, :], xs_, False, True, ["DI", xk], [ygk])
                b.tt(yp_[:, g4 * 512:(g4 + 1) * 512], yo_[:], YG_[:], ALU.add, [yok2, ygk], [(ypk, g4)])

            def tail():
                b.dma(yp_s[gc], yp_[:], [(ypk, j) for j in range(4)], [("yp_s", gc)])
                b.dma(xdf_s[gc], xdec2[s2][0][:], ["xdec%d_0" % s2], [("xdf_s", gc)])
                ec_, eck = eact[s2], "eact%d" % s2
                b.cp(ec_[:, 0:32], ea[:, 0:32], [eak], [eck])
                b.cp(ec_[:, 32:64], cd[:, 0:32], [cdk], [eck])
                b.dma(eac_s[gc], ec_[:], [eck], [("eac_s", gc)])
                state_update(bt_, bk, xdec2[s2][1], "xdec%d_1" % s2, cd, cdk, 32)
                if last and not SEQS[si][2]:
                    store_state(sb_o[si - 1])
            return [head] + [(lambda g4=g4: group(g4)) for g4 in range(4)] + [tail]

        gnw_fm = b.sb("gnw_fm", [128, 16])
        b.dma(gnw_fm[:], gnw_d, [], ["gnw_fm"])
        w1stg = [b.sb("w1stg%d" % i, [128, D]) for i in range(2)]
        w1ob = [b.sb("w1ob%d" % i, [128, D], BF16) for i in range(2)]
        w1o_s = g["w1o_s"]

        def w1o_load(kc):
            b.dma(w1stg[kc % 2][:], w1out[kc * 128:(kc + 1) * 128, :], [], ["w1stg%d" % (kc % 2)])

        def w1o_scale(kc):
            b.act(w1ob[kc % 2][:], w1stg[kc % 2][:], AF.Copy, ["w1stg%d" % (kc % 2), "gnw_fm"], ["w1ob%d" % (kc % 2)],
                  scale=gnw_fm[:, kc:kc + 1])

        def w1o_store(kc):
            b.dma(w1o_s[kc * 128:(kc + 1) * 128, :], w1ob[kc % 2][:], ["w1ob%d" % (kc % 2)], [("w1o_s", kc)])

        load_b(0)
        for st in early_stages(0):
            st()
        for i in range(len(items)):
            es_ = early_stages(i + 1) if i + 1 < len(items) else None
            ls_ = late_stages(i)
            ls_[0]()
            kper = -(-16 // max(1, len(items) - 2))
            slabs = lambda j: range(j * kper, min(16, (j + 1) * kper)) if j >= 0 else range(0)
            if kper == 1:
                for kc_ in slabs(i - 2):
                    w1o_store(kc_)
                for kc_ in slabs(i):
                    w1o_load(kc_)
                for kc_ in slabs(i - 1):
                    w1o_scale(kc_)
            else:
                for kc_ in slabs(i):
                    w1o_load(kc_)
                    w1o_scale(kc_)
                    w1o_store(kc_)
            if es_:
                es_[0]()
            for q4 in range(4):
                if es_:
                    es_[1 + q4]()
                ls_[1 + q4]()
            ls_[5]()
        tk.barrier()
        b.es = es_save
    if LCUT <= 2:
        return

    with ExitStack() as ph:
        es_save = b.es
        b.es = ph
        w1o = b.sb("w1o", [128, 16, D], BF16)
        w1o_s = g["w1o_s"]
        for kc in range(16):
            b.dma(w1o[:, kc, :], w1o_s[kc * 128:(kc + 1) * 128, :], [("w1o_s", kc)], [("w1o", kc)])
        NB2 = 3
        bt3 = [b.sb("c_btok%d" % i, [128, 512], BF16) for i in range(NB2)]
        cT3 = [b.sb("c_cT%d" % i, [128, 4, 128], BF16) for i in range(NB2)]
        xd3 = [b.sb("c_xdf%d" % i, [128, 2048], BF16) for i in range(NB2)]
        yp3 = [b.sb("c_yp%d" % i, [128, 2048]) for i in range(NB2)]
        ec3 = [b.sb("c_ec%d" % i, [128, 64]) for i in range(NB2)]
        sz3 = [b.sb("c_sz%d" % i, [128, 2048], BF16) for i in range(NB2)]
        x13 = [b.sb("c_x1%d" % i, [128, D]) for i in range(NB2)]
        ygw3 = [b.sb("c_ygw%d" % i, [128, 2048], BF16) for i in range(NB2)]
        ss3 = [b.sb("c_ss%d" % i, [128, 8]) for i in range(NB2)]
        rs3 = [b.sb("c_rs%d" % i, [128, 2]) for i in range(NB2)]
        ygT = b.sb("c_ygT", [128, 2048], BF16)
        yout = [b.sb("c_yout%d" % i, [128, D]) for i in range(2)]
        junk2 = b.sb("c_junk2", [128, 512], BF16)
        yot = [b.sb("c_yot%d" % i, [128, 512]) for i in range(2)]
        hst = b.sb("c_hst", [128, 2048])
        hst_bf = b.sb("c_hst_bf", [128, 2048], BF16)
        htmp = b.sb("c_htmp", [128, 512])
        stin = b.sb("c_stin", [128, 128])
        stout = b.sb("c_stout", [128, 128])
        YO2 = [b.ps("c_YO%d" % i, [128, 512]) for i in range(2)]
        ST = b.ps("c_ST", [128, 512], keys=["b_ST"])
        TP2 = b.ps("c_TP", [128, 1024], keys=[("c_TP", 0), ("c_TP", 1)])
        OP2 = b.ps("c_OP", [128, 1024], keys=[("c_OP", 0), ("c_OP", 1)])
        ST2 = b.ps("c_ST2", [128, 512], keys=["b_ST2"])
        htmp2 = b.sb("c_htmp2", [128, 512])
        load_state, zero_state, store_state, state_update = make_state_fns(hst, hst_bf, htmp, stin, stout, ST, ST2, htmp2)
        TPv = TP2[:].bitcast(BF16)

        def chunk_front(gc, s_, cond):
            bt_, bk = bt3[s_], "c_btok%d" % s_
            cT_, cTk = cT3[s_], "c_cT%d" % s_
            xd_, xdk = xd3[s_], "c_xdf%d" % s_
            yp_, ypk = yp3[s_], "c_yp%d" % s_
            ec_, eck = ec3[s_], "c_ec%d" % s_
            sz_, szk = sz3[s_], "c_sz%d" % s_
            x1_, x1k = x13[s_], "c_x1%d" % s_
            ygw, ygk = ygw3[s_], "c_ygw%d" % s_
            ss4, ssk = ss3[s_], "c_ss%d" % s_
            rs, rsk = rs3[s_], "c_rs%d" % s_
            for g4 in range(4):
                YO_, yk = YO2[g4 % 2], "c_YO%d" % (g4 % 2)
                yo_, yok2 = yot[g4 % 2], "c_yot%d" % (g4 % 2)
                b.mm(YO_[:], cT_[:, g4, :], hst_bf[:, g4 * 512:(g4 + 1) * 512], True, True, [cTk, ("hst_bf", g4)], [yk])
                b.tt(yo_[:].rearrange("p (h q) -> p h q", h=8), YO_[:].rearrange("p (h q) -> p h q", h=8),
                     bc(ec_[:, g4 * 8:(g4 + 1) * 8].unsqueeze(2), [128, 8, 64]), ALU.mult, [yk, eck], [yok2])
                ysl = yp_[:, g4 * 512:(g4 + 1) * 512]
                b.tt(ysl, ysl, yo_[:], ALU.add, [(ypk, g4), yok2], [(ypk, g4)])
            state_update(bt_, bk, xd_, xdk, ec_, eck, 32)
            for g4 in range(4):
                ysl = yp_[:, g4 * 512:(g4 + 1) * 512]
                ygs = ygw[:, g4 * 512:(g4 + 1) * 512]
                b.tt(ygs, ysl, sz_[:, g4 * 512:(g4 + 1) * 512], ALU.mult, [(ypk, g4), szk], [(ygk, g4)],
                     q=("pool" if g4 % 2 == 0 else "dve"))
                b.act(junk2[:], ygs, AF.Square, [(ygk, g4)], ["c_junk2"], accum_out=ss4[:, g4:g4 + 1])
            b.tt(ss4[:, 4:5], ss4[:, 0:1], ss4[:, 1:2], ALU.add, ["c_junk2"], [ssk + "a"])
            b.tt(ss4[:, 5:6], ss4[:, 2:3], ss4[:, 3:4], ALU.add, ["c_junk2"], [ssk + "b"])
            b.tt(ss4[:, 6:7], ss4[:, 4:5], ss4[:, 5:6], ALU.add, [ssk + "a", ssk + "b"], [ssk + "c"])
            b.act(rs[:, 0:1], ss4[:, 6:7], AF.Ln, [ssk + "c", "eps_t"], [rsk + "0"], scale=1.0 / 2048, bias=eps_t[:, 0:1])
            b.act(rs[:, 1:2], rs[:, 0:1], AF.Exp, [rsk + "0"], [rsk], scale=-0.5)

        def load_c(gc, s_):
            b.dma(bt3[s_][:], btok_s[gc], [("btok_s", gc)], ["c_btok%d" % s_])
            b.dma(cT3[s_][:], cT_s[gc], [("cT_s", gc)], ["c_cT%d" % s_])
            b.dma(ec3[s_][:], eac_s[gc], [("eac_s", gc)], ["c_ec%d" % s_])
            b.dma(xd3[s_][:], xdf_s[gc], [("xdf_s", gc)], ["c_xdf%d" % s_])
            b.dma(yp3[s_][:], yp_s[gc], [("yp_s", gc)], [("c_yp%d" % s_, i) for i in range(4)])
            b.dma(sz3[s_][:], sz_s[gc], [("sz_s", gc)], ["c_sz%d" % s_])
            b.dma(x13[s_][:], x1s[gc * 128:(gc + 1) * 128, :], [("x1s", gc * 128)], ["c_x1%d" % s_])

        def chunk_back(gc, s_, cond, oi):
            x1_, x1k = x13[s_], "c_x1%d" % s_
            ygw, ygk = ygw3[s_], "c_ygw%d" % s_
            rs, rsk = rs3[s_], "c_rs%d" % s_
            yo, yok = yout[oi % 2], "c_yout%d" % (oi % 2)
            for kc in range(16):
                b.tr(TPv[:, kc * 128:(kc + 1) * 128], ygw[:, kc * 128:(kc + 1) * 128], ident_bf[:],
                     [(ygk, kc // 4), "ident_bf"], [("c_TP", kc // 8)])
            b.act(ygT[:, 0:1024], TPv[:, 0:1024], AF.Copy, [("c_TP", 0)], [("c_ygT", 0)])
            b.cp(ygT[:, 1024:2048], TPv[:, 1024:2048], [("c_TP", 1)], [("c_ygT", 1)])
            for half in range(2):
                for kc in range(16):
                    b.mm(OP2[:, half * 512:(half + 1) * 512], ygT[:, kc * 128:(kc + 1) * 128],
                         w1o[:, kc, half * 512:(half + 1) * 512], kc == 0, kc == 15,
                         [("c_ygT", kc // 8), ("w1o", kc)], [("c_OP", half)])
                b.stt(yo[:, half * 512:(half + 1) * 512], OP2[:, half * 512:(half + 1) * 512], rs[:, 1:2],
                      gate_bc[1][cond][:, half * 512:(half + 1) * 512], ALU.mult, ALU.mult,
                      [("c_OP", half), rsk, "gbc1%d" % cond], [(yok, half)])
                b.tt(yo[:, half * 512:(half + 1) * 512], yo[:, half * 512:(half + 1) * 512],
                     x1_[:, half * 512:(half + 1) * 512], ALU.add, [(yok, half), x1k], [(yok, half)],
                     q=("dve" if half == 0 else "pool"))
            b.dma(y_o[gc * 128:(gc + 1) * 128, :], yo[:], [(yok, 0), (yok, 1)], [], is_out=True)

        li = 0
        oi = 0
        prompt_i = 0
        pending = None
        for si, (tok0, T, is_s) in enumerate(SEQS):
            nch = T // 128
            gc0 = tok0 // 128
            cond = 0 if is_s else 1
            if is_s:
                load_state(stf)
            else:
                zero_state()
            for c in range(nch):
                s_ = li % NB2
                if li == 0:
                    load_c(gc0 + c, s_)
                if gc0 + c + 1 < NCH:
                    load_c(gc0 + c + 1, (li + 1) % NB2)
                li += 1
                chunk_front(gc0 + c, s_, cond)
                if pending is not None:
                    chunk_back(*pending, oi)
                    oi += 1
                pending = (gc0 + c, s_, cond)
            if not is_s:
                store_state(sf_o[prompt_i])
                prompt_i += 1
        if pending is not None:
            chunk_back(*pending, oi)
        b.es = es_save


def _consts():
    ident = np.eye(128, dtype=np.float32)
    pm = np.zeros((128, 128), np.float32)
    for m in range(128):
        d = m % 64
        part = (d % 32) // 16
        k = m + 16 if part == 0 else m - 16
        pm[k, m] = 1.0
    bones = np.zeros((128, 128), np.float32)
    bones[0:64, 0:64] = 1.0
    bones[64:128, 64:128] = 1.0
    s = np.arange(128)[:, None]
    l = np.arange(128)[None, :]
    utf = (s <= l).astype(np.float32)
    utb = (s >= l).astype(np.float32)
    negf = np.where(s < l, -30000.0, 0.0).astype(np.float32)
    negb = np.where(s > l, -30000.0, 0.0).astype(np.float32)
    i4 = np.tile(ident, (1, 4)).astype(np.float32)
    t = np.arange(TS)
    pos = np.stack([t // 64, t % 64]).astype(np.float32)
    inv = (1.0 / (10000.0 ** (np.arange(0, 32, 2, dtype=np.float32) / 32.0))).astype(np.float32)
    cosT = np.zeros((128, TS), np.float32)
    sinT = np.zeros((128, TS), np.float32)
    for m in range(128):
        d = m % 64
        j, part, i = d // 32, (d % 32) // 16, d % 16
        ang = (pos[j] * inv[i]).astype(np.float32)
        cosT[m] = np.cos(ang)
        sinT[m] = np.sin(ang) * (-1.0 if part == 0 else 1.0)
    return dict(ident=ident, pm=pm, bones=bones, utf=utf, utb=utb, negf=negf, negb=negb, i4=i4, cosT=cosT, sinT=sinT)


def _fm(v, n):
    return np.ascontiguousarray(np.asarray(v, np.float32).reshape(n, 128).T)


def make_in_maps(inp):
    c = _consts()
    f32 = lambda a: np.ascontiguousarray(np.asarray(a, np.float32))
    shared = dict(c)
    for k in ("l0_mod_w", "l1_mod_w", "l0_w_in", "l0_w_out", "l1_w_in", "l1_w_out"):
        shared[k] = f32(inp[k])
    shared["l0_nw"] = _fm(inp["l0_norm_w"], 8)
    shared["l1_nw"] = _fm(inp["l1_norm_w"], 8)
    shared["l0_mb"] = _fm(inp["l0_mod_b"], 24)
    shared["l1_mb"] = _fm(inp["l1_mod_b"], 24)
    shared["l0_mbg"] = f32(inp["l0_mod_b"])[2048:3072].reshape(1, 1024)
    shared["l1_mbg"] = f32(inp["l1_mod_b"])[2048:3072].reshape(1, 1024)
    gq, gk = f32(inp["l0_q_norm"]), f32(inp["l0_k_norm"])
    d = np.arange(128) % 64
    part = (d % 32) // 16
    pi = np.where(part == 0, d + 16, d - 16)
    shared["gvec"] = np.ascontiguousarray(np.stack([gq[d], gq[pi], gk[d], gk[pi]], axis=1))
    cw = f32(inp["l1_conv_w"])
    cb_ = f32(inp["l1_conv_b"])
    convw = np.stack([cw[0], cw[1], cw[2], cb_], axis=1)
    shared["convw"] = np.ascontiguousarray(convw.reshape(24, 128, 4).transpose(1, 0, 2))
    shared["dtb"] = np.concatenate([f32(inp["l1_dt_bias_f"]), f32(inp["l1_dt_bias_b"])]).reshape(1, 64)
    shared["alog"] = np.concatenate([f32(inp["l1_a_log_f"]), f32(inp["l1_a_log_b"])]).reshape(1, 64)
    shared["dsk"] = f32(inp["l1_d_skip"]).reshape(1, 32)
    shared["gnw"] = _fm(inp["l1_gnorm_w"], 16)
    shared["gnrow"] = f32(inp["l1_gnorm_w"]).reshape(1, 2048)
    xp, xsm = f32(inp["x_prompt"]), f32(inp["x_sample"])
    maps = []
    for core in range(8):
        m = dict(shared)
        m["xs"] = np.ascontiguousarray(np.concatenate([xsm[core], xp[2 * core], xp[2 * core + 1]], axis=0))
        m["ck"] = f32(inp["cache_k_l0"])[core].reshape(CTX, 256)
        m["cv"] = f32(inp["cache_v_l0"])[core].reshape(CTX, 256)
        m["stf"] = f32(inp["state_fwd_l1"])[core].reshape(2048, 128)
        m["stb"] = f32(inp["state_bwd_l1"])[core].reshape(2048, 128)
        cc = np.stack([f32(inp["c"])[core], f32(inp["c_ctx"])], axis=1)
        m["cvec"] = np.ascontiguousarray(cc.reshape(8, 128, 2).transpose(1, 0, 2))
        maps.append(m)
    return maps


_NC_CACHE = {}


def kernel(**inputs):
    if "nc" not in _NC_CACHE:
        _NC_CACHE["nc"] = build_program()
    nc = _NC_CACHE["nc"]
    maps = make_in_maps(inputs)
    res = run_bass_kernel_spmd(nc, maps, core_ids=list(range(8)))
    r = res.results
    y_prompt = np.zeros((16, TP, D), np.float32)
    y_sample = np.zeros((8, TS, D), np.float32)
    nk = np.zeros((16, TP, 4, 64), np.float32)
    nv = np.zeros((16, TP, 4, 64), np.float32)
    sf = np.zeros((16, 32, 64, 128), np.float32)
    sbw = np.zeros((16, 32, 64, 128), np.float32)
    for c in range(8):
        y = r[c]["y"]
        y_sample[c] = y[0:TS]
        y_prompt[2 * c] = y[TS:TS + TP]
        y_prompt[2 * c + 1] = y[TS + TP:]
        nk[2 * c:2 * c + 2] = r[c]["nk"].reshape(2, TP, 4, 64)
        nv[2 * c:2 * c + 2] = r[c]["nv"].reshape(2, TP, 4, 64)
        sf[2 * c:2 * c + 2] = r[c]["sf"].reshape(2, 32, 64, 128)
        sbw[2 * c:2 * c + 2] = r[c]["sbw"].reshape(2, 32, 64, 128)
    return (y_prompt, y_sample, nk, nv, sf, sbw)
```

```python
import os
import numpy as np
from contextlib import ExitStack
import concourse.bass as bass
import concourse.mybir as mybir
from concourse.bass_utils import run_bass_kernel_spmd

F32 = mybir.dt.float32
BF16 = mybir.dt.bfloat16
AF = mybir.ActivationFunctionType
ALU = mybir.AluOpType

D = 1024
KC = 8
TS = int(os.environ.get("K_TS", "4096"))
TP = 256
NTOK = TS + 2 * TP
CTX = 512
EPS = 1e-6
SEQS = [(0, TS, True), (TS, TP, False), (TS + TP, TP, False)]
NCH = NTOK // 128
H1OFF = [0, TS + 2, TS + 2 + TP + 2]
H1COLS = TS + 2 + 2 * (TP + 2)

ENGS = ["pe", "act", "dve", "pool", "sp"]
NSLOT = 8


class Op:
    __slots__ = ("q", "name", "args", "kw", "deps", "sig", "cnt", "dma", "slot", "dval", "gi")


class TK:
    def __init__(self):
        self.ops = {q: [] for q in ENGS}
        self.lastw = {}
        self.rd = {}
        self.ndma = {q: 0 for q in ENGS}
        self.fence = []
        self.fenced = {q: True for q in ENGS}
        self.last = {q: None for q in ENGS}
        self.dma_live = {}
        self.n = 0
        self.psum_keys = set()

    def op(self, q, name, args=(), kw=None, R=(), W=(), dma=False):
        o = Op()
        o.q, o.name, o.args, o.kw, o.dma = q, name, args, (kw or {}), dma
        o.sig, o.cnt, o.slot, o.dval = False, 0, 0, 0
        o.gi = self.n
        self.n += 1
        deps = {}
        if self.psum_keys:
            pr = [k for k in R if k in self.psum_keys]
            if pr:
                R = [k for k in R if k not in self.psum_keys]
                W = list(W) + [k for k in pr if k not in W]

        def need(d, raw):
            if d is None:
                return
            if d.dma or dma or d.q != q or raw or q != "pe":
                deps[d.gi] = d

        for k in R:
            need(self.lastw.get(k), True)
        for k in W:
            need(self.lastw.get(k), False)
            for r in self.rd.get(k, ()):
                need(r, False)
        if not self.fenced[q]:
            for d in self.fence:
                if d is not None:
                    deps[d.gi] = d
            self.fenced[q] = True
        o.deps = list(deps.values())
        for k in R:
            lst = self.rd.setdefault(k, [])
            if not dma:
                for i, r in enumerate(lst):
                    if (not r.dma) and r.q == q:
                        lst[i] = o
                        break
                else:
                    lst.append(o)
            else:
                lst.append(o)
        for k in W:
            self.lastw[k] = o
            self.rd[k] = []
        if dma:
            i = self.ndma[q]
            self.ndma[q] += 1
            o.slot = i % NSLOT
            o.dval = 16 * (i // NSLOT + 1)
            self.dma_live[(q, o.slot)] = o
        else:
            self.last[q] = o
        self.ops[q].append(o)
        return o

    def barrier(self, skip_dma_queues=()):
        self.fence = [self.last[q] for q in ENGS] + [o for (q, _), o in self.dma_live.items()
                                                     if q not in skip_dma_queues]
        self.fenced = {q: False for q in ENGS}

    def emit(self, nc, es, final_deps):
        fo = Op()
        fo.q, fo.name, fo.args, fo.kw, fo.dma = "sp", None, (), {}, False
        fo.sig, fo.cnt, fo.slot, fo.dval, fo.gi = False, 0, 0, 0, self.n
        fo.deps = [d for d in final_deps if d is not None] + [self.last[q] for q in ENGS if self.last[q] is not None] \
            + list(self.dma_live.values())
        self.ops["sp"].append(fo)
        for q in ENGS:
            for o in self.ops[q]:
                for d in o.deps:
                    if not d.dma:
                        d.sig = True
        for q in ENGS:
            c = 0
            for o in self.ops[q]:
                if o.sig:
                    c += 1
                    o.cnt = c
        sem = {q: es.enter_context(nc.semaphore("s_" + q)) for q in ENGS}
        dsem = {}
        for q in ENGS:
            if self.ndma[q]:
                dsem[q] = [es.enter_context(nc.semaphore("d_%s%d" % (q, i))) for i in range(NSLOT)]
        block = es.enter_context(nc.Block())
        bname = {"pe": "tensor", "act": "scalar", "dve": "vector", "pool": "gpsimd", "sp": "sync"}
        ops = self.ops

        def run(eng, q):
            waited = {}

            def w(key, s, val):
                if waited.get(key, 0) < val:
                    eng.wait_ge(s, val)
                    waited[key] = val

            for o in ops[q]:
                for d in o.deps:
                    if d.dma:
                        w((d.q, d.slot), dsem[d.q][d.slot], d.dval)
                    else:
                        w(d.q, sem[d.q], d.cnt)
                if o.dma and o.dval > 16:
                    w((q, o.slot), dsem[q][o.slot], o.dval - 16)
                if o.name is None:
                    continue
                ins = getattr(eng, o.name)(*o.args, **o.kw)
                if o.dma:
                    ins.then_inc(dsem[q][o.slot], 16)
                elif o.sig:
                    ins.then_inc(sem[q], 1)

        for q in ENGS:
            getattr(block, bname[q])(lambda eng, q=q: run(eng, q))


class B:
    def __init__(self, nc, es):
        self.nc, self.es, self.tk = nc, es, TK()
        self.outs = []

    def sb(self, name, shape, dt=F32):
        return self.es.enter_context(self.nc.sbuf_tensor(name, list(shape), dt))

    def ps(self, name, shape, dt=F32, keys=None):
        for k in (keys or [name]):
            self.tk.psum_keys.add(k)
        return self.es.enter_context(self.nc.psum_tensor(name, list(shape), dt))

    def dma(self, out, in_, R, W, q="sp", is_out=False, **kw):
        o = self.tk.op(q, "dma_start", (), dict(out=out, in_=in_, **kw), R, W, dma=True)
        if is_out:
            self.outs.append(o)
        return o

    def mm(self, out, lhsT, rhs, start, stop, R, W, **kw):
        return self.tk.op("pe", "matmul", (out,), dict(lhsT=lhsT, rhs=rhs, start=start, stop=stop, **kw), R, W)

    def tr(self, out, in_, ident, R, W):
        return self.tk.op("pe", "transpose", (out, in_, ident), {}, R, W)

    def act(self, out, in_, func, R, W, **kw):
        return self.tk.op("act", "activation", (), dict(out=out, in_=in_, func=func, **kw), R, W)

    def tt(self, out, in0, in1, op, R, W, q="dve"):
        return self.tk.op(q, "tensor_tensor", (), dict(out=out, in0=in0, in1=in1, op=op), R, W)

    def ts(self, out, in0, s1, s2, op0, op1, R, W, q="dve"):
        kw = dict(out=out, in0=in0, scalar1=s1, scalar2=s2, op0=op0)
        if op1 is not None:
            kw["op1"] = op1
        return self.tk.op(q, "tensor_scalar", (), kw, R, W)

    def stt(self, out, in0, scalar, in1, op0, op1, R, W, q="dve"):
        return self.tk.op(q, "scalar_tensor_tensor", (), dict(out=out, in0=in0, scalar=scalar, in1=in1, op0=op0, op1=op1), R, W)

    def cp(self, out, in_, R, W, q="dve"):
        return self.tk.op(q, "tensor_copy", (), dict(out=out, in_=in_), R, W)

    def rcp(self, out, in_, R, W):
        return self.tk.op("dve", "reciprocal", (), dict(out=out, in_=in_), R, W)

    def mset(self, ap, val, W, q="dve"):
        return self.tk.op(q, "memset", (ap, val), {}, (), W)


def bc(ap, shape):
    return ap.to_broadcast(list(shape))


def build_program(stage=99, debug=False):
    nc = bass.Bass("TRN2", target_bir_lowering=False)
    es = ExitStack()
    b = B(nc, es)
    tk = b.tk

    def din(name, shape, dt=F32):
        return nc.dram_tensor(name, list(shape), dt, kind="ExternalInput").ap()

    def dout(name, shape, dt=F32):
        return nc.dram_tensor(name, list(shape), dt, kind="ExternalOutput").ap()

    def dscr(name, shape, dt=F32, dbg=False):
        if dbg and debug:
            return nc.dram_tensor(name, list(shape), dt, kind="ExternalOutput").ap()
        return nc.dram_tensor(name, list(shape), dt).ap()

    xs = din("xs", [NTOK, D])
    ck = din("ck", [CTX, 256])
    cv = din("cv", [CTX, 256])
    stf = din("stf", [2048, 128])
    stb = din("stb", [2048, 128])
    cvec = din("cvec", [128, KC, 2])
    modw = [din("l0_mod_w", [D, 3 * D]), din("l1_mod_w", [D, 3 * D])]
    w0in = din("l0_w_in", [D, 2560])
    w0out = din("l0_w_out", [D, D])
    w1in = din("l1_w_in", [D, 5184])
    w1out = din("l1_w_out", [2048, D])
    nw_d = [din("l0_nw", [128, KC]), din("l1_nw", [128, KC])]
    mb_d = [din("l0_mb", [128, 24]), din("l1_mb", [128, 24])]
    mbg_d = [din("l0_mbg", [1, D]), din("l1_mbg", [1, D])]
    gvec_d = din("gvec", [128, 4])
    convw_d = din("convw", [128, 24, 4])
    dtb_d = din("dtb", [1, 64])
    alog_d = din("alog", [1, 64])
    dsk_d = din("dsk", [1, 32])
    gnw_d = din("gnw", [128, 16])
    gnrow_d = din("gnrow", [1, 2048])
    ident_d = din("ident", [128, 128])
    pm_d = din("pm", [128, 128])
    bones_d = din("bones", [128, 128])
    utf_d = din("utf", [128, 128])
    utb_d = din("utb", [128, 128])
    negf_d = din("negf", [128, 128])
    negb_d = din("negb", [128, 128])
    i4_d = din("i4", [128, 512])
    cos_d = din("cosT", [128, TS])
    sin_d = din("sinT", [128, TS])

    y_o = dout("y", [NTOK, D])
    nk_o = dout("nk", [2 * TP, 256])
    nv_o = dout("nv", [2 * TP, 256])
    sf_o = dout("sf", [2, 2048, 128])
    sb_o = dout("sbw", [2, 2048, 128])

    h0T = dscr("h0T", [128, KC, NTOK], BF16, dbg=True)
    h1T = dscr("h1T", [128, KC, H1COLS], BF16, dbg=True)
    x1s = dscr("x1s", [NTOK, D], F32, dbg=True)
    wqg_s = dscr("wqg_s", [16, 128, KC, 128], BF16)
    w1_s = dscr("w1_s", [D, 5184], BF16)
    w1o_s = dscr("w1o_s", [2048, D], BF16)

    ident_bf = b.sb("ident_bf", [128, 128], BF16)
    ident_f = b.sb("ident_f", [128, 128], F32)
    ones_bf = b.sb("ones_bf", [128, 128], BF16)
    ones_f = b.sb("ones_f", [128, 128], F32)
    mw = [[b.sb("mw%d%d" % (l, c), [128, KC]) for c in range(2)] for l in range(2)]
    sh = [[b.sb("sh%d%d" % (l, c), [128, KC]) for c in range(2)] for l in range(2)]
    gate_bc = [[b.sb("gbc%d%d" % (l, c), [128, D]) for c in range(2)] for l in range(2)]

    b.dma(ident_f[:], ident_d, [], ["ident_f"])
    b.dma(ident_bf[:], ident_d, [], ["ident_bf"], q="pool")
    b.mset(ones_bf[:], 1.0, ["ones_bf"])
    b.mset(ones_f[:], 1.0, ["ones_f"])

    eps_t = b.sb("eps_t", [128, 1])
    b.mset(eps_t[:], EPS, ["eps_t"])
    prep_tmp = {"junk": b.sb("pjunk", [128, D], BF16), "ss": b.sb("pss", [128, 4]),
                "xn": [b.sb("pxn%d" % i, [128, D], BF16) for i in range(2)]}

    l0w_scope = ExitStack()
    _es0 = b.es
    b.es = l0w_scope
    wkv = b.sb("wkv", [128, KC, 512], BF16)
    wout = b.sb("wout", [128, 8, D], BF16)
    pm_bf = b.sb("pm_bf", [128, 128], BF16)
    bones_bf = b.sb("bones_bf", [128, 128], BF16)
    gv = b.sb("gv", [128, 4])
    VA = b.sb("VA", [128, (TS + CTX) // 128, 4, 128], BF16)
    ckt = b.sb("ckt", [128, 4, 256], BF16)
    b.es = _es0
    b.mset(VA[:, :, :, 64:128], 1.0, ["VAones"])
    for g4_ in range(4):
        b.dma(VA[:, 0:4, g4_, 0:64], cv[:, g4_ * 64:(g4_ + 1) * 64].rearrange("(kt p) d -> p kt d", p=128),
              [], [("VA", kt_) for kt_ in range(4)], q="pool")
    b.dma(ckt[:], ck.rearrange("(kt p) c -> p kt c", p=128), [], ["ckt"], q="pool")
    b.dma(wkv[:], w0in[:, 1024:1536].rearrange("(kc p) c -> p kc c", p=128), [], ["wkv"], q="pool")
    for p_ in range(8):
        a_, i_ = p_ // 4, p_ % 4
        hx_, hy_ = 8 * a_ + i_, 8 * a_ + 4 + i_
        b.dma(wout[0:64, p_, :], w0out[hx_ * 64:(hx_ + 1) * 64, :], [], [("wout", p_)], q="pool")
        b.dma(wout[64:128, p_, :], w0out[hy_ * 64:(hy_ + 1) * 64, :], [], [("wout", p_)], q="pool")
    b.dma(pm_bf[:], pm_d, [], ["pm_bf"], q="pool")
    b.dma(bones_bf[:], bones_d, [], ["bones_bf"], q="pool")
    b.dma(gv[:], gvec_d, [], ["gv"])


    with ExitStack() as ph:
        es_save = b.es
        b.es = ph
        cv_f = b.sb("cv_f", [128, KC, 2])
        scT = b.sb("scT", [128, KC, 2], BF16)
        screp = [b.sb("screp%d" % c, [128, KC, 128], BF16) for c in range(2)]
        wpart = b.sb("wpart", [128, KC, D], BF16)
        wpf = [b.sb("wpf%d" % i, [128, KC, D]) for i in range(2)]
        npart = [0]
        nw_t = b.sb("nw_t", [128, KC])
        mb_t = b.sb("mb_t", [128, 24])
        mbg_t = b.sb("mbg_t", [128, D])
        tmp8 = b.sb("tmp8", [128, KC])
        pm0 = b.ps("pm0", [128, 512])
        pg = [b.ps("pg%d" % i, [128, 512]) for i in range(2)]

        b.dma(cv_f[:], cvec, [], ["cv_f"])
        b.act(scT[:], cv_f[:], AF.Silu, ["cv_f"], ["scT"])
        for c in range(2):
            b.cp(screp[c][:], bc(scT[:, :, c:c + 1], [128, KC, 128]), ["scT"], ["screp%d" % c])
        for l in range(2):
            b.dma(nw_t[:], nw_d[l], [], ["nw_t"])
            b.dma(mb_t[:], mb_d[l], [], ["mb_t"])
            b.dma(mbg_t[:], mbg_d[l].partition_broadcast(128), [], ["mbg_t"])
            for part in range(3):
                wf_, wfk = wpf[npart[0] % 2], "wpf%d" % (npart[0] % 2)
                npart[0] += 1
                for hq, qn in ((0, "sp"), (1, "act")):
                    b.dma(wf_[:, hq * 4:(hq + 1) * 4, :],
                          modw[l][hq * 512:(hq + 1) * 512, part * D:(part + 1) * D].rearrange("(kc p) c -> p kc c", p=128),
                          [], [(wfk, hq)], q=qn)
                for kc in range(KC):
                    if kc % 3 == 2:
                        b.act(wpart[:, kc, :], wf_[:, kc, :], AF.Copy, [(wfk, kc // 4)], [("wpart", kc)])
                    else:
                        b.cp(wpart[:, kc, :], wf_[:, kc, :], [(wfk, kc // 4)], [("wpart", kc)],
                             q=("dve" if kc % 3 == 0 else "pool"))
                if part < 2:
                    pmv = pm0[:, 0:16].rearrange("p (f c) -> p f c", c=2)
                    for fc in range(KC):
                        for kc in range(KC):
                            b.mm(pmv[:, fc, :], wpart[:, kc, fc * 128:(fc + 1) * 128], scT[:, kc, :],
                                 kc == 0, kc == KC - 1, [("wpart", kc), "scT"], ["pm0"])
                    for c in range(2):
                        if part == 0:
                            b.tt(sh[l][c][:], pmv[:, :, c], mb_t[:, 0:8], ALU.add, ["pm0", "mb_t"], ["sh%d%d" % (l, c)])
                        else:
                            b.stt(tmp8[:], pmv[:, :, c], 1.0, mb_t[:, 8:16], ALU.add, ALU.add,
                                  ["pm0", "mb_t"], ["tmp8"])
                            b.tt(mw[l][c][:], tmp8[:], nw_t[:], ALU.mult, ["tmp8", "nw_t"], ["mw%d%d" % (l, c)])
                else:
                    for c in range(2):
                        for half in range(2):
                            for kc in range(KC):
                                b.mm(pg[half][:], screp[c][:, kc, :], wpart[:, kc, half * 512:(half + 1) * 512],
                                     kc == 0, kc == KC - 1, [("wpart", kc), "screp%d" % c], ["pg%d" % half])
                            b.tt(gate_bc[l][c][:, half * 512:(half + 1) * 512], pg[half][:],
                                 mbg_t[:, half * 512:(half + 1) * 512], ALU.add, ["pg%d" % half, "mbg_t"],
                                 ["gbc%d%d" % (l, c)])
        b.es = es_save
        ph0_keep = ph.pop_all()

    def prep_block(ph_tag, xtiles, nt, l, cond, hblk, hkey, pst, pskeys):
        for ti, (xap, xkey) in enumerate(xtiles):
            prep_tile(ti, xap, xkey, pst, pskeys)
        prep_evac(nt, l, cond, hblk, hkey, pst, pskeys)

    def prep_tile(ti, xap, xkey, pst, pskeys):
        xkeys = list(xkey) if isinstance(xkey, list) else [xkey]
        if True:
            junk, ss, xn = prep_tmp["junk"], prep_tmp["ss"], prep_tmp["xn"][ti % 2]
            xnk = "xn%d" % (ti % 2)
            b.act(junk[:], xap, AF.Square, xkeys, ["pjunk"], accum_out=ss[:, 0:1])
            b.act(ss[:, 1:2], ss[:, 0:1], AF.Ln, ["pjunk", "eps_t"], ["pss1"], scale=1.0 / D, bias=eps_t[:, 0:1])
            b.act(ss[:, 2:3], ss[:, 1:2], AF.Exp, ["pss1"], ["pss2"], scale=-0.5)
            b.ts(xn[:], xap, ss[:, 2:3], None, ALU.mult, None, xkeys + ["pss2"], [xnk])
            for kc in range(KC):
                pv = pst[kc // 2][:].bitcast(BF16)
                c0 = (kc % 2) * 512 + ti * 128
                b.tr(pv[:, c0:c0 + 128], xn[:, kc * 128:(kc + 1) * 128], ident_bf[:], [xnk, "ident_bf"],
                     [pskeys[kc // 2]])
    def prep_evac(nt, l, cond, hblk, hkey, pst, pskeys):
        for kc in range(KC):
            pv = pst[kc // 2][:].bitcast(BF16)
            c0 = (kc % 2) * 512
            b.ts(hblk[:, kc, 0:nt], pv[:, c0:c0 + nt], mw[l][cond][:, kc:kc + 1], sh[l][cond][:, kc:kc + 1],
                 ALU.mult, ALU.add, [pskeys[kc // 2], "mw%d%d" % (l, cond), "sh%d%d" % (l, cond)], [hkey])


    CUT = float(os.environ.get("K_CUT", "99"))

    def pair_heads(p):
        a, i = p // 4, p % 4
        return 8 * a + i, 8 * a + 4 + i

    with ExitStack() as ph:
        es_save = b.es
        b.es = ph
        for t in range(16):
            p = t % 8
            base = 0 if t < 8 else 1536
            hx, hy = pair_heads(p)
            for half, hh in enumerate((hx, hy)):
                b.dma(wqg_s[t, :, :, half * 64:(half + 1) * 64],
                      w0in[:, base + hh * 64:base + (hh + 1) * 64].rearrange("(kc p) c -> p kc c", p=128),
                      [], [("wqg_s", t)], q="pool")
        tk.barrier(skip_dma_queues=("pool",))
        b.es = es_save
    ph0_keep.close()

    if stage >= 1:
        with ExitStack() as ph:
            es_save = b.es
            b.es = ph
            layer0(b, locals())
            tk.barrier()
            b.es = es_save
    l0w_scope.close()

    if stage >= 2:
        with ExitStack() as ph:
            es_save = b.es
            b.es = ph
            layer1(b, locals())
            b.es = es_save

    tk.emit(nc, es, b.outs)
    es.close()
    return nc


def layer0(b, g):
    tk = b.tk
    nc = b.nc
    xs, ck, cv, w0in, w0out, gvec_d = g["xs"], g["ck"], g["cv"], g["w0in"], g["w0out"], g["gvec_d"]
    pm_d, bones_d, cos_d, sin_d = g["pm_d"], g["bones_d"], g["cos_d"], g["sin_d"]
    h0T, h1T, x1s, wqg_s = g["h0T"], g["h1T"], g["x1s"], g["wqg_s"]
    nk_o, nv_o = g["nk_o"], g["nv_o"]
    ident_bf, ident_f, gate_bc, eps_t = g["ident_bf"], g["ident_f"], g["gate_bc"], g["eps_t"]
    prep_block, pair_heads = g["prep_block"], g["pair_heads"]
    prep_tile, prep_evac = g["prep_tile"], g["prep_evac"]
    CUT = g["CUT"]

    wkv, wout, pm_bf, bones_bf, gv = g["wkv"], g["wout"], g["pm_bf"], g["bones_bf"], g["gv"]

    NKT = (TS + CTX) // 128
    KT = b.sb("KT", [128, 2, TS + CTX], BF16)
    VA, ckt = g["VA"], g["ckt"]

    hblk = b.sb("hblk", [128, KC, 512], BF16)
    qr = b.sb("qr", [128, 8, 512], BF16)
    sg = b.sb("sg", [128, 8, 512], BF16)
    Pt = [b.sb("Pt%d" % i, [128, 2, 512], BF16) for i in range(3)]
    wt = [b.sb("wt%d" % i, [128, KC, 128], BF16) for i in range(2)]
    cosr = b.sb("cosr", [128, 512])
    sinr = b.sb("sinr", [128, 512])
    cosg = [b.sb("cosg%d" % i, [128, 512]) for i in range(2)]
    sing = [b.sb("sing%d" % i, [128, 512]) for i in range(2)]
    qb2 = [b.sb("qb%d" % i, [128, 512], BF16) for i in range(2)]
    sq2 = [b.sb("sq%d" % i, [128, 512], BF16) for i in range(2)]
    t12 = [b.sb("t1_%d" % i, [128, 512]) for i in range(2)]
    t22 = [b.sb("t2_%d" % i, [128, 512]) for i in range(2)]
    lnr2 = [b.sb("lnr%d" % i, [128, 512]) for i in range(2)]
    rstd2 = [b.sb("rstd%d" % i, [128, 512]) for i in range(2)]
    kf = b.sb("kf", [128, 256])
    kfT = b.sb("kfT", [128, 256])
    vf = b.sb("vf", [128, 256])
    ftmp = b.sb("ftmp", [128, 512])
    frec = b.sb("frec", [128, 512])
    xt = [b.sb("l0x%d" % i, [128, D]) for i in range(2)]
    x1 = [b.sb("l0x1%d" % i, [128, D]) for i in range(2)]
    h1b = b.sb("h1b", [128, KC, 512], BF16)
    zcol = b.sb("zcol", [128, KC, 2], BF16)
    b.mset(zcol[:], 0.0, ["zcol"])

    SA = b.ps("SA", [128, 1024], keys=[("SA", 0), ("SA", 1)])
    SB = b.ps("SB", [128, 1024], keys=[("SB", 0), ("SB", 1)])
    OA = b.ps("OA", [128, 512])
    OB = b.ps("OB", [128, 512])
    R0 = b.ps("R0", [128, 512])
    R1 = b.ps("R1", [128, 512])
    Sb = [SA, SB]
    Sk = [[("SA", 0), ("SA", 1)], [("SB", 0), ("SB", 1)]]
    Ob = [[OA, OB], [R0, R1]]
    Okey = [["OA", "OB"], ["R0", "R1"]]

    ncall = [0]

    def qk_stages(src, skey, nt, ci, outs):
        ncall[0] += 1
        par = ncall[0] % 2
        qb, sq, t1, t2, lnr, rstd = qb2[par], sq2[par], t12[par], t22[par], lnr2[par], rstd2[par]
        kq, ks, k1, k2, kl, kr = "qb%d" % par, "sq%d" % par, "t1_%d" % par, "t2_%d" % par, "lnr%d" % par, "rstd%d" % par
        (Ra, rak), (Rb, rbk) = ((R0, "R0"), (R1, "R1")) if par == 0 else ((OA, "OA"), (OB, "OB"))

        def A1():
            b.tt(t1[:, 0:nt], src, cosg[ci][:, 0:nt], ALU.mult, [skey, "cosg%d" % ci], [k1])
            b.act(qb[:, 0:nt], src, AF.Copy, [skey], [kq])
            b.tt(sq[:, 0:nt], qb[:, 0:nt], qb[:, 0:nt], ALU.mult, [kq], [ks])
            b.mm(Ra[:, 0:nt], bones_bf[:], sq[:, 0:nt], True, True, ["bones_bf", ks], [rak])
            b.mm(Rb[:, 0:nt], pm_bf[:], qb[:, 0:nt], True, True, ["pm_bf", kq], [rbk])

        def A2():
            b.act(lnr[:, 0:nt], Ra[:, 0:nt], AF.Ln, [rak, "eps_t"], [kl], scale=1.0 / 64, bias=eps_t[:, 0:1])
            b.act(rstd[:, 0:nt], lnr[:, 0:nt], AF.Exp, [kl], [kr], scale=-0.5)
            b.tt(t2[:, 0:nt], Rb[:, 0:nt], sing[ci][:, 0:nt], ALU.mult, [rbk, "sing%d" % ci], [k2])
            b.tt(t1[:, 0:nt], t1[:, 0:nt], t2[:, 0:nt], ALU.add, [k1, k2], [k1])

        def Bst():
            for (oap, okey) in outs:
                b.tt(oap, t1[:, 0:nt], rstd[:, 0:nt], ALU.mult, [k1, kr], [okey])
        return A1, A2, Bst

    def qk_pipe(src, skey, nt, ci, outs):
        for st in qk_stages(src, skey, nt, ci, outs):
            st()

    pst = [R0, R1, OA, OB]
    pk = ["R0", "R1", "OA", "OB"]
    if CUT <= 1:
        return
    wti = 0
    prompt_i = 0
    for si, (tok0, T, is_s) in enumerate(SEQS):
        nt = 512 if is_s else 256
        cond = 0 if is_s else 1
        ctx = CTX if is_s else 0
        nkt = (T + ctx) // 128
        nblk = T // nt
        if is_s:
            for kt in range(4):
                for a in range(2):
                    pv = R0[:].bitcast(BF16)
                    b.tr(pv[:, 0:128], ckt[:, kt, a * 128:(a + 1) * 128], ident_bf[:], ["ckt", "ident_bf"], ["R0"])
                    b.cp(KT[:, a, kt * 128:(kt + 1) * 128], pv[:, 0:128], ["R0"], [("KT", a)])
        if si == 0:
            w1in_, w1out_, w1_s_, w1o_s_ = g["w1in"], g["w1out"], g["w1_s"], g["w1o_s"]
            for r4 in range(4):
                b.dma(w1_s_[r4 * 256:(r4 + 1) * 256, :], w1in_[r4 * 256:(r4 + 1) * 256, :], [], [("w1_s", r4)], q="pool")
        if CUT <= 2:
            return
        if not is_s:
            b.mset(cosr[:], 1.0, ["cosr"])
            b.mset(sinr[:], 0.0, ["sinr"])
            for ci in range(2):
                b.ts(cosg[ci][:], cosr[:], gv[:, 2 * ci:2 * ci + 1], None, ALU.mult, None, ["cosr", "gv"], ["cosg%d" % ci])
                b.ts(sing[ci][:], sinr[:], gv[:, 2 * ci + 1:2 * ci + 2], None, ALU.mult, None, ["sinr", "gv"], ["sing%d" % ci])

        def load_tables(t0):
            if not is_s:
                return
            b.dma(cosr[:], cos_d[:, t0:t0 + 512], [], ["cosr"])
            b.dma(sinr[:], sin_d[:, t0:t0 + 512], [], ["sinr"])
            for ci in range(2):
                b.ts(cosg[ci][:], cosr[:], gv[:, 2 * ci:2 * ci + 1], None, ALU.mult, None, ["cosr", "gv"], ["cosg%d" % ci])
                b.ts(sing[ci][:], sinr[:], gv[:, 2 * ci + 1:2 * ci + 2], None, ALU.mult, None, ["sinr", "gv"], ["sing%d" % ci])

        for bi in range(nblk):
            t0 = bi * nt
            xbufs = [(xt[0], "l0x0"), (xt[1], "l0x1"), (x1[0], "l0x10"), (x1[1], "l0x11")]
            for ti in range(nt // 128):
                xb_, xk_ = xbufs[ti]
                g0_ = tok0 + t0 + ti * 128
                wk_ = [xk_] if ti < 2 else [(xk_, 0), (xk_, 1)]
                b.dma(xb_[:], xs[g0_:g0_ + 128, :], [], wk_)
                prep_tile(ti, xb_[:], wk_, pst, pk)
            prep_evac(nt, 0, cond, hblk, "hblk", pst, pk)
            b.dma(h0T[:, :, tok0 + t0:tok0 + t0 + nt], hblk[:, :, 0:nt], ["hblk"], [("h0T", tok0 + t0)])
            load_tables(t0)
            for a in range(2):
                src = SA[:, a * 512:a * 512 + nt]
                for kc in range(KC):
                    b.mm(src, wkv[:, kc, a * 128:(a + 1) * 128], hblk[:, kc, 0:nt], kc == 0, kc == KC - 1,
                         ["wkv", "hblk"], [("SA", a)])
                if CUT <= 2.2:
                    return
                outs = [(KT[:, a, ctx + t0:ctx + t0 + nt], ("KT", a))]
                if not is_s:
                    outs.append((kf[:, 0:nt], "kf"))
                qk_pipe(src, ("SA", a), nt, 1, outs)
                if CUT <= 2.5:
                    return
                if not is_s:
                    for ti in range(nt // 128):
                        b.tr(SB[:, ti * 128:(ti + 1) * 128], kf[:, ti * 128:(ti + 1) * 128], ident_f[:], ["kf", "ident_f"],
                             [("SB", 0)])
                    for ti in range(nt // 128):
                        b.cp(kfT[:, ti * 128:(ti + 1) * 128], SB[:, ti * 128:(ti + 1) * 128], [("SB", 0)], ["kfT"])
                        r0 = prompt_i * TP + t0 + ti * 128
                        b.dma(nk_o[r0:r0 + 128, a * 128:(a + 1) * 128], kfT[:, ti * 128:(ti + 1) * 128], ["kfT"], [],
                              is_out=True)
            if CUT <= 2.6:
                return
            for ti in range(nt // 128):
                kt = (ctx + t0) // 128 + ti
                vp = SB[:, 512:768]
                for kc in range(KC):
                    b.mm(vp, hblk[:, kc, ti * 128:(ti + 1) * 128], wkv[:, kc, 256:512], kc == 0, kc == KC - 1,
                         ["wkv", "hblk"], [("SB", 1)])
                if CUT <= 2.7:
                    return
                b.cp(VA[:, kt, :, 0:64], vp.rearrange("p (g d) -> p g d", g=4), [("SB", 1)], [("VA", kt)])
                if not is_s:
                    b.act(vf[:], vp, AF.Copy, [("SB", 1)], ["vf"])
                    r0 = prompt_i * TP + t0 + ti * 128
                    b.dma(nv_o[r0:r0 + 128, :], vf[:], ["vf"], [], is_out=True)

        if CUT <= 3:
            return
        def load_block_inputs(bi_):
            t0_ = bi_ * nt
            b.dma(hblk[:, :, 0:nt], h0T[:, :, tok0 + t0_:tok0 + t0_ + nt], [("h0T", tok0 + t0_)], ["hblk"])
            load_tables(t0_)

        load_block_inputs(0)
        gt_total = nblk * 16
        issued = set()

        def issue_w(gt):
            if gt >= gt_total or gt in issued:
                return
            issued.add(gt)
            b.dma(wt[gt % 2][:], wqg_s[gt % 16], [("wqg_s", gt % 16)], ["wt%d" % (gt % 2)])

        issue_w(0)
        for bi in range(nblk):
            t0 = bi * nt

            def proj(t):
                gt = bi * 16 + t
                w, wk = wt[gt % 2], "wt%d" % (gt % 2)
                issue_w(gt + 1)
                src = SA[:, (t % 2) * 512:(t % 2) * 512 + nt]
                skey = ("SA", t % 2)
                for kc in range(KC):
                    b.mm(src, w[:, kc, :], hblk[:, kc, 0:nt], kc == 0, kc == KC - 1, [wk, "hblk"], [skey])

            proj(0)
            prev = None
            for t in range(16):
                p = t % 8
                if t + 1 < 16:
                    proj(t + 1)
                src = SA[:, (t % 2) * 512:(t % 2) * 512 + nt]
                skey = ("SA", t % 2)
                if t < 8:
                    st3 = qk_stages(src, skey, nt, 0, [(qr[:, p, 0:nt], ("qr", p))])
                    st3[0]()
                    if prev is not None:
                        prev[1]()
                        prev[2]()
                    prev = st3
                else:
                    if prev is not None:
                        prev[1]()
                        prev[2]()
                        prev = None
                    b.act(sg[:, p, 0:nt], src, AF.Silu, [skey], [("sg", p)])
            if bi + 1 < nblk:
                load_block_inputs(bi + 1)
            if CUT <= 4:
                return
            for p in range(8):
                a = p // 4
                ob = Ob[p % 2]
                okey = Okey[p % 2]

                def qk(kt):
                    S = Sb[kt % 2]
                    for hh in range(2):
                        b.mm(S[:, hh * 512:hh * 512 + nt], KT[hh * 64:(hh + 1) * 64, a, kt * 128:(kt + 1) * 128],
                             qr[hh * 64:(hh + 1) * 64, p, 0:nt], True, True, [("KT", a), ("qr", p)],
                             [Sk[kt % 2][hh]])

                qk(0)
                if nkt > 1:
                    qk(1)
                for kt in range(nkt):
                    S = Sb[kt % 2]
                    P = Pt[kt % 3]
                    b.act(P[:, :, 0:nt], S[:].rearrange("p (h t) -> p h t", h=2)[:, :, 0:nt], AF.Exp,
                          Sk[kt % 2], [("P", kt % 3)], scale=0.125)
                    if kt + 2 < nkt:
                        qk(kt + 2)
                    for hh in range(2):
                        b.mm(ob[hh][:, 0:nt], VA[:, kt, 2 * a + hh, :], P[:, hh, 0:nt], kt == 0, kt == nkt - 1,
                             [("VA", kt), "VAones", ("P", kt % 3)], [okey[hh]])
                for hh in range(2):
                    lo, hi = hh * 64, (hh + 1) * 64
                    b.tt(ftmp[lo:hi, 0:nt], ob[hh][0:64, 0:nt], sg[lo:hi, p, 0:nt], ALU.mult,
                         [okey[hh], ("sg", p)], [("ftmp", hh)])
                    b.rcp(frec[lo:hi, 0:nt], ob[hh][64:128, 0:nt], [okey[hh]], [("frec", hh)])
                    b.tt(qr[lo:hi, p, 0:nt], ftmp[lo:hi, 0:nt], frec[lo:hi, 0:nt], ALU.mult,
                         [("ftmp", hh), ("frec", hh)], [("qr", p)])
            if CUT <= 5:
                return
            issue_w((bi + 1) * 16)
            issue_w((bi + 1) * 16 + 1)
            def outproj(ti_):
                S_, sn = (SA, "SA") if ti_ % 2 == 0 else (SB, "SB")
                g0_ = tok0 + t0 + ti_ * 128
                b.dma(xt[ti_ % 2][:], xs[g0_:g0_ + 128, :], [], ["l0x%d" % (ti_ % 2)])
                for half in range(2):
                    for p in range(8):
                        b.mm(S_[:, half * 512:(half + 1) * 512], qr[:, p, ti_ * 128:(ti_ + 1) * 128],
                             wout[:, p, half * 512:(half + 1) * 512], p == 0, p == 7,
                             [("qr", p), ("wout", p)], [(sn, half)])

            outproj(0)
            for ti in range(nt // 128):
                g0 = tok0 + t0 + ti * 128
                xti = xt[ti % 2]
                S_, sn = (SA, "SA") if ti % 2 == 0 else (SB, "SB")
                if ti + 1 < nt // 128:
                    outproj(ti + 1)
                for half in range(2):
                    x1h = x1[ti % 2][:, half * 512:(half + 1) * 512]
                    b.tt(x1h, S_[:, half * 512:(half + 1) * 512],
                         gate_bc[0][cond][:, half * 512:(half + 1) * 512], ALU.mult,
                         [(sn, half), "gbc0%d" % cond], [("l0x1%d" % (ti % 2), half)])
                    b.tt(x1h, x1h, xti[:, half * 512:(half + 1) * 512], ALU.add,
                         [("l0x1%d" % (ti % 2), half), "l0x%d" % (ti % 2)], [("l0x1%d" % (ti % 2), half)])
                x1k_ = [("l0x1%d" % (ti % 2), 0), ("l0x1%d" % (ti % 2), 1)]
                b.dma(x1s[g0:g0 + 128, :], x1[ti % 2][:], x1k_, [("x1s", g0)])
                prep_tile(ti, x1[ti % 2][:], x1k_, pst, pk)
            prep_evac(nt, 1, cond, h1b, "h1b", pst, pk)
            c0 = H1OFF[si] + 1 + t0
            b.dma(h1T[:, :, c0:c0 + nt], h1b[:, :, 0:nt], ["h1b"], [("h1T", si)])
        if CUT <= 7:
            return
        b.dma(h1T[:, :, H1OFF[si]:H1OFF[si] + 1], zcol[:, :, 0:1], ["zcol"], [("h1T", si)],
              allow_slow_non_contiguous=True)
        b.dma(h1T[:, :, H1OFF[si] + T + 1:H1OFF[si] + T + 2], zcol[:, :, 1:2], ["zcol"], [("h1T", si)],
              allow_slow_non_contiguous=True)
        if not is_s:
            prompt_i += 1


def layer1(b, g):
    tk = b.tk
    nc = b.nc
    w1in, w1out = g["w1in"], g["w1out"]
    h1T, x1s = g["h1T"], g["x1s"]
    convw_d, dtb_d, alog_d, dsk_d, gnw_d = g["convw_d"], g["dtb_d"], g["alog_d"], g["dsk_d"], g["gnw_d"]
    utf_d, utb_d, negf_d, negb_d, i4_d = g["utf_d"], g["utb_d"], g["negf_d"], g["negb_d"], g["i4_d"]
    stf, stb, y_o, sf_o, sb_o = g["stf"], g["stb"], g["y_o"], g["sf_o"], g["sb_o"]
    ident_bf, ident_f, ones_bf, gate_bc, eps_t = g["ident_bf"], g["ident_f"], g["ones_bf"], g["gate_bc"], g["eps_t"]
    dscr = g["dscr"]
    LCUT = float(os.environ.get("K_LCUT", "99"))

    xtok_s = dscr("xtok_s", [NCH, 128, 2048], BF16, dbg=True)
    btok_s = dscr("btok_s", [NCH, 128, 512], BF16, dbg=True)
    bT_s = dscr("bT_s", [NCH, 128, 4, 128], BF16, dbg=True)
    cT_s = dscr("cT_s", [NCH, 128, 4, 128], BF16, dbg=True)
    sz_s = dscr("sz_s", [NCH, 128, 2048], BF16, dbg=True)
    dts_s = dscr("dts_s", [NCH, 128, 192], F32, dbg=True)
    yp_s = dscr("yp_s", [NCH, 128, 2048], F32, dbg=True)
    xdf_s = dscr("xdf_s", [NCH, 128, 2048], BF16)
    eac_s = dscr("eac_s", [NCH, 128, 64], F32)

    with ExitStack() as ph:
        es_save = b.es
        b.es = ph
        w1 = b.sb("w1", [128, KC, 5184], BF16)
        w1_s = g["w1_s"]
        for kc in range(KC):
            b.dma(w1[:, kc, :], w1_s[kc * 128:(kc + 1) * 128, :], [("w1_s", kc // 2)], [("w1", kc)])
        w1k = [("w1", kc) for kc in range(KC)]
        cw = b.sb("cw", [128, 24, 4])
        dtb_bc = b.sb("dtb_bc", [128, 64])
        A_bc = b.sb("A_bc", [128, 64])
        b.dma(cw[:], convw_d, [], ["cw"])
        b.dma(dtb_bc[:], dtb_d.partition_broadcast(128), [], ["dtb_bc"])
        b.dma(A_bc[:], alog_d.partition_broadcast(128), [], ["A_bc"])
        b.act(A_bc[:], A_bc[:], AF.Exp, ["A_bc"], ["A_bc"])
        b.ts(A_bc[:], A_bc[:], -1.0, None, ALU.mult, None, ["A_bc"], ["A_bc"])
        hwin = [b.sb("hwin%d" % i, [128, KC, 258], BF16) for i in range(2)]
        xbcT = b.sb("xbcT", [128, 24, 256], BF16)
        acc = [b.sb("cacc%d" % i, [128, 256]) for i in range(3)]
        rawb = [b.sb("rawb%d" % i, [128, 258]) for i in range(3)]
        xtok = [b.sb("a_xtok%d" % i, [128, 2048], BF16) for i in range(2)]
        btok = [b.sb("a_btok%d" % i, [128, 512], BF16) for i in range(2)]
        szt = [b.sb("a_sz%d" % i, [128, 2048], BF16) for i in range(2)]
        dtt = [b.sb("a_dt%d" % i, [128, 192]) for i in range(2)]
        dtmp = b.sb("a_dtmp", [128, 64])
        RA = [b.ps("a_R%d" % i, [128, 512]) for i in range(2)]
        TA = b.ps("a_T", [128, 1024], keys=[("a_T", 0), ("a_T", 1)])
        ZA = [b.ps("a_Z%d" % i, [128, 512]) for i in range(2)]
        DA = b.ps("a_D", [128, 512])
        TAv = TA[:].bitcast(BF16)
        DAv = DA[:].bitcast(BF16)
        ci = 0
        wins = [(si, tok0, w0) for si, (tok0, T, is_s) in enumerate(SEQS) for w0 in range(0, T, 256)]

        def load_win(i):
            si_, _, w0_ = wins[i]
            c0_ = H1OFF[si_] + w0_
            b.dma(hwin[i % 2][:], h1T[:, :, c0_:c0_ + 258], [("h1T", si_)], ["hwin%d" % (i % 2)])

        load_win(0)
        for wi, (si, tok0, w0) in enumerate(wins):
            if True:
                hw_ = hwin[wi % 2]
                hk = "hwin%d" % (wi % 2)
                if wi + 1 < len(wins):
                    load_win(wi + 1)
                for cb in range(24):
                    R_ = RA[cb % 2]
                    rk = "a_R%d" % (cb % 2)
                    ac_ = acc[cb % 3]
                    ak = "cacc%d" % (cb % 3)
                    for kc in range(KC):
                        b.mm(R_[:, 0:258], w1[:, kc, 2048 + cb * 128:2048 + (cb + 1) * 128], hw_[:, kc, :],
                             kc == 0, kc == KC - 1, [w1k[kc], hk], [rk])
                    rw_ = rawb[cb % 3]
                    rwk = "rawb%d" % (cb % 3)
                    eng = "dve"
                    b.act(rw_[:], R_[:, 0:258], AF.Copy, [rk], [rwk])
                    b.ts(ac_[:], rw_[:, 1:257], cw[:, cb, 1:2], cw[:, cb, 3:4], ALU.mult, ALU.add, [rwk, "cw"], [ak], q=eng)
                    b.stt(ac_[:], rw_[:, 0:256], cw[:, cb, 0:1], ac_[:], ALU.mult, ALU.add, [rwk, ak, "cw"], [ak], q=eng)
                    b.stt(ac_[:], rw_[:, 2:258], cw[:, cb, 2:3], ac_[:], ALU.mult, ALU.add, [rwk, ak, "cw"], [ak], q=eng)
                    b.act(xbcT[:, cb, :], ac_[:], AF.Silu, [ak], [("xbcT", cb)])
                for ch in range(2):
                    gc = (tok0 + w0) // 128 + ch
                    cs = slice(ch * 128, (ch + 1) * 128)
                    xt_, xk = xtok[ci % 2], "a_xtok%d" % (ci % 2)
                    bt_, bk = btok[ci % 2], "a_btok%d" % (ci % 2)
                    sz_, sk = szt[ci % 2], "a_sz%d" % (ci % 2)
                    dt_, dk = dtt[ci % 2], "a_dt%d" % (ci % 2)
                    ci += 1
                    for cb in range(16):
                        b.tr(TAv[:, cb * 128:(cb + 1) * 128], xbcT[:, cb, cs], ident_bf[:], [("xbcT", cb), "ident_bf"],
                             [("a_T", cb // 8)])
                    b.act(xt_[:, 0:1024], TAv[:, 0:1024], AF.Copy, [("a_T", 0)], [(xk, 0)])
                    b.cp(xt_[:, 1024:2048], TAv[:, 1024:2048], [("a_T", 1)], [(xk, 1)])
                    b.dma(xtok_s[gc], xt_[:], [(xk, 0), (xk, 1)], [("xtok_s", gc)])
                    for g4 in range(4):
                        b.tr(DAv[:, g4 * 128:(g4 + 1) * 128], xbcT[:, 16 + g4, cs], ident_bf[:],
                             [("xbcT", 16 + g4), "ident_bf"], ["a_D"])
                    b.cp(bt_[:], DAv[:, 0:512], ["a_D"], [bk])
                    b.dma(btok_s[gc], bt_[:], [bk], [("btok_s", gc)])
                    b.dma(bT_s[gc], xbcT[:, 16:20, cs], [("xbcT", 16 + i) for i in range(4)], [("bT_s", gc)])
                    b.dma(cT_s[gc], xbcT[:, 20:24, cs], [("xbcT", 20 + i) for i in range(4)], [("cT_s", gc)])
                    for zb in range(4):
                        Z_ = ZA[zb % 2]
                        zk = "a_Z%d" % (zb % 2)
                        for kc in range(KC):
                            b.mm(Z_[:], hw_[:, kc, 1 + ch * 128:1 + (ch + 1) * 128], w1[:, kc, zb * 512:(zb + 1) * 512],
                                 kc == 0, kc == KC - 1, [w1k[kc], hk], [zk])
                        b.act(sz_[:, zb * 512:(zb + 1) * 512], Z_[:], AF.Silu, [zk], [(sk, zb)])
                    b.dma(sz_s[gc], sz_[:], [(sk, i) for i in range(4)], [("sz_s", gc)])
                    for kc in range(KC):
                        b.mm(DA[:, 256:320], hw_[:, kc, 1 + ch * 128:1 + (ch + 1) * 128], w1[:, kc, 5120:5184],
                             kc == 0, kc == KC - 1, [w1k[kc], hk], ["a_D"])
                    b.tt(dtmp[:], DA[:, 256:320], dtb_bc[:], ALU.add, ["a_D", "dtb_bc"], ["a_dtmp"])
                    b.act(dtmp[:], dtmp[:], AF.Exp, ["a_dtmp"], ["a_dtmp"])
                    b.act(dt_[:, 0:64], dtmp[:], AF.Ln, ["a_dtmp"], [dk], bias=1.0)
                    b.act(dt_[:, 64:128], dt_[:, 0:64], AF.Ln, [dk], [dk])
                    b.tt(dt_[:, 128:192], dt_[:, 0:64], A_bc[:], ALU.mult, [dk, "A_bc"], [dk])
                    b.dma(dts_s[gc], dt_[:], [dk], [("dts_s", gc)])
        tk.barrier()
        b.es = es_save
    if LCUT <= 1:
        return

    def make_state_fns(hst, hst_bf, htmp, stin, stout, ST, ST_alt=None, htmp_alt=None):
        def load_state(src):
            for blk in range(16):
                b.dma(stin[:], src[blk * 128:(blk + 1) * 128, :], [], ["stin"])
                b.tr(ST[:, 0:128], stin[:], ident_f[:], ["stin", "ident_f"], ["b_ST"])
                b.cp(hst[:, blk * 128:(blk + 1) * 128], ST[:, 0:128], ["b_ST"], [("hst", blk // 4)])
            for g4 in range(4):
                b.act(hst_bf[:, g4 * 512:(g4 + 1) * 512], hst[:, g4 * 512:(g4 + 1) * 512], AF.Copy, [("hst", g4)],
                      [("hst_bf", g4)])

        def zero_state():
            for g4 in range(4):
                b.mset(hst[:, g4 * 512:(g4 + 1) * 512], 0.0, [("hst", g4)])
                b.mset(hst_bf[:, g4 * 512:(g4 + 1) * 512], 0.0, [("hst_bf", g4)], q="pool")

        def store_state(dst):
            for blk in range(16):
                b.tr(ST[:, 0:128], hst[:, blk * 128:(blk + 1) * 128], ident_f[:], [("hst", blk // 4), "ident_f"], ["b_ST"])
                b.cp(stout[:], ST[:, 0:128], ["b_ST"], ["stout"])
                b.dma(dst[blk * 128:(blk + 1) * 128, :], stout[:], ["stout"], [], is_out=True)

        def state_update(bt_, bk, xd_, xdk, cdt, cdk, off):
            sts = [(ST, "b_ST")] + ([(ST_alt, "b_ST2")] if ST_alt is not None else [])
            tmps = [(htmp, "htmp")] + ([(htmp_alt, "htmp2")] if htmp_alt is not None else [])
            nb_ = len(sts)
            for g0 in range(0, 4, nb_):
                for g4 in range(g0, g0 + nb_):
                    S_, sk_ = sts[g4 % nb_]
                    T_, tk_ = tmps[g4 % len(tmps)]
                    b.mm(S_[:], bt_[:, g4 * 128:(g4 + 1) * 128], xd_[:, g4 * 512:(g4 + 1) * 512], True, True,
                         [bk, xdk], [sk_])
                    hv = hst[:, g4 * 512:(g4 + 1) * 512].rearrange("p (h q) -> p h q", h=8)
                    b.tt(T_[:].rearrange("p (h q) -> p h q", h=8), hv,
                         bc(cdt[:, off + g4 * 8:off + (g4 + 1) * 8].unsqueeze(2), [128, 8, 64]), ALU.mult,
                         [("hst", g4), cdk], [tk_], q="pool")
                for g4 in range(g0, g0 + nb_):
                    S_, sk_ = sts[g4 % nb_]
                    T_, tk_ = tmps[g4 % len(tmps)]
                    b.tt(hst[:, g4 * 512:(g4 + 1) * 512], T_[:], S_[:], ALU.add, [tk_, sk_], [("hst", g4)])
                    b.act(hst_bf[:, g4 * 512:(g4 + 1) * 512], hst[:, g4 * 512:(g4 + 1) * 512], AF.Copy, [("hst", g4)],
                          [("hst_bf", g4)])
        return load_state, zero_state, store_state, state_update

    with ExitStack() as ph:
        es_save = b.es
        b.es = ph
        utri = [b.sb("utri%d" % d, [128, 128]) for d in range(2)]
        negm = [b.sb("negm%d" % d, [128, 128], BF16) for d in range(2)]
        i4 = b.sb("i4_sb", [128, 512], BF16)
        D_bc = b.sb("D_bc", [128, 32])
        DI = b.sb("DI", [128, 32, 128], BF16)
        ones_f = g["ones_f"]
        b.dma(utri[0][:], utf_d, [], ["utri0"])
        b.dma(utri[1][:], utb_d, [], ["utri1"])
        b.dma(negm[0][:], negf_d, [], ["negm0"], q="pool")
        b.dma(negm[1][:], negb_d, [], ["negm1"], q="pool")
        b.dma(i4[:], i4_d, [], ["i4"], q="pool")
        b.dma(D_bc[:], dsk_d.partition_broadcast(128), [], ["D_bc"])
        b.tt(DI[:], bc(ident_bf[:].unsqueeze(1), [128, 32, 128]), bc(D_bc[:].unsqueeze(2), [128, 32, 128]), ALU.mult,
             ["ident_bf", "D_bc"], ["DI"])

        NL = 3
        xt2 = [b.sb("b_xtok%d" % i, [128, 2048], BF16) for i in range(NL)]
        bt2 = [b.sb("b_btok%d" % i, [128, 512], BF16) for i in range(NL)]
        bT2 = [b.sb("b_bT%d" % i, [128, 4, 128], BF16) for i in range(NL)]
        cT2 = [b.sb("b_cT%d" % i, [128, 4, 128], BF16) for i in range(NL)]
        dt2 = [b.sb("b_dt%d" % i, [128, 192]) for i in range(NL)]
        acs2 = [b.sb("acs%d" % i, [128, 64]) for i in range(2)]
        ea2 = [b.sb("ea%d" % i, [128, 64]) for i in range(2)]
        de2 = [b.sb("de%d" % i, [128, 64]) for i in range(2)]
        cd2 = [b.sb("cd%d" % i, [128, 64]) for i in range(2)]
        wl2 = [b.sb("wl%d" % i, [128, 64]) for i in range(2)]
        nb2 = [b.sb("nb%d" % i, [128, 64]) for i in range(2)]
        Dm = [b.sb("Dm%d" % i, [128, 512]) for i in range(4)]
        Eb = [b.sb("E%d" % d, [128, 4096], BF16) for d in range(2)]
        MT2 = [[b.sb("MT%d_%d" % (i, d), [128, 4096], BF16) for d in range(2)] for i in range(2)]
        cbT = b.sb("cbT", [128, 512], BF16)
        xdec2 = [[b.sb("xdec%d_%d" % (i, d), [128, 2048], BF16) for d in range(2)] for i in range(2)]
        hst = b.sb("hst", [128, 2048])
        hst_bf = b.sb("hst_bf", [128, 2048], BF16)
        htmp = b.sb("htmp", [128, 512])
        ypt = [b.sb("ypt%d" % i, [128, 2048]) for i in range(2)]
        yot = [b.sb("yot%d" % i, [128, 512]) for i in range(2)]
        stin = b.sb("stin", [128, 128])
        stout = b.sb("stout", [128, 128])
        eact = [b.sb("eact%d" % i, [128, 64]) for i in range(2)]

        AC = b.ps("b_AC", [128, 512])
        CB = b.ps("b_CB", [128, 512])
        DB = [b.ps("b_DB%d" % i, [128, 512]) for i in range(2)]
        YG = [b.ps("b_YG%d" % i, [128, 512]) for i in range(2)]
        YO = b.ps("b_YO", [128, 512])
        ST = b.ps("b_ST", [128, 512])
        load_state, zero_state, store_state, state_update = make_state_fns(hst, hst_bf, htmp, stin, stout, ST)
        v3 = lambda t: t[:].rearrange("p (h l) -> p h l", h=32)

        items = []
        for si, (tok0, T, is_s) in enumerate(SEQS):
            nch = T // 128
            for c in range(nch - 1, -1, -1):
                items.append((tok0 // 128 + c, c == nch - 1, c == 0, si))

        def load_b(i):
            gcx, sx = items[i][0], i % NL
            b.dma(xt2[sx][:], xtok_s[gcx], [("xtok_s", gcx)], ["b_xtok%d" % sx])
            b.dma(bt2[sx][:], btok_s[gcx], [("btok_s", gcx)], ["b_btok%d" % sx])
            b.dma(bT2[sx][:], bT_s[gcx], [("bT_s", gcx)], ["b_bT%d" % sx])
            b.dma(cT2[sx][:], cT_s[gcx], [("cT_s", gcx)], ["b_cT%d" % sx])
            b.dma(dt2[sx][:], dts_s[gcx], [("dts_s", gcx)], ["b_dt%d" % sx])

        def early_stages(i):
            sl, s2 = i % NL, i % 2
            xt_, xk = xt2[sl], "b_xtok%d" % sl
            bT_, bTk = bT2[sl], "b_bT%d" % sl
            cT_, cTk = cT2[sl], "b_cT%d" % sl
            dt_, dk = dt2[sl], "b_dt%d" % sl
            acs, ea, de, cd, wl, nb = acs2[s2], ea2[s2], de2[s2], cd2[s2], wl2[s2], nb2[s2]
            ka = lambda n: "%s%d" % (n, s2)
            a_ = dt_[:, 128:192]

            def prologue():
                if i + 1 < len(items):
                    load_b(i + 1)
                for d in range(2):
                    b.mm(AC[:, d * 32:(d + 1) * 32], utri[d][:], a_[:, d * 32:(d + 1) * 32], True, True,
                         ["utri%d" % d, dk], ["b_AC"])
                    b.mm(AC[:, 64 + d * 32:64 + (d + 1) * 32], ones_f[:], a_[:, d * 32:(d + 1) * 32], True, True,
                         ["ones_f", dk], ["b_AC"])
                b.cp(acs[:], AC[:, 0:64], ["b_AC"], [ka("acs")])
                b.act(ea[:], acs[:], AF.Exp, [ka("acs")], [ka("ea")])
                b.act(cd[:], AC[:, 64:128], AF.Exp, ["b_AC"], [ka("cd")])
                b.tt(de[:], AC[:, 64:128], acs[:], ALU.subtract, ["b_AC", ka("acs")], [ka("de")])
                b.act(de[:], de[:], AF.Exp, [ka("de")], [ka("de")])
                b.tt(wl[:], dt_[:, 0:64], de[:], ALU.mult, [dk, ka("de")], [ka("wl")])
                b.tt(nb[:], dt_[:, 64:128], acs[:], ALU.subtract, [dk, ka("acs")], [ka("nb")])
                for g4 in range(4):
                    b.mm(CB[:, g4 * 128:(g4 + 1) * 128], bT_[:, g4, :], cT_[:, g4, :], True, True, [bTk, cTk], ["b_CB"])
                b.act(cbT[:], CB[:], AF.Copy, ["b_CB"], ["cbT"])
                xv = xt_[:].rearrange("p (h q) -> p h q", h=32)
                for d in range(2):
                    b.tt(xdec2[s2][d][:].rearrange("p (h q) -> p h q", h=32), xv,
                         bc(wl[:, d * 32:(d + 1) * 32].unsqueeze(2), [128, 32, 64]), ALU.mult, [xk, ka("wl")],
                         ["xdec%d_%d" % (s2, d)], q=("pool" if d == 0 else "dve"))

            def banks(d, js):
                for j in js:
                    DB_, dbk = DB[j % 2], "b_DB%d" % (j % 2)
                    Dm_, dmk = Dm[j % 4], "Dm%d" % (j % 4)
                    b.mm(DB_[:], negm[d][:], i4[:], True, False, ["negm%d" % d, "i4"], [dbk])
                    for hh in range(4):
                        h = j * 4 + hh
                        b.mm(DB_[:, hh * 128:(hh + 1) * 128], bc(a_[:, d * 32 + h:d * 32 + h + 1], [128, 128]),
                             utri[d][:], False, hh == 3, [dk, "utri%d" % d], [dbk])
                    b.tt(Dm_[:].rearrange("p (h l) -> p h l", h=4), DB_[:].rearrange("p (h l) -> p h l", h=4),
                         bc(nb[:, d * 32 + j * 4:d * 32 + (j + 1) * 4].unsqueeze(2), [128, 4, 128]), ALU.add,
                         [dbk, ka("nb")], [dmk])
                    b.act(Eb[d][:, j * 512:(j + 1) * 512], Dm_[:], AF.Exp, [dmk], [("E", d, j)])

            def mtbuild(d):
                b.tt(MT2[s2][d][:].rearrange("p (g r l) -> p g r l", g=4, r=8),
                     Eb[d][:].rearrange("p (g r l) -> p g r l", g=4, r=8),
                     bc(cbT[:].rearrange("p (g l) -> p g l", g=4).unsqueeze(2), [128, 4, 8, 128]), ALU.mult,
                     [("E", d, j) for j in range(8)] + ["cbT"], ["MT%d_%d" % (s2, d)], q="pool")

            def q0():
                banks(0, range(0, 4))

            def q1():
                banks(0, range(4, 8))
                mtbuild(0)

            def q2():
                banks(1, range(0, 4))

            def q3():
                banks(1, range(4, 8))
                mtbuild(1)
            return [prologue, q0, q1, q2, q3]

        def late_stages(i):
            gc, first, last, si = items[i]
            sl, s2 = i % NL, i % 2
            xt_, xk = xt2[sl], "b_xtok%d" % sl
            bt_, bk = bt2[sl], "b_btok%d" % sl
            cT_, cTk = cT2[sl], "b_cT%d" % sl
            ea, cd = ea2[s2], cd2[s2]
            eak, cdk = "ea%d" % s2, "cd%d" % s2
            yp_, ypk = ypt[s2], "ypt%d" % s2
            MT = MT2[s2]

            def head():
                if first:
                    if SEQS[si][2]:
                        load_state(stb)
                    else:
                        zero_state()

            def group(g4):
                YG_, ygk = YG[g4 % 2], "b_YG%d" % (g4 % 2)
                yo_, yok2 = yot[g4 % 2], "yot%d" % (g4 % 2)
                b.mm(YO[:], cT_[:, g4, :], hst_bf[:, g4 * 512:(g4 + 1) * 512], True, True, [cTk, ("hst_bf", g4)],
                     ["b_YO"])
                b.tt(yo_[:].rearrange("p (h q) -> p h q", h=8), YO[:].rearrange("p (h q) -> p h q", h=8),
                     bc(ea[:, 32 + g4 * 8:32 + (g4 + 1) * 8].unsqueeze(2), [128, 8, 64]), ALU.mult,
                     ["b_YO", eak], [yok2])
                for hh in range(8):
                    h = g4 * 8 + hh
                    xs_ = xt_[:, h * 64:(h + 1) * 64]
                    b.mm(YG_[:, hh * 64:(hh + 1) * 64], MT[0][:, h * 128:(h + 1) * 128], xs_, True, False,
                         ["MT%d_0" % s2, xk], [ygk])
                    b.mm(YG_[:, hh * 64:(hh + 1) * 64], MT[1][:, h * 128:(h + 1) * 128], xs_, False, False,
                         ["MT%d_1" % s2, xk], [ygk])
                    b.mm(YG_[:, hh * 64:(hh + 1) * 64], DI[:, h, :], xs_, False, True, ["DI", xk], [ygk])
                b.tt(yp_[:, g4 * 512:(g4 + 1) * 512], yo_[:], YG_[:], ALU.add, [yok2, ygk], [(ypk, g4)])

            def tail():
                b.dma(yp_s[gc], yp_[:], [(ypk, j) for j in range(4)], [("yp_s", gc)])
                b.dma(xdf_s[gc], xdec2[s2][0][:], ["xdec%d_0" % s2], [("xdf_s", gc)])
                ec_, eck = eact[s2], "eact%d" % s2
                b.cp(ec_[:, 0:32], ea[:, 0:32], [eak], [eck])
                b.cp(ec_[:, 32:64], cd[:, 0:32], [cdk], [eck])
                b.dma(eac_s[gc], ec_[:], [eck], [("eac_s", gc)])
                state_update(bt_, bk, xdec2[s2][1], "xdec%d_1" % s2, cd, cdk, 32)
                if last and not SEQS[si][2]:
                    store_state(sb_o[si - 1])
            return [head] + [(lambda g4=g4: group(g4)) for g4 in range(4)] + [tail]

        gnw_fm = b.sb("gnw_fm", [128, 16])
        b.dma(gnw_fm[:], gnw_d, [], ["gnw_fm"])
        w1stg = [b.sb("w1stg%d" % i, [128, D]) for i in range(2)]
        w1ob = [b.sb("w1ob%d" % i, [128, D], BF16) for i in range(2)]
        w1o_s = g["w1o_s"]

        def w1o_load(kc):
            b.dma(w1stg[kc % 2][:], w1out[kc * 128:(kc + 1) * 128, :], [], ["w1stg%d" % (kc % 2)])

        def w1o_scale(kc):
            b.act(w1ob[kc % 2][:], w1stg[kc % 2][:], AF.Copy, ["w1stg%d" % (kc % 2), "gnw_fm"], ["w1ob%d" % (kc % 2)],
                  scale=gnw_fm[:, kc:kc + 1])

        def w1o_store(kc):
            b.dma(w1o_s[kc * 128:(kc + 1) * 128, :], w1ob[kc % 2][:], ["w1ob%d" % (kc % 2)], [("w1o_s", kc)])

        load_b(0)
        for st in early_stages(0):
            st()
        for i in range(len(items)):
            es_ = early_stages(i + 1) if i + 1 < len(items) else None
            ls_ = late_stages(i)
            ls_[0]()
            kper = -(-16 // max(1, len(items) - 2))
            slabs = lambda j: range(j * kper, min(16, (j + 1) * kper)) if j >= 0 else range(0)
            if kper == 1:
                for kc_ in slabs(i - 2):
                    w1o_store(kc_)
                for kc_ in slabs(i):
                    w1o_load(kc_)
                for kc_ in slabs(i - 1):
                    w1o_scale(kc_)
            else:
                for kc_ in slabs(i):
                    w1o_load(kc_)
                    w1o_scale(kc_)
                    w1o_store(kc_)
            if es_:
                es_[0]()
            for q4 in range(4):
                if es_:
                    es_[1 + q4]()
                ls_[1 + q4]()
            ls_[5]()
        tk.barrier()
        b.es = es_save
    if LCUT <= 2:
        return

    with ExitStack() as ph:
        es_save = b.es
        b.es = ph
        w1o = b.sb("w1o", [128, 16, D], BF16)
        w1o_s = g["w1o_s"]
        for kc in range(16):
            b.dma(w1o[:, kc, :], w1o_s[kc * 128:(kc + 1) * 128, :], [("w1o_s", kc)], [("w1o", kc)])
        NB2 = 3
        bt3 = [b.sb("c_btok%d" % i, [128, 512], BF16) for i in range(NB2)]
        cT3 = [b.sb("c_cT%d" % i, [128, 4, 128], BF16) for i in range(NB2)]
        xd3 = [b.sb("c_xdf%d" % i, [128, 2048], BF16) for i in range(NB2)]
        yp3 = [b.sb("c_yp%d" % i, [128, 2048]) for i in range(NB2)]
        ec3 = [b.sb("c_ec%d" % i, [128, 64]) for i in range(NB2)]
        sz3 = [b.sb("c_sz%d" % i, [128, 2048], BF16) for i in range(NB2)]
        x13 = [b.sb("c_x1%d" % i, [128, D]) for i in range(NB2)]
        ygw3 = [b.sb("c_ygw%d" % i, [128, 2048], BF16) for i in range(NB2)]
        ss3 = [b.sb("c_ss%d" % i, [128, 8]) for i in range(NB2)]
        rs3 = [b.sb("c_rs%d" % i, [128, 2]) for i in range(NB2)]
        ygT = b.sb("c_ygT", [128, 2048], BF16)
        yout = [b.sb("c_yout%d" % i, [128, D]) for i in range(2)]
        junk2 = b.sb("c_junk2", [128, 512], BF16)
        yot = [b.sb("c_yot%d" % i, [128, 512]) for i in range(2)]
        hst = b.sb("c_hst", [128, 2048])
        hst_bf = b.sb("c_hst_bf", [128, 2048], BF16)
        htmp = b.sb("c_htmp", [128, 512])
        stin = b.sb("c_stin", [128, 128])
        stout = b.sb("c_stout", [128, 128])
        YO2 = [b.ps("c_YO%d" % i, [128, 512]) for i in range(2)]
        ST = b.ps("c_ST", [128, 512], keys=["b_ST"])
        TP2 = b.ps("c_TP", [128, 1024], keys=[("c_TP", 0), ("c_TP", 1)])
        OP2 = b.ps("c_OP", [128, 1024], keys=[("c_OP", 0), ("c_OP", 1)])
        ST2 = b.ps("c_ST2", [128, 512], keys=["b_ST2"])
        htmp2 = b.sb("c_htmp2", [128, 512])
        load_state, zero_state, store_state, state_update = make_state_fns(hst, hst_bf, htmp, stin, stout, ST, ST2, htmp2)
        TPv = TP2[:].bitcast(BF16)

        def chunk_front(gc, s_, cond):
            bt_, bk = bt3[s_], "c_btok%d" % s_
            cT_, cTk = cT3[s_], "c_cT%d" % s_
            xd_, xdk = xd3[s_], "c_xdf%d" % s_
            yp_, ypk = yp3[s_], "c_yp%d" % s_
            ec_, eck = ec3[s_], "c_ec%d" % s_
            sz_, szk = sz3[s_], "c_sz%d" % s_
            x1_, x1k = x13[s_], "c_x1%d" % s_
            ygw, ygk = ygw3[s_], "c_ygw%d" % s_
            ss4, ssk = ss3[s_], "c_ss%d" % s_
            rs, rsk = rs3[s_], "c_rs%d" % s_
            for g4 in range(4):
                YO_, yk = YO2[g4 % 2], "c_YO%d" % (g4 % 2)
                yo_, yok2 = yot[g4 % 2], "c_yot%d" % (g4 % 2)
                b.mm(YO_[:], cT_[:, g4, :], hst_bf[:, g4 * 512:(g4 + 1) * 512], True, True, [cTk, ("hst_bf", g4)], [yk])
                b.tt(yo_[:].rearrange("p (h q) -> p h q", h=8), YO_[:].rearrange("p (h q) -> p h q", h=8),
                     bc(ec_[:, g4 * 8:(g4 + 1) * 8].unsqueeze(2), [128, 8, 64]), ALU.mult, [yk, eck], [yok2])
                ysl = yp_[:, g4 * 512:(g4 + 1) * 512]
                b.tt(ysl, ysl, yo_[:], ALU.add, [(ypk, g4), yok2], [(ypk, g4)])
            state_update(bt_, bk, xd_, xdk, ec_, eck, 32)
            for g4 in range(4):
                ysl = yp_[:, g4 * 512:(g4 + 1) * 512]
                ygs = ygw[:, g4 * 512:(g4 + 1) * 512]
                b.tt(ygs, ysl, sz_[:, g4 * 512:(g4 + 1) * 512], ALU.mult, [(ypk, g4), szk], [(ygk, g4)],
                     q=("pool" if g4 % 2 == 0 else "dve"))
                b.act(junk2[:], ygs, AF.Square, [(ygk, g4)], ["c_junk2"], accum_out=ss4[:, g4:g4 + 1])
            b.tt(ss4[:, 4:5], ss4[:, 0:1], ss4[:, 1:2], ALU.add, ["c_junk2"], [ssk + "a"])
            b.tt(ss4[:, 5:6], ss4[:, 2:3], ss4[:, 3:4], ALU.add, ["c_junk2"], [ssk + "b"])
            b.tt(ss4[:, 6:7], ss4[:, 4:5], ss4[:, 5:6], ALU.add, [ssk + "a", ssk + "b"], [ssk + "c"])
            b.act(rs[:, 0:1], ss4[:, 6:7], AF.Ln, [ssk + "c", "eps_t"], [rsk + "0"], scale=1.0 / 2048, bias=eps_t[:, 0:1])
            b.act(rs[:, 1:2], rs[:, 0:1], AF.Exp, [rsk + "0"], [rsk], scale=-0.5)

        def load_c(gc, s_):
            b.dma(bt3[s_][:], btok_s[gc], [("btok_s", gc)], ["c_btok%d" % s_])
            b.dma(cT3[s_][:], cT_s[gc], [("cT_s", gc)], ["c_cT%d" % s_])
            b.dma(ec3[s_][:], eac_s[gc], [("eac_s", gc)], ["c_ec%d" % s_])
            b.dma(xd3[s_][:], xdf_s[gc], [("xdf_s", gc)], ["c_xdf%d" % s_])
            b.dma(yp3[s_][:], yp_s[gc], [("yp_s", gc)], [("c_yp%d" % s_, i) for i in range(4)])
            b.dma(sz3[s_][:], sz_s[gc], [("sz_s", gc)], ["c_sz%d" % s_])
            b.dma(x13[s_][:], x1s[gc * 128:(gc + 1) * 128, :], [("x1s", gc * 128)], ["c_x1%d" % s_])

        def chunk_back(gc, s_, cond, oi):
            x1_, x1k = x13[s_], "c_x1%d" % s_
            ygw, ygk = ygw3[s_], "c_ygw%d" % s_
            rs, rsk = rs3[s_], "c_rs%d" % s_
            yo, yok = yout[oi % 2], "c_yout%d" % (oi % 2)
            for kc in range(16):
                b.tr(TPv[:, kc * 128:(kc + 1) * 128], ygw[:, kc * 128:(kc + 1) * 128], ident_bf[:],
                     [(ygk, kc // 4), "ident_bf"], [("c_TP", kc // 8)])
            b.act(ygT[:, 0:1024], TPv[:, 0:1024], AF.Copy, [("c_TP", 0)], [("c_ygT", 0)])
            b.cp(ygT[:, 1024:2048], TPv[:, 1024:2048], [("c_TP", 1)], [("c_ygT", 1)])
            for half in range(2):
                for kc in range(16):
                    b.mm(OP2[:, half * 512:(half + 1) * 512], ygT[:, kc * 128:(kc + 1) * 128],
                         w1o[:, kc, half * 512:(half + 1) * 512], kc == 0, kc == 15,
                         [("c_ygT", kc // 8), ("w1o", kc)], [("c_OP", half)])
                b.stt(yo[:, half * 512:(half + 1) * 512], OP2[:, half * 512:(half + 1) * 512], rs[:, 1:2],
                      gate_bc[1][cond][:, half * 512:(half + 1) * 512], ALU.mult, ALU.mult,
                      [("c_OP", half), rsk, "gbc1%d" % cond], [(yok, half)])
                b.tt(yo[:, half * 512:(half + 1) * 512], yo[:, half * 512:(half + 1) * 512],
                     x1_[:, half * 512:(half + 1) * 512], ALU.add, [(yok, half), x1k], [(yok, half)], q="pool")
            b.dma(y_o[gc * 128:(gc + 1) * 128, :], yo[:], [(yok, 0), (yok, 1)], [], is_out=True)

        li = 0
        oi = 0
        prompt_i = 0
        pending = None
        for si, (tok0, T, is_s) in enumerate(SEQS):
            nch = T // 128
            gc0 = tok0 // 128
            cond = 0 if is_s else 1
            if is_s:
                load_state(stf)
            else:
                zero_state()
            for c in range(nch):
                s_ = li % NB2
                if li == 0:
                    load_c(gc0 + c, s_)
                if gc0 + c + 1 < NCH:
                    load_c(gc0 + c + 1, (li + 1) % NB2)
                li += 1
                chunk_front(gc0 + c, s_, cond)
                if pending is not None:
                    chunk_back(*pending, oi)
                    oi += 1
                pending = (gc0 + c, s_, cond)
            if not is_s:
                store_state(sf_o[prompt_i])
                prompt_i += 1
        if pending is not None:
            chunk_back(*pending, oi)
        b.es = es_save


def _consts():
    ident = np.eye(128, dtype=np.float32)
    pm = np.zeros((128, 128), np.float32)
    for m in range(128):
        d = m % 64
        part = (d % 32) // 16
        k = m + 16 if part == 0 else m - 16
        pm[k, m] = 1.0
    bones = np.zeros((128, 128), np.float32)
    bones[0:64, 0:64] = 1.0
    bones[64:128, 64:128] = 1.0
    s = np.arange(128)[:, None]
    l = np.arange(128)[None, :]
    utf = (s <= l).astype(np.float32)
    utb = (s >= l).astype(np.float32)
    negf = np.where(s < l, -30000.0, 0.0).astype(np.float32)
    negb = np.where(s > l, -30000.0, 0.0).astype(np.float32)
    i4 = np.tile(ident, (1, 4)).astype(np.float32)
    t = np.arange(TS)
    pos = np.stack([t // 64, t % 64]).astype(np.float32)
    inv = (1.0 / (10000.0 ** (np.arange(0, 32, 2, dtype=np.float32) / 32.0))).astype(np.float32)
    cosT = np.zeros((128, TS), np.float32)
    sinT = np.zeros((128, TS), np.float32)
    for m in range(128):
        d = m % 64
        j, part, i = d // 32, (d % 32) // 16, d % 16
        ang = (pos[j] * inv[i]).astype(np.float32)
        cosT[m] = np.cos(ang)
        sinT[m] = np.sin(ang) * (-1.0 if part == 0 else 1.0)
    return dict(ident=ident, pm=pm, bones=bones, utf=utf, utb=utb, negf=negf, negb=negb, i4=i4, cosT=cosT, sinT=sinT)


def _fm(v, n):
    return np.ascontiguousarray(np.asarray(v, np.float32).reshape(n, 128).T)


def make_in_maps(inp):
    c = _consts()
    f32 = lambda a: np.ascontiguousarray(np.asarray(a, np.float32))
    shared = dict(c)
    for k in ("l0_mod_w", "l1_mod_w", "l0_w_in", "l0_w_out", "l1_w_in", "l1_w_out"):
        shared[k] = f32(inp[k])
    shared["l0_nw"] = _fm(inp["l0_norm_w"], 8)
    shared["l1_nw"] = _fm(inp["l1_norm_w"], 8)
    shared["l0_mb"] = _fm(inp["l0_mod_b"], 24)
    shared["l1_mb"] = _fm(inp["l1_mod_b"], 24)
    shared["l0_mbg"] = f32(inp["l0_mod_b"])[2048:3072].reshape(1, 1024)
    shared["l1_mbg"] = f32(inp["l1_mod_b"])[2048:3072].reshape(1, 1024)
    gq, gk = f32(inp["l0_q_norm"]), f32(inp["l0_k_norm"])
    d = np.arange(128) % 64
    part = (d % 32) // 16
    pi = np.where(part == 0, d + 16, d - 16)
    shared["gvec"] = np.ascontiguousarray(np.stack([gq[d], gq[pi], gk[d], gk[pi]], axis=1))
    cw = f32(inp["l1_conv_w"])
    cb_ = f32(inp["l1_conv_b"])
    convw = np.stack([cw[0], cw[1], cw[2], cb_], axis=1)
    shared["convw"] = np.ascontiguousarray(convw.reshape(24, 128, 4).transpose(1, 0, 2))
    shared["dtb"] = np.concatenate([f32(inp["l1_dt_bias_f"]), f32(inp["l1_dt_bias_b"])]).reshape(1, 64)
    shared["alog"] = np.concatenate([f32(inp["l1_a_log_f"]), f32(inp["l1_a_log_b"])]).reshape(1, 64)
    shared["dsk"] = f32(inp["l1_d_skip"]).reshape(1, 32)
    shared["gnw"] = _fm(inp["l1_gnorm_w"], 16)
    shared["gnrow"] = f32(inp["l1_gnorm_w"]).reshape(1, 2048)
    xp, xsm = f32(inp["x_prompt"]), f32(inp["x_sample"])
    maps = []
    for core in range(8):
        m = dict(shared)
        m["xs"] = np.ascontiguousarray(np.concatenate([xsm[core], xp[2 * core], xp[2 * core + 1]], axis=0))
        m["ck"] = f32(inp["cache_k_l0"])[core].reshape(CTX, 256)
        m["cv"] = f32(inp["cache_v_l0"])[core].reshape(CTX, 256)
        m["stf"] = f32(inp["state_fwd_l1"])[core].reshape(2048, 128)
        m["stb"] = f32(inp["state_bwd_l1"])[core].reshape(2048, 128)
        cc = np.stack([f32(inp["c"])[core], f32(inp["c_ctx"])], axis=1)
        m["cvec"] = np.ascontiguousarray(cc.reshape(8, 128, 2).transpose(1, 0, 2))
        maps.append(m)
    return maps


_NC_CACHE = {}


def kernel(**inputs):
    if "nc" not in _NC_CACHE:
        _NC_CACHE["nc"] = build_program()
    nc = _NC_CACHE["nc"]
    maps = make_in_maps(inputs)
    res = run_bass_kernel_spmd(nc, maps, core_ids=list(range(8)))
    r = res.results
    y_prompt = np.zeros((16, TP, D), np.float32)
    y_sample = np.zeros((8, TS, D), np.float32)
    nk = np.zeros((16, TP, 4, 64), np.float32)
    nv = np.zeros((16, TP, 4, 64), np.float32)
    sf = np.zeros((16, 32, 64, 128), np.float32)
    sbw = np.zeros((16, 32, 64, 128), np.float32)
    for c in range(8):
        y = r[c]["y"]
        y_sample[c] = y[0:TS]
        y_prompt[2 * c] = y[TS:TS + TP]
        y_prompt[2 * c + 1] = y[TS + TP:]
        nk[2 * c:2 * c + 2] = r[c]["nk"].reshape(2, TP, 4, 64)
        nv[2 * c:2 * c + 2] = r[c]["nv"].reshape(2, TP, 4, 64)
        sf[2 * c:2 * c + 2] = r[c]["sf"].reshape(2, 32, 64, 128)
        sbw[2 * c:2 * c + 2] = r[c]["sbw"].reshape(2, 32, 64, 128)
    return (y_prompt, y_sample, nk, nv, sf, sbw)
```
